# Optimizing a Trainium2 kernel written in Bass

```python
import math
import jax, jax.numpy as jnp
from jax import lax
import numpy as np

D_MODEL = 2048
BATCH = 4
SEQ = 4096
DEPTH = 2

ATT_HEAD_DIM = 64
ATT_SLOTS = 8
DILATION_PAIRS = ((128, 1), (512, 4), (2048, 16))
N_DIL = len(DILATION_PAIRS)
ATT_HEADS = ATT_SLOTS * N_DIL
ATT_WIDTH = ATT_HEADS * ATT_HEAD_DIM
ATT_OUT = ATT_SLOTS * ATT_HEAD_DIM
SSD_HEAD_DIM = 64
SSD_INNER = D_MODEL // 2
SSD_HEADS = SSD_INNER // SSD_HEAD_DIM
SSD_GROUPS = 2
SSD_HPG = SSD_HEADS // SSD_GROUPS
SSD_STATE = 128
SSD_CONV = 5
SSD_CHUNK = 128
CONV_CH = SSD_INNER + 2 * SSD_GROUPS * SSD_STATE
POOL_WINDOWS = (2, 4, 8, 16)
POOL_GROUP = D_MODEL // 16
POOL_WIDTH = POOL_GROUP * len(POOL_WINDOWS)
MIX_WIDTH = ATT_OUT + SSD_INNER + POOL_WIDTH
IN_SIZES = (ATT_WIDTH, ATT_WIDTH, ATT_WIDTH, SSD_INNER, CONV_CH, 2 * SSD_HEADS, POOL_WIDTH)
IN_WIDTH = sum(IN_SIZES)
D_FF = -(-8 * D_MODEL // (3 * 256)) * 256
RMS_EPS = 1e-6
NEG_INF = -1e30

kernel_name = "hybrid_dilated_ssd_pool_encoder"


def rms_norm(x, w):
    xf = x.astype(jnp.float32)
    y = xf * lax.rsqrt(jnp.mean(xf * xf, axis=-1, keepdims=True) + RMS_EPS)
    return (y * w.astype(jnp.float32)).astype(x.dtype)


def alibi_slopes():
    k = jnp.arange(1, ATT_HEADS + 1, dtype=jnp.float32)
    return (2.0 ** (-8.0 * k / ATT_HEADS)).reshape(N_DIL, ATT_SLOTS)


def dilated_window_attention(q, k, v, slopes, window, dilation):
    b, s, h, e = q.shape
    half = window // (2 * dilation)
    L = s // dilation
    nb = -(-L // half)
    Lp = nb * half

    def split(t):
        t = t.reshape(b, L, dilation, h, e).transpose(0, 2, 3, 1, 4)
        return jnp.pad(t, ((0, 0), (0, 0), (0, 0), (0, Lp - L), (0, 0)))

    def band(t):
        tp = jnp.pad(t, ((0, 0), (0, 0), (0, 0), (half, half), (0, 0))).reshape(b, dilation, h, nb + 2, half, e)
        return jnp.concatenate([tp[:, :, :, :-2], tp[:, :, :, 1:-1], tp[:, :, :, 2:]], axis=-2)

    qb = split(q).reshape(b, dilation, h, nb, half, e)
    kb = band(split(k))
    vb = band(split(v))
    scores = jnp.einsum('bdhiqe,bdhike->bdhiqk', qb, kb) * (e ** -0.5)
    qpos = jnp.arange(nb)[:, None] * half + jnp.arange(half)[None, :]
    kpos = jnp.arange(nb)[:, None] * half - half + jnp.arange(3 * half)[None, :]
    rel = kpos[:, None, :] - qpos[:, :, None]
    valid = (jnp.abs(rel) <= half) & (kpos[:, None, :] >= 0) & (kpos[:, None, :] < L)
    dist = (jnp.abs(rel) * dilation).astype(jnp.float32)
    scores = scores - slopes[:, None, None, None] * dist[None]
    scores = jnp.where(valid, scores, NEG_INF)
    m = jnp.max(scores, axis=-1, keepdims=True)
    p = jnp.exp(scores - m)
    l = jnp.sum(p, axis=-1, keepdims=True)
    o = jnp.einsum('bdhiqk,bdhike->bdhiqe', p, vb) / l
    lse = m[..., 0] + jnp.log(l[..., 0])
    o = o.reshape(b, dilation, h, Lp, e)[:, :, :, :L].transpose(0, 3, 1, 2, 4).reshape(b, s, h, e)
    lse = lse.reshape(b, dilation, h, Lp)[:, :, :, :L].transpose(0, 3, 1, 2).reshape(b, s, h)
    return o, lse


def ssd_scan(x, dt, a, bm, cm):
    bsz, s, g, h, p = x.shape
    n = bm.shape[-1]
    q = SSD_CHUNK
    c = s // q
    x = x.reshape(bsz, c, q, g, h, p)
    dt = dt.reshape(bsz, c, q, g, h)
    bm = bm.reshape(bsz, c, q, g, n)
    cm = cm.reshape(bsz, c, q, g, n)
    a_cs = jnp.cumsum((dt * a).transpose(0, 3, 4, 1, 2), axis=-1)
    lower = jnp.tril(jnp.ones((q, q), dtype=bool))
    diff = a_cs[..., :, None] - a_cs[..., None, :]
    seg = jnp.where(lower, jnp.exp(jnp.minimum(diff, 0.0)), 0.0)
    xdt = x * dt[..., None]
    cb = jnp.einsum('bclgn,bcsgn->bgcls', cm, bm)
    y_diag = jnp.einsum('bgcls,bghcls,bcsghp->bclghp', cb, seg, xdt)
    decay_states = jnp.exp(a_cs[..., -1:] - a_cs)
    states = jnp.einsum('bclgn,bghcl,bclghp->cbghpn', bm, decay_states, xdt)
    chunk_decay = jnp.exp(a_cs[..., -1]).transpose(3, 0, 1, 2)

    def step(carry, inp):
        st, dec = inp
        return carry * dec[..., None, None] + st, carry

    init = jnp.zeros((bsz, g, h, p, n), dtype=x.dtype)
    _, prev = lax.scan(step, init, (states, chunk_decay))
    y_off = jnp.einsum('bclgn,cbghpn,bghcl->bclghp', cm, prev, jnp.exp(a_cs))
    return (y_diag + y_off).reshape(bsz, s, g, h, p)


def ssd_mixer(z, xbc_raw, dt_raw, conv_w, conv_b, dt_bias, a_log, d_skip, norm_w):
    b, s, _ = z.shape
    pad = SSD_CONV // 2
    xbc = lax.conv_general_dilated(xbc_raw, conv_w.astype(jnp.float32)[:, None, :], window_strides=(1,),
                                   padding=[(pad, pad)], dimension_numbers=('NWC', 'WIO', 'NWC'),
                                   feature_group_count=CONV_CH)
    xbc = jax.nn.silu(xbc + conv_b.astype(jnp.float32))
    xs = xbc[..., :SSD_INNER].reshape(b, s, SSD_GROUPS, SSD_HPG, SSD_HEAD_DIM)
    gn = SSD_GROUPS * SSD_STATE
    bm = xbc[..., SSD_INNER:SSD_INNER + gn].reshape(b, s, SSD_GROUPS, SSD_STATE)
    cm = xbc[..., SSD_INNER + gn:].reshape(b, s, SSD_GROUPS, SSD_STATE)
    a = -jnp.exp(a_log.astype(jnp.float32)).reshape(2, SSD_GROUPS, SSD_HPG)
    dt = jax.nn.softplus(dt_raw.reshape(b, s, 2, SSD_GROUPS, SSD_HPG)
                         + dt_bias.astype(jnp.float32).reshape(2, SSD_GROUPS, SSD_HPG))
    flip = lambda t: t[:, ::-1]
    y_fwd = ssd_scan(xs, dt[:, :, 0], a[0], bm, cm)
    y_bwd = flip(ssd_scan(flip(xs), flip(dt[:, :, 1]), a[1], flip(bm), flip(cm)))
    y = y_fwd + y_bwd + xs * d_skip.astype(jnp.float32).reshape(SSD_GROUPS, SSD_HPG)[..., None]
    y = y.reshape(b, s, SSD_INNER) * jax.nn.silu(z)
    yg = y.reshape(b, s, SSD_GROUPS, SSD_INNER // SSD_GROUPS)
    yg = yg * lax.rsqrt(jnp.mean(yg * yg, axis=-1, keepdims=True) + RMS_EPS)
    return yg.reshape(b, s, SSD_INNER) * norm_w.astype(jnp.float32)


def multiscale_pool(u, pool_w, pool_scale):
    b, s, _ = u.shape
    cs = jnp.concatenate([jnp.zeros((b, 1, POOL_WIDTH), u.dtype), jnp.cumsum(u, axis=1)], axis=1)
    t = jnp.arange(s)
    outs = []
    for g, w in enumerate(POOL_WINDOWS):
        sl = slice(g * POOL_GROUP, (g + 1) * POOL_GROUP)
        lo = jnp.clip(t - w // 2, 0, s)
        hi = jnp.clip(t + w // 2, 0, s)
        cnt = (hi - lo).astype(jnp.float32)
        mean = (cs[:, hi, sl] - cs[:, lo, sl]) / cnt[None, :, None]
        outs.append(jnp.einsum('bsc,cd->bsd', mean - u[..., sl], pool_w[g].astype(jnp.float32)))
    return jnp.concatenate(outs, axis=-1) * pool_scale.astype(jnp.float32)


def hybrid_layer(x, norm1_w, w_in, conv_w, conv_b, dt_bias, a_log, d_skip, ssd_norm_w,
                 pool_w, pool_scale, w_out, norm2_w, w_gate, w_up, w_down):
    b, s, _ = x.shape
    h = rms_norm(x, norm1_w)
    proj = jnp.matmul(h, w_in).astype(jnp.float32)
    cuts = [int(c) for c in np.cumsum(IN_SIZES)[:-1]]
    q, k, v, z, xbc, dt_raw, u = jnp.split(proj, cuts, axis=-1)
    q = q.reshape(b, s, N_DIL, ATT_SLOTS, ATT_HEAD_DIM)
    k = k.reshape(b, s, N_DIL, ATT_SLOTS, ATT_HEAD_DIM)
    v = v.reshape(b, s, N_DIL, ATT_SLOTS, ATT_HEAD_DIM)
    slopes = alibi_slopes()
    outs, lses = [], []
    for gi, (window, dilation) in enumerate(DILATION_PAIRS):
        o, l = dilated_window_attention(q[:, :, gi], k[:, :, gi], v[:, :, gi], slopes[gi], window, dilation)
        outs.append(o)
        lses.append(l)
    wts = jax.nn.softmax(jnp.stack(lses, axis=-1), axis=-1)
    att = jnp.sum(jnp.stack(outs, axis=-1) * wts[:, :, :, None, :], axis=-1).reshape(b, s, ATT_OUT)
    ssd = ssd_mixer(z, xbc, dt_raw, conv_w, conv_b, dt_bias, a_log, d_skip, ssd_norm_w)
    pool = multiscale_pool(u, pool_w, pool_scale)
    mix = jnp.concatenate([att, ssd, pool], axis=-1).astype(x.dtype)
    x = x + jnp.matmul(mix, w_out)
    h = rms_norm(x, norm2_w)
    ff = jax.nn.silu(jnp.matmul(h, w_gate)) * jnp.matmul(h, w_up)
    return x + jnp.matmul(ff, w_down)


def setup_inputs(seed: int = 0) -> dict:
    key = jax.random.key(seed)
    ks = jax.random.split(key, 20)
    f32 = jnp.float32
    nrm = lambda k, shape, scale: jax.random.normal(k, shape, f32) * scale
    dt0 = jnp.exp(jax.random.uniform(ks[6], (DEPTH, 2, SSD_HEADS), f32) * (math.log(0.1) - math.log(0.001))
                  + math.log(0.001))
    return {
        "x": nrm(ks[0], (BATCH, SEQ, D_MODEL), 1.0),
        "norm1_w": 1.0 + nrm(ks[1], (DEPTH, D_MODEL), 0.02),
        "w_in": nrm(ks[2], (DEPTH, D_MODEL, IN_WIDTH), D_MODEL ** -0.5),
        "conv_w": nrm(ks[3], (DEPTH, SSD_CONV, CONV_CH), SSD_CONV ** -0.5),
        "conv_b": nrm(ks[4], (DEPTH, CONV_CH), 0.02),
        "dt_bias": dt0 + jnp.log(-jnp.expm1(-dt0)),
        "a_log": jnp.log(jax.random.uniform(ks[7], (DEPTH, 2, SSD_HEADS), f32, 1.0, 16.0)),
        "d_skip": 1.0 + nrm(ks[8], (DEPTH, SSD_HEADS), 0.02),
        "ssd_norm_w": 1.0 + nrm(ks[9], (DEPTH, SSD_INNER), 0.02),
        "pool_w": nrm(ks[10], (DEPTH, len(POOL_WINDOWS), POOL_GROUP, POOL_GROUP), POOL_GROUP ** -0.5),
        "pool_scale": 1.0 + nrm(ks[11], (DEPTH, POOL_WIDTH), 0.02),
        "w_out": nrm(ks[12], (DEPTH, MIX_WIDTH, D_MODEL), MIX_WIDTH ** -0.5),
        "norm2_w": 1.0 + nrm(ks[13], (DEPTH, D_MODEL), 0.02),
        "w_gate": nrm(ks[14], (DEPTH, D_MODEL, D_FF), D_MODEL ** -0.5),
        "w_up": nrm(ks[15], (DEPTH, D_MODEL, D_FF), D_MODEL ** -0.5),
        "w_down": nrm(ks[16], (DEPTH, D_FF, D_MODEL), D_FF ** -0.5),
        "final_norm_w": 1.0 + nrm(ks[17], (D_MODEL,), 0.02),
    }


def reference(x, norm1_w, w_in, conv_w, conv_b, dt_bias, a_log, d_skip, ssd_norm_w, pool_w, pool_scale,
              w_out, norm2_w, w_gate, w_up, w_down, final_norm_w):
    for l in range(DEPTH):
        x = hybrid_layer(x, norm1_w[l], w_in[l], conv_w[l], conv_b[l], dt_bias[l], a_log[l], d_skip[l],
                         ssd_norm_w[l], pool_w[l], pool_scale[l], w_out[l], norm2_w[l],
                         w_gate[l], w_up[l], w_down[l])
    return rms_norm(x, final_norm_w)
```

```python
import numpy as np
from contextlib import ExitStack
import ml_dtypes
import concourse.bass as bass
import concourse.mybir as mybir
from concourse.bass_utils import run_bass_kernel_spmd

F32, BF16 = mybir.dt.float32, mybir.dt.bfloat16
AF = mybir.ActivationFunctionType
ALU = mybir.AluOpType
AX = mybir.AxisListType

D = 2048
S_LEN = 4096
DEPTH = 2
IN_W = 7712
DFF = 5632
EPS = 1e-6
NQT = S_LEN // 128
DILS = (1, 4, 16)


class Sched:
    BLK = {'pe': 'tensor', 'act': 'scalar', 'dve': 'vector', 'pool': 'gpsimd', 'sp': 'sync'}
    NSLOT = {'sp': 28, 'pool': 12, 'act': 6}

    def __init__(self, nc, es):
        self.nc = nc
        self.sem = {e: es.enter_context(nc.semaphore("s_" + e)) for e in ('pe', 'act', 'dve', 'pool')}
        self.dsem = {e: [es.enter_context(nc.semaphore("d_%s%d" % (e, i))) for i in range(n)]
                     for e, n in self.NSLOT.items()}
        self.cnt = {e: 0 for e in self.sem}
        self.slot_uses = {e: [0] * n for e, n in self.NSLOT.items()}
        self.next_slot = {e: 0 for e in self.NSLOT}
        self.waited = {e: {} for e in self.BLK}
        self.reset()

    def reset(self):
        self.ops = []
        self.lw = {}
        self.rd = {}

    def op(self, eng, fn, r=(), w=(), dma=False):
        deps = set()
        for b in r:
            x = self.lw.get(b)
            if x is not None:
                deps.add(x)
        for b in w:
            x = self.lw.get(b)
            if x is not None:
                deps.add(x)
            rb = self.rd.get(b)
            if rb:
                deps.update(rb.values())
        i = len(self.ops)
        self.ops.append([eng, fn, deps, dma, False, 0, 0, 0])
        for b in r:
            self.rd.setdefault(b, {})[('d', i) if dma else eng] = i
        for b in w:
            self.lw[b] = i
            self.rd[b] = {}
        return i

    def flush(self, name=None):
        ops = self.ops
        for o in ops:
            for d in o[2]:
                D_ = ops[d]
                if D_[3]:
                    continue
                if D_[0] == 'pe' and o[0] == 'pe' and not o[3]:
                    continue
                D_[4] = True
        for o in ops:
            e = o[0]
            if o[3]:
                s = self.next_slot[e]
                self.next_slot[e] = (s + 1) % self.NSLOT[e]
                o[7] = 16 * self.slot_uses[e][s]
                self.slot_uses[e][s] += 1
                o[5] = 16 * self.slot_uses[e][s]
                o[6] = s
            elif o[4]:
                self.cnt[e] += 1
                o[5] = self.cnt[e]
        with self.nc.Block() as block:
            for e, bname in self.BLK.items():
                eops = [o for o in ops if o[0] == e]
                if not eops:
                    continue

                def body(eng, eops=eops, e=e):
                    waited = self.waited[e]
                    for o in eops:
                        reqs = {}
                        for d in o[2]:
                            D_ = ops[d]
                            if D_[3]:
                                key = ('d', D_[0], D_[6])
                            else:
                                if D_[0] == 'pe' and e == 'pe' and not o[3]:
                                    continue
                                key = ('e', D_[0])
                            if reqs.get(key, 0) < D_[5]:
                                reqs[key] = D_[5]
                        if o[3] and o[7] > 0:
                            key = ('d', e, o[6])
                            if reqs.get(key, 0) < o[7]:
                                reqs[key] = o[7]
                        for key, val in reqs.items():
                            if waited.get(key, 0) < val:
                                sem = self.sem[key[1]] if key[0] == 'e' else self.dsem[key[1]][key[2]]
                                eng.wait_ge(sem, val)
                                waited[key] = val
                        ins = o[1](eng)
                        if o[3]:
                            ins.then_inc(self.dsem[e][o[6]], 16)
                        elif o[4]:
                            ins.then_inc(self.sem[e], 1)
                    if e in self.NSLOT:
                        for s in range(self.NSLOT[e]):
                            val = 16 * self.slot_uses[e][s]
                            key = ('d', e, s)
                            if waited.get(key, 0) < val:
                                eng.wait_ge(self.dsem[e][s], val)
                                waited[key] = val

                getattr(block, bname)(body)
        self.reset()


class Ctx:
    pass


_UID = [0]


def _uniq(n):
    _UID[0] += 1
    return "%s_%d" % (n, _UID[0])


def MM(K, out, lhsT, rhs, start, stop, r, w):
    K.S.op('pe', lambda e: e.matmul(out, lhsT=lhsT, rhs=rhs, start=start, stop=stop), r=r, w=w)


def TR(K, out, in_, r, w, ident=None):
    idn = K.ident[:] if ident is None else ident
    K.S.op('pe', lambda e: e.transpose(out=out, in_=in_, identity=idn), r=r, w=w)


def DMA(K, q, out, in_, r=(), w=(), slow=False):
    if slow:
        K.S.op(q, lambda e: e.dma_start(out=out, in_=in_, allow_slow_non_contiguous=True), r=r, w=w, dma=True)
    else:
        K.S.op(q, lambda e: e.dma_start(out=out, in_=in_), r=r, w=w, dma=True)


def ACTV(K, out, in_, func, r, w, bias=None, scale=None, accum_out=None):
    kw = {}
    if bias is not None:
        kw['bias'] = bias
    if scale is not None:
        kw['scale'] = scale
    if accum_out is not None:
        kw['accum_out'] = accum_out
    K.S.op('act', lambda e: e.activation(out=out, in_=in_, func=func, **kw), r=r, w=w)


def TT(K, eng, out, in0, in1, op, r, w):
    K.S.op(eng, lambda e: e.tensor_tensor(out=out, in0=in0, in1=in1, op=op), r=r, w=w)


def TS(K, eng, out, in0, s1, op0, r, w, s2=None, op1=None, accum_out=None):
    kw = {}
    if op1 is not None:
        kw['op1'] = op1
    if accum_out is not None:
        kw['accum_out'] = accum_out
    K.S.op(eng, lambda e: e.tensor_scalar(out=out, in0=in0, scalar1=s1, scalar2=s2, op0=op0, **kw), r=r, w=w)


def STT(K, out, in0, scalar, in1, op0, op1, r, w, accum_out=None):
    kw = {}
    if accum_out is not None:
        kw['accum_out'] = accum_out
    K.S.op('dve', lambda e: e.scalar_tensor_tensor(out=out, in0=in0, scalar=scalar, in1=in1, op0=op0, op1=op1, **kw),
           r=r, w=w)


def CP(K, eng, out, in_, r, w):
    if eng == 'act':
        K.S.op('act', lambda e: e.activation(out=out, in_=in_, func=AF.Copy), r=r, w=w)
    else:
        K.S.op(eng, lambda e: e.tensor_copy(out=out, in_=in_), r=r, w=w)


def _evac(K, idx, out, in_, r, w, scale=None):
    if idx % 2 == 0:
        if scale is None:
            K.S.op('act', lambda e: e.activation(out=out, in_=in_, func=AF.Copy), r=r, w=w)
        else:
            K.S.op('act', lambda e: e.activation(out=out, in_=in_, func=AF.Copy, scale=scale), r=r, w=w)
    else:
        if scale is None:
            K.S.op('dve', lambda e: e.tensor_copy(out=out, in_=in_), r=r, w=w)
        else:
            K.S.op('dve', lambda e: e.tensor_scalar(out=out, in0=in_, scalar1=scale, scalar2=None, op0=ALU.mult), r=r, w=w)


def rms_to_hT(K, es_tiles, x_src, tok0, ntile, wbc, hT, hT_key, xt, hb, st, junk, ptr, st_base=0, keep_x=None):
    S = K.S
    for tt in range(ntile):
        i = tt % len(hb)
        T0 = tok0 + tt * 128
        xti = xt[i] if keep_x is None else keep_x[tt]
        xkey = ('xt', i) if keep_x is None else ('xk', tt)
        if keep_x is None:
            S.op('sp', lambda e, xti=xti, T0=T0: e.dma_start(out=xti[:], in_=x_src[T0:T0 + 128, :]), w=[xkey], dma=True)
        c = (st_base + tt) * 4
        sk = ('st', st_base + tt)
        S.op('dve', lambda e, xti=xti, c=c: e.scalar_tensor_tensor(
            out=junk[:], in0=xti[:], scalar=1.0, in1=xti[:], op0=ALU.mult, op1=ALU.mult,
            accum_out=st[:, c:c + 1]), r=[xkey], w=['junk', sk])
        S.op('act', lambda e, c=c: e.activation(out=st[:, c + 1:c + 2], in_=st[:, c:c + 1], func=AF.Sqrt,
                                                scale=1.0 / D, bias=K.eps_t[:, 0:1]), r=[sk], w=[sk])
        S.op('dve', lambda e, c=c: e.reciprocal(out=st[:, c + 2:c + 3], in_=st[:, c + 1:c + 2]), r=[sk], w=[sk])
        S.op('dve', lambda e, xti=xti, c=c, i=i: e.scalar_tensor_tensor(
            out=hb[i][:], in0=xti[:], scalar=st[:, c + 2:c + 3], in1=wbc[:], op0=ALU.mult, op1=ALU.mult),
            r=[xkey, sk, 'wbc'], w=[('hb', i)])
        for hh in range(2):
            for cc in range(8):
                ch = hh * 8 + cc
                S.op('pe', lambda e, hh=hh, cc=cc, ch=ch, i=i: e.transpose(
                    out=ptr[hh][:, cc, :], in_=hb[i][:, ch * 128:(ch + 1) * 128], identity=K.ident[:]),
                    r=[('hb', i), 'ident'], w=[('ptr', hh)])
            _evac(K, hh, hT[:, hh * 8:(hh + 1) * 8, tt * 128:(tt + 1) * 128], ptr[hh][:],
                  r=[('ptr', hh)], w=[(hT_key, tt)])


def phase_A(K, l, x_src):
    nc, S = K.nc, K.S
    with ExitStack() as es:
        sb = lambda n, s, d: es.enter_context(nc.sbuf_tensor(_uniq(n), s, d))
        ps = lambda n, s, d: es.enter_context(nc.psum_tensor(_uniq(n), s, d))
        hT = sb("hT", [128, 16, 2048], BF16)
        wbc = sb("wbc", [128, D], F32)
        xt = [sb("xt%d" % i, [128, D], F32) for i in range(2)]
        junk = sb("junk", [128, D], BF16)
        hb = [sb("hb%d" % i, [128, D], BF16) for i in range(2)]
        st = sb("st", [128, 16 * 4], F32)
        wbuf = [sb("wb%d" % i, [128, 16, 544], BF16) for i in range(2)]
        obh = [sb("obh%d" % i, [128, 512], BF16) for i in range(4)]
        obf = [sb("obf%d" % i, [128, 512], F32) for i in range(4)]
        obig = [sb("obig%d" % i, [128, 2048], BF16) for i in range(3)]
        ptr = [ps("ptr%d" % i, [128, 8, 128], BF16) for i in range(2)]
        pmm = [ps("pmm%d" % i, [128, 512], F32) for i in range(4)]

        S.op('sp', lambda e: e.dma_start(out=wbc[:], in_=K.norm1_w[l:l + 1, :].partition_broadcast(128)),
             w=['wbc'], dma=True)
        blocks = []
        for g in range(3):
            blocks.append(('qk', K.qT, g, g * 512, 512))
        for g in range(3):
            blocks.append(('qk', K.kT, g, 1536 + g * 512, 512))
        for g in range(3):
            blocks.append(('v', None, g, 3072 + g * 512, 512))
        for j in range(2):
            blocks.append(('z', None, j, 4608 + j * 512, 512))
        for j in range(3):
            blocks.append(('xbc', None, j, 5632 + j * 512, 512))
        blocks.append(('dtu', None, 0, 7168, 544))
        hT_all = [('hT', t) for t in range(16)]
        nblk = 0
        kk = 0
        nbig = 0

        def load_w(bi, c0, wd):
            src = K.w_in[l, :, c0:c0 + wd].rearrange("(c p) n -> p c n", p=128)
            S.op('pool', lambda e: e.dma_start(out=wbuf[bi][:, :, 0:wd], in_=src), w=[('wb', bi)], dma=True)

        for half in range(2):
            H0 = half * 2048
            load_w(nblk % 2, blocks[0][3], blocks[0][4])
            rms_to_hT(K, es, x_src, H0, 16, wbc, hT, 'hT', xt, hb, st, junk, ptr)
            for bidx, (mode, dst, g, c0, wd) in enumerate(blocks):
                bi = nblk % 2
                nblk += 1
                if bidx + 1 < len(blocks):
                    load_w(nblk % 2, blocks[bidx + 1][3], blocks[bidx + 1][4])
                wb = wbuf[bi]
                wkey = ('wb', bi)
                rk = [wkey] + hT_all
                if mode == 'qk':
                    d = DILS[g]
                    npr = 2048 // d
                    for ct in range(4):
                        dview = dst[g * 512 + ct * 128:g * 512 + (ct + 1) * 128, 64:64 + S_LEN] \
                            .rearrange("p (r u) -> p r u", r=d)
                        ob_i = nbig % 3
                        nbig += 1
                        og_ = obig[ob_i][:].rearrange("p (r u) -> p r u", r=d)
                        for j in range(4):
                            k = kk % 4
                            kk += 1
                            for ch in range(16):
                                MM(K, pmm[k][:], wb[:, ch, ct * 128:(ct + 1) * 128], hT[:, ch, j * 512:(j + 1) * 512],
                                   ch == 0, ch == 15, r=[wkey] + hT_all[4 * j:4 * j + 4], w=[('pmm', k)])
                            _evac(K, kk, og_[:, :, j * (512 // d):(j + 1) * (512 // d)],
                                  pmm[k][:].rearrange("p (u r) -> p r u", r=d), r=[('pmm', k)], w=[('obig', ob_i)],
                                  scale=(0.125 if dst is K.qT else None))
                        DMA(K, 'sp', dview[:, :, half * npr:(half + 1) * npr], og_, r=[('obig', ob_i)])
                elif mode in ('v', 'z', 'dtu'):
                    d = DILS[g] if mode == 'v' else 1
                    npr = 2048 // d
                    L = S_LEN // d
                    for i in range(16):
                        k = kk % 4
                        kk += 1
                        r_ = (128 * i) // npr
                        u0 = (128 * i) % npr
                        t0 = r_ + d * u0
                        if d > 1:
                            lsel = lambda ch, t0=t0, d=d: hT[:, ch, t0:t0 + 127 * d + 1:d]
                            rki = [wkey] + hT_all
                        else:
                            lsel = lambda ch, i=i: hT[:, ch, i * 128:(i + 1) * 128]
                            rki = [wkey, hT_all[i]]
                        tok = H0 + i * 128
                        if mode == 'dtu':
                            for ch in range(16):
                                MM(K, pmm[k][:], lsel(ch), wb[:, ch, 32:544], ch == 0, ch == 15, r=rki, w=[('pmm', k)])
                            _evac(K, kk, obh[k][:], pmm[k][:], r=[('pmm', k)], w=[('obh', k)])
                            DMA(K, 'sp', K.uu[128 + tok:128 + tok + 128, :], obh[k][:], r=[('obh', k)])
                            k2 = kk % 4
                            kk += 1
                            for ch in range(16):
                                MM(K, pmm[k2][:, 0:32], lsel(ch), wb[:, ch, 0:32], ch == 0, ch == 15, r=rki, w=[('pmm', k2)])
                            _evac(K, kk, obf[k2][:, 0:32], pmm[k2][:, 0:32], r=[('pmm', k2)], w=[('obf', k2)])
                            DMA(K, 'sp', K.dtr[tok:tok + 128, :], obf[k2][:, 0:32], r=[('obf', k2)])
                            continue
                        for ch in range(16):
                            MM(K, pmm[k][:], lsel(ch), wb[:, ch, 0:512], ch == 0, ch == 15, r=rki, w=[('pmm', k)])
                        if mode == 'v':
                            P = 64 + r_ * L + half * npr + u0
                            _evac(K, kk, obh[k][:], pmm[k][:], r=[('pmm', k)], w=[('obh', k)])
                            DMA(K, 'sp', K.vv[g, P:P + 128, :], obh[k][:], r=[('obh', k)])
                        else:
                            _evac(K, kk, obf[k][:], pmm[k][:], r=[('pmm', k)], w=[('obf', k)])
                            DMA(K, 'sp', K.zz[tok:tok + 128, g * 512:(g + 1) * 512], obf[k][:], r=[('obf', k)])
                else:
                    for ct in range(4):
                        for j in range(4):
                            k = kk % 4
                            kk += 1
                            for ch in range(16):
                                MM(K, pmm[k][:], wb[:, ch, ct * 128:(ct + 1) * 128], hT[:, ch, j * 512:(j + 1) * 512],
                                   ch == 0, ch == 15, r=[wkey] + hT_all[4 * j:4 * j + 4], w=[('pmm', k)])
                            _evac(K, kk, obh[k][:], pmm[k][:], r=[('pmm', k)], w=[('obh', k)])
                            row = g * 512 + ct * 128
                            col = 2 + H0 + j * 512
                            DMA(K, 'sp', K.xbcT[row:row + 128, col:col + 512], obh[k][:], r=[('obh', k)])
        S.flush()


def phase_CD(K, l, x_src, x_dst, final_w=None, out_dst=None):
    nc, S = K.nc, K.S
    TB = 512
    NT = TB // 128
    with ExitStack() as es:
        sb = lambda n, s, d: es.enter_context(nc.sbuf_tensor(_uniq(n), s, d))
        ps = lambda n, s, d: es.enter_context(nc.psum_tensor(_uniq(n), s, d))
        h2T = sb("h2T", [128, 16, TB], BF16)
        xm = [sb("xm%d" % i, [128, D], F32) for i in range(NT)]
        wbc = sb("wbc2", [128, D], F32)
        wbf = sb("wbcf", [128, D], F32) if final_w is not None else None
        hb = [sb("hb2_%d" % i, [128, D], BF16) for i in range(1)]
        junk = sb("junk2", [128, D], BF16)
        st = sb("st2", [128, 8 * NT * 4], F32)
        wA = [sb("wA%d" % i, [128, 16, 512], BF16) for i in range(2)]
        wB = [sb("wB%d" % i, [128, 16, 512], BF16) for i in range(2)]
        ffT = [sb("ffT%d" % i, [128, 4, TB], BF16) for i in range(2)]
        wd = [sb("wd%d" % i, [128, 4, D], BF16) for i in range(1)]
        tmp = [sb("tmp%d" % i, [128, 512], F32) for i in range(2)]
        ptr = [ps("ptr2_%d" % i, [128, 8, 128], BF16) for i in range(2)]
        pgu = [ps("pgu%d" % i, [128, 512], F32) for i in range(4)]
        pd = [ps("pd%d" % i, [128, 512], F32) for i in range(2)]
        DMA(K, 'sp', wbc[:], K.norm2_w[l:l + 1, :].partition_broadcast(128), w=['wbc'])
        if final_w is not None:
            DMA(K, 'sp', wbf[:], final_w.partition_broadcast(128), w=['wbf'])
        na = 0
        nb = 0
        nf = 0
        kd = 0
        for blk in range(S_LEN // TB):
            T0 = blk * TB
            mixT = wB[nb % 2]
            mkey = ('wB', nb % 2)
            nb += 1
            DMA(K, 'sp', mixT[:], K.mixT[:, T0:T0 + TB].rearrange("(c p) t -> p c t", p=128), w=[mkey])
            for tt in range(NT):
                DMA(K, 'sp', xm[tt][:], x_src[T0 + tt * 128:T0 + (tt + 1) * 128, :], w=[('xk', tt)])
            for cb in range(4):
                wo = wA[na % 2]
                wkey = ('wA', na % 2)
                na += 1
                DMA(K, 'pool', wo[:], K.w_out[l, :, cb * 512:(cb + 1) * 512].rearrange("(c p) n -> p c n", p=128), w=[wkey])
                for tt in range(NT):
                    k = kd % 2
                    kd += 1
                    for ch in range(16):
                        MM(K, pd[k][:], mixT[:, ch, tt * 128:(tt + 1) * 128], wo[:, ch, :], ch == 0, ch == 15,
                           r=[mkey, wkey], w=[('pd', k)])
                    TT(K, 'dve', xm[tt][:, cb * 512:(cb + 1) * 512], pd[k][:], xm[tt][:, cb * 512:(cb + 1) * 512], ALU.add,
                       r=[('pd', k), ('xk', tt)], w=[('xk', tt)])
            rms_to_hT(K, None, None, 0, NT, wbc, h2T, 'h2T', None, hb, st, junk, ptr, st_base=(blk % 8) * NT, keep_x=xm)
            h_all = [('h2T', t) for t in range(NT)]
            for gi in range(DFF // 512):
                wg = wA[na % 2]
                gkey = ('wA', na % 2)
                na += 1
                wu = wB[nb % 2]
                ukey = ('wB', nb % 2)
                nb += 1
                DMA(K, 'pool', wg[:], K.w_gate[l, :, gi * 512:(gi + 1) * 512].rearrange("(c p) n -> p c n", p=128), w=[gkey])
                DMA(K, 'pool', wu[:], K.w_up[l, :, gi * 512:(gi + 1) * 512].rearrange("(c p) n -> p c n", p=128), w=[ukey])
                ff = ffT[nf % 2]
                fkey = ('ffT', nf % 2)
                nf += 1
                for fb in range(4):
                    pg = pgu[2 * (fb % 2)]
                    pu = pgu[2 * (fb % 2) + 1]
                    kg = ('pgu', 2 * (fb % 2))
                    ku = ('pgu', 2 * (fb % 2) + 1)
                    for ch in range(16):
                        MM(K, pg[:], wg[:, ch, fb * 128:(fb + 1) * 128], h2T[:, ch, :], ch == 0, ch == 15,
                           r=[gkey] + h_all, w=[kg])
                    for ch in range(16):
                        MM(K, pu[:], wu[:, ch, fb * 128:(fb + 1) * 128], h2T[:, ch, :], ch == 0, ch == 15,
                           r=[ukey] + h_all, w=[ku])
                    tm = tmp[fb % 2]
                    ACTV(K, tm[:], pg[:], AF.Silu, r=[kg], w=[('tmp', fb % 2)])
                    TT(K, 'dve', ff[:, fb, :], tm[:], pu[:], ALU.mult, r=[('tmp', fb % 2), ku], w=[fkey])
                wdn = wd[0]
                DMA(K, 'pool', wdn[:], K.w_down[l, gi * 512:(gi + 1) * 512, :].rearrange("(c p) n -> p c n", p=128), w=['wd'])
                for tt in range(NT):
                    for cb in range(4):
                        k = kd % 2
                        kd += 1
                        for c in range(4):
                            MM(K, pd[k][:], ff[:, c, tt * 128:(tt + 1) * 128], wdn[:, c, cb * 512:(cb + 1) * 512], c == 0, c == 3,
                               r=[fkey, 'wd'], w=[('pd', k)])
                        TT(K, 'dve', xm[tt][:, cb * 512:(cb + 1) * 512], pd[k][:], xm[tt][:, cb * 512:(cb + 1) * 512], ALU.add,
                           r=[('pd', k), ('xk', tt)], w=[('xk', tt)])
            for tt in range(NT):
                rows = slice(T0 + tt * 128, T0 + (tt + 1) * 128)
                if final_w is None:
                    DMA(K, 'sp', x_dst[rows, :], xm[tt][:], r=[('xk', tt)])
                else:
                    c = ((blk % 8) * NT + tt) * 4
                    sk = ('stf', tt)
                    STT(K, junk[:], xm[tt][:], 1.0, xm[tt][:], ALU.mult, ALU.mult, r=[('xk', tt)], w=['junk', sk],
                        accum_out=st[:, c + 3:c + 4])
                    ACTV(K, st[:, c + 1:c + 2], st[:, c + 3:c + 4], AF.Sqrt, r=[sk], w=[sk], scale=1.0 / D, bias=K.eps_t[:, 0:1])
                    K.S.op('dve', lambda e, c=c: e.reciprocal(out=st[:, c + 2:c + 3], in_=st[:, c + 1:c + 2]), r=[sk], w=[sk])
                    STT(K, xm[tt][:], xm[tt][:], st[:, c + 2:c + 3], wbf[:], ALU.mult, ALU.mult, r=[('xk', tt), sk, 'wbf'],
                        w=[('xk', tt)])
                    DMA(K, 'sp', out_dst[rows, :], xm[tt][:], r=[('xk', tt)])
        S.flush()


def phase_att(K, l):
    nc, S = K.nc, K.S
    with ExitStack() as es:
        sb = lambda n, s, d: es.enter_context(nc.sbuf_tensor(_uniq(n), s, d))
        ps = lambda n, s, d: es.enter_context(nc.psum_tensor(_uniq(n), s, d))
        b0 = sb("b0", [128, 9, 256], BF16)
        sid = sb("sid", [128, 24, 128], BF16)
        zt = sb("zt", [128, 512], BF16)
        qs = [sb("qs%d" % i, [128, 4, 128], BF16) for i in range(2)]
        ks = [sb("ks%d" % i, [128, 4, 256], BF16) for i in range(2)]
        vs = [sb("vs%d" % i, [128, 2, 512], BF16) for i in range(2)]
        Pm = [sb("Pm%d" % i, [128, 4, 256], BF16) for i in range(2)]
        PT = [sb("PT%d" % i, [128, 8, 128], BF16) for i in range(2)]
        mx = [sb("mx%d" % i, [128, 8], F32) for i in range(2)]
        nmx = [sb("nmx%d" % i, [128, 8], F32) for i in range(2)]
        ogt = [sb("ogt%d" % i, [128, 512], F32) for i in range(2)]
        mlt = [sb("mlt%d" % i, [128, 16], F32) for i in range(2)]
        psc = [ps("psc%d" % i, [128, 4, 256], F32) for i in range(2)]
        pT = [ps("pT%d" % i, [128, 8, 128], BF16) for i in range(2)]
        po = [ps("po%d" % i, [128, 512], F32) for i in range(2)]
        DMA(K, 'sp', b0[:], K.c_b0.rearrange("v p k -> p v k"), w=['b0'])
        DMA(K, 'sp', sid[:], K.c_sid.rearrange("h p k -> p h k"), w=['sid'])
        S.op('dve', lambda e: e.memset(zt[:], 0.0), w=['zt'])
        NP = S_LEN + 128
        for a_ in range(12):
            DMA(K, 'sp', K.kT[a_ * 128:(a_ + 1) * 128, 0:64], zt[:, 0:64], r=['zt'], w=['kT'])
            DMA(K, 'sp', K.kT[a_ * 128:(a_ + 1) * 128, NP - 64:NP], zt[:, 0:64], r=['zt'], w=['kT'])
        for g in range(3):
            DMA(K, 'sp', K.vv[g, 0:64, :], zt[0:64, :], r=['zt'], w=['vv'])
            DMA(K, 'sp', K.vv[g, NP - 64:NP, :], zt[0:64, :], r=['zt'], w=['vv'])
        n = 0
        for g in range(3):
            d = DILS[g]
            L = S_LEN // d
            for ti in range(NQT):
                i = n % 2
                n += 1
                r_ = (ti * 128) // L
                u0 = (ti * 128) % L
                var = 1 if u0 == 0 else (2 if u0 + 128 == L else 0)
                P0 = 64 + r_ * L + u0
                rows = slice(g * 512, (g + 1) * 512)
                DMA(K, 'sp', qs[i][:], K.qT[rows, P0:P0 + 128].rearrange("(a p) t -> p a t", p=128), w=[('qs', i)])
                DMA(K, 'sp', ks[i][:], K.kT[rows, P0 - 64:P0 + 192].rearrange("(a p) t -> p a t", p=128), r=['kT'], w=[('ks', i)])
                DMA(K, 'sp', vs[i][:], K.vv[g, P0 - 64:P0 + 192, :].rearrange("(kt p) c -> p kt c", p=128), r=['vv'], w=[('vs', i)])
                for hf in range(2):
                    for sl in range(4):
                        s_ = 4 * hf + sl
                        pr = s_ // 2
                        prt = slice(64 * (s_ % 2), 64 * (s_ % 2) + 64)
                        MM(K, psc[hf][:, sl, :], qs[i][prt, pr, :], ks[i][prt, pr, :], True, False,
                           r=[('qs', i), ('ks', i)], w=[('psc', hf)])
                        MM(K, psc[hf][:, sl, :], sid[:, g * 8 + s_, :], b0[:, g * 3 + var, :], False, True,
                           r=['sid', 'b0'], w=[('psc', hf)])
                    nms = mlt[i][:, 4 * hf:4 * hf + 4]
                    S.op('dve', lambda e, nms=nms, hf=hf: e.tensor_reduce(out=nms, in_=psc[hf][:], axis=AX.X, op=ALU.max, negate=True),
                         r=[('psc', hf)], w=[('mltm', i, hf)])
                    for sl in range(4):
                        s_ = 4 * hf + sl
                        ACTV(K, Pm[hf][:, sl, :], psc[hf][:, sl, :], AF.Exp, r=[('psc', hf), ('mltm', i, hf)],
                             w=[('Pm', hf, sl), ('mltl', i, s_)], bias=mlt[i][:, s_:s_ + 1],
                             accum_out=mlt[i][:, 8 + s_:9 + s_])
                    for sl in range(4):
                        for kt in range(2):
                            TR(K, pT[hf][:, sl * 2 + kt, :], Pm[hf][:, sl, kt * 128:(kt + 1) * 128],
                               r=[('Pm', hf, sl), 'ident'], w=[('pT', hf)])
                    CP(K, 'dve' if hf == 0 else 'act', PT[hf][:], pT[hf][:], r=[('pT', hf)], w=[('PT', hf)])
                    for sl in range(4):
                        s_ = 4 * hf + sl
                        for kt in range(2):
                            MM(K, po[i][:, s_ * 64:(s_ + 1) * 64], PT[hf][:, sl * 2 + kt, :], vs[i][:, kt, s_ * 64:(s_ + 1) * 64],
                               kt == 0, kt == 1, r=[('PT', hf), ('vs', i)], w=[('po', i)])
                CP(K, 'act', ogt[i][:], po[i][:], r=[('po', i)], w=[('ogt', i)])
                t0 = r_ + d * u0
                tsl = slice(t0, t0 + 127 * d + 1, d) if d > 1 else slice(t0, t0 + 128)
                DMA(K, 'sp', K.og[g, tsl, :], ogt[i][:], r=[('ogt', i)])
                DMA(K, 'sp', K.mlg[g, tsl, :], mlt[i][:], r=[('mltm', i, 0), ('mltm', i, 1)] + [('mltl', i, q_) for q_ in range(8)])
        S.flush()
    with ExitStack() as es:
        sb = lambda n, s, d: es.enter_context(nc.sbuf_tensor(_uniq(n), s, d))
        ps = lambda n, s, d: es.enter_context(nc.psum_tensor(_uniq(n), s, d))
        o3 = [sb("o3_%d" % i, [128, 3, 512], F32) for i in range(2)]
        ml3 = [sb("ml3_%d" % i, [128, 3, 16], F32) for i in range(2)]
        sm = [sb("sm%d" % i, [128, 64], F32) for i in range(2)]
        w3 = [sb("w3_%d" % i, [128, 3, 8], F32) for i in range(2)]
        acc = [sb("acc%d" % i, [128, 512], F32) for i in range(2)]
        t2 = [sb("t2_%d" % i, [128, 512], F32) for i in range(2)]
        ab = [sb("ab%d" % i, [128, 512], BF16) for i in range(2)]
        aT = [sb("aT%d" % i, [128, 4, 128], BF16) for i in range(2)]
        pa = [ps("pa%d" % i, [128, 4, 128], BF16) for i in range(2)]
        for ti in range(NQT):
            i = ti % 2
            rows = slice(ti * 128, (ti + 1) * 128)
            DMA(K, 'sp', o3[i][:], K.og[:, rows, :].rearrange("g p c -> p g c"), w=[('o3', i)])
            DMA(K, 'sp', ml3[i][:], K.mlg[:, rows, :].rearrange("g p c -> p g c"), w=[('ml3', i)])
            M = sm[i][:, 0:8]
            den = sm[i][:, 8:16]
            rden = sm[i][:, 16:24]
            k3 = [('ml3', i)]
            ks_ = [('sm', i)]
            TT(K, 'dve', M, ml3[i][:, 0, 0:8], ml3[i][:, 1, 0:8], ALU.min, r=k3, w=ks_)
            TT(K, 'dve', M, M, ml3[i][:, 2, 0:8], ALU.min, r=k3 + ks_, w=ks_)
            for g in range(3):
                TT(K, 'dve', w3[i][:, g, :], ml3[i][:, g, 0:8], M, ALU.subtract, r=k3 + ks_, w=[('w3', i)])
            ACTV(K, w3[i][:], w3[i][:], AF.Exp, r=[('w3', i)], w=[('w3', i)], scale=-1.0)
            for g in range(3):
                TT(K, 'dve', sm[i][:, 24 + 8 * g:32 + 8 * g], w3[i][:, g, :], ml3[i][:, g, 8:16], ALU.mult,
                   r=k3 + [('w3', i)], w=ks_)
            TT(K, 'dve', den, sm[i][:, 24:32], sm[i][:, 32:40], ALU.add, r=ks_, w=ks_)
            TT(K, 'dve', den, den, sm[i][:, 40:48], ALU.add, r=ks_, w=ks_)
            S.op('dve', lambda e, rden=rden, den=den: e.reciprocal(out=rden, in_=den), r=ks_, w=ks_)
            for g in range(3):
                TT(K, 'dve', w3[i][:, g, :], w3[i][:, g, :], rden, ALU.mult, r=ks_ + [('w3', i)], w=[('w3', i)])
            for g in range(3):
                wb_ = w3[i][:, g, :].unsqueeze(2).to_broadcast([128, 8, 64])
                src = o3[i][:, g, :].rearrange("p (s e) -> p s e", e=64)
                dst = (acc[i] if g == 0 else t2[i])[:].rearrange("p (s e) -> p s e", e=64)
                eng = 'dve' if g != 1 else 'pool'
                TT(K, eng, dst, src, wb_, ALU.mult, r=[('o3', i), ('w3', i)], w=[('acc', i) if g == 0 else ('t2', i)])
                if g > 0:
                    outap = acc[i][:] if g == 1 else ab[i][:]
                    TT(K, 'dve', outap, acc[i][:], t2[i][:], ALU.add, r=[('acc', i), ('t2', i)],
                       w=[('acc', i)] if g == 1 else [('ab', i)])
            for a_ in range(4):
                TR(K, pa[i][:, a_, :], ab[i][:, a_ * 128:(a_ + 1) * 128], r=[('ab', i), 'ident'], w=[('pa', i)])
            CP(K, 'act', aT[i][:], pa[i][:], r=[('pa', i)], w=[('aT', i)])
            DMA(K, 'sp', K.mixT[0:512, rows].rearrange("(a p) t -> p a t", p=128), aT[i][:], r=[('aT', i)])
        S.flush()


def phase_pool(K, l):
    nc, S = K.nc, K.S
    with ExitStack() as es:
        sb = lambda n, s, d: es.enter_context(nc.sbuf_tensor(_uniq(n), s, d))
        ps = lambda n, s, d: es.enter_context(nc.psum_tensor(_uniq(n), s, d))
        band = sb("band", [128, 20, 128], BF16)
        pw = sb("pw", [128, 4, 128], BF16)
        psc_ = sb("pscale", [128, 4], F32)
        ut = [sb("ut%d" % i, [128, 3, 512], BF16) for i in range(2)]
        rt = [sb("rt%d" % i, [128, 4, 128], BF16) for i in range(2)]
        ot = [sb("ot%d" % i, [128, 4, 128], BF16) for i in range(2)]
        pr = [ps("pr%d" % i, [128, 4, 128], F32) for i in range(2)]
        pq = [ps("pq%d" % i, [128, 4, 128], F32) for i in range(2)]
        DMA(K, 'sp', band[:], K.c_band.rearrange("g v p k -> p (g v) k"), w=['band'])
        DMA(K, 'pool', pw[:], K.pool_w[l].rearrange("g p k -> p g k"), w=['pw'])
        DMA(K, 'sp', psc_[:], K.pool_scale[l].rearrange("(g p) -> p g", p=128), w=['pscale'], slow=True)
        for ti in range(NQT):
            i = ti % 2
            lo = 0 if ti > 0 else 1
            hi = 3 if ti < NQT - 1 else 2
            R0 = 128 + (ti - 1) * 128
            DMA(K, 'sp', ut[i][:, lo:hi, :], K.uu[R0 + lo * 128:R0 + hi * 128, :].rearrange("(a p) c -> p a c", p=128),
                w=[('ut', i)])
            for g in range(4):
                own = 1 if ti == 0 else (2 if ti == NQT - 1 else 0)
                terms = [(1, own)]
                if ti > 0:
                    terms.append((0, 3))
                if ti < NQT - 1:
                    terms.append((2, 4))
                for n_, (a_, v) in enumerate(terms):
                    MM(K, pr[i][:, g, :], ut[i][:, a_, g * 128:(g + 1) * 128], band[:, g * 5 + v, :], n_ == 0, n_ == len(terms) - 1,
                       r=[('ut', i), 'band'], w=[('pr', i)])
            CP(K, 'act', rt[i][:], pr[i][:], r=[('pr', i)], w=[('rt', i)])
            for g in range(4):
                MM(K, pq[i][:, g, :], pw[:, g, :], rt[i][:, g, :], True, True, r=['pw', ('rt', i)], w=[('pq', i)])
            for g in range(4):
                TS(K, 'dve', ot[i][:, g, :], pq[i][:, g, :], psc_[:, g:g + 1], ALU.mult, r=[('pq', i), 'pscale'], w=[('ot', i)])
            DMA(K, 'sp', K.mixT[1536:2048, ti * 128:(ti + 1) * 128].rearrange("(a p) t -> p a t", p=128), ot[i][:],
                r=[('ot', i)])
        S.flush()


def phase_ssd(K, l):
    nc, S = K.nc, K.S
    NC_ = NQT
    with ExitStack() as es:
        sb = lambda n, s, d: es.enter_context(nc.sbuf_tensor(_uniq(n), s, d))
        bk = [es.enter_context(nc.psum_tensor(_uniq("bk%d" % i), [128, 512], F32)) for i in range(8)]
        B = lambda i: ('bk', i)
        tri = sb("tri", [128, 4, 128], F32)
        onesf = sb("onesf", [128, 128], F32)
        identf = sb("identf", [128, 128], F32)
        one_t = sb("one_t", [128, 1], F32)
        zt = sb("zt2", [128, 16], BF16)
        dt_all = sb("dt_all", [128, NC_, 32], F32)
        dta = sb("dta", [128, NC_, 32], F32)
        tmpa = sb("tmpa", [128, NC_, 32], F32)
        tmpb = sb("tmpb", [128, NC_, 32], F32)
        dtb = sb("dtb", [128, 32], F32)
        abc = sb("abc", [128, 32], F32)
        E = sb("E", [128, NC_, 64], F32)
        cd = sb("cd", [128, NC_, 32], F32)
        wx = sb("wx", [128, NC_, 4, 16], F32)
        cw = sb("cw", [128, 5, 12], F32)
        cb = sb("cb", [128, 12], F32)
        dg = sb("dg", [128, 12, 5, 128], BF16)
        dsk = sb("dsk", [128, 16], F32)
        nw = sb("nw", [128, 1024], F32)
        CTall = sb("CTall", [128, NC_, 2, 128], BF16)
        xin = sb("xin", [128, 12, 132], BF16)
        xc = sb("xc", [128, 12, 128], BF16)
        xsB = sb("xsB", [128, 1280], BF16)
        cbm = sb("cbm", [128, 2, 2, 128], F32)
        X = sb("X", [128, 2, 16, 128], F32)
        seg = sb("seg", [128, 2, 16, 128], BF16)
        MT = sb("MT", [128, 2, 16, 128], BF16)
        xdt = sb("xdt", [128, 2, 1024], BF16)
        xdd = sb("xdd", [128, 2, 1024], BF16)
        tA = sb("tA", [128, 1024], F32)
        tB = sb("tB", [128, 1024], F32)
        Sf = sb("Sf", [128, 1024], F32)
        Sfb = sb("Sfb", [128, 1024], BF16)
        stt = sb("stt", [128, 1024], F32)
        yb = sb("yb", [128, 1024], BF16)
        yT = sb("yT", [128, 8, 128], BF16)
        st = sb("st3", [128, 8], F32)

        DMA(K, 'sp', tri[:], K.c_tri.rearrange("v p k -> p v k"), w=['tri'])
        DMA(K, 'sp', identf[:], K.c_identf[:, :], w=['identf'])
        S.op('dve', lambda e: e.memset(onesf[:], 1.0), w=['onesf'])
        S.op('dve', lambda e: e.memset(one_t[:], 1.0), w=['one_t'])
        S.op('dve', lambda e: e.memset(zt[:], 0.0), w=['zt'])
        for a_ in range(12):
            DMA(K, 'sp', K.xbcT[a_ * 128:(a_ + 1) * 128, 0:2], zt[:, 0:2], r=['zt'], w=['xbcT'])
            DMA(K, 'sp', K.xbcT[a_ * 128:(a_ + 1) * 128, S_LEN + 2:S_LEN + 4], zt[:, 0:2], r=['zt'], w=['xbcT'])
        DMA(K, 'sp', dt_all[:], K.dtr.rearrange("(c p) k -> p c k", p=128), w=['dt_all'])
        DMA(K, 'sp', dtb[:], K.dt_bias[l:l + 1, :].partition_broadcast(128), w=['dtb'])
        DMA(K, 'sp', abc[:], K.a_log[l:l + 1, :].partition_broadcast(128), w=['abc'])
        DMA(K, 'sp', dsk[:], K.d_skip[l:l + 1, :].partition_broadcast(128), w=['dsk'])
        DMA(K, 'sp', nw[:], K.ssd_norm_w[l:l + 1, :].partition_broadcast(128), w=['nw'])
        for k in range(5):
            DMA(K, 'sp', cw[:, k, :], K.conv_w[l, k].rearrange("(ct p) -> p ct", p=128), w=['cw'], slow=True)
        DMA(K, 'sp', cb[:], K.conv_b[l].rearrange("(ct p) -> p ct", p=128), w=['cb'], slow=True)
        for ct in range(12):
            for k in range(5):
                TS(K, 'pool' if (ct + k) % 2 else 'dve', dg[:, ct, k, :], identf[:], cw[:, k, ct:ct + 1], ALU.mult,
                   r=['identf', 'cw'], w=['dg'])
        ACTV(K, abc[:], abc[:], AF.Exp, r=['abc'], w=['abc'])
        TS(K, 'dve', abc[:], abc[:], -1.0, ALU.mult, r=['abc'], w=['abc'])
        bc32 = lambda t: t[:].unsqueeze(1).to_broadcast([128, NC_, 32])
        TT(K, 'dve', dt_all[:], dt_all[:], bc32(dtb), ALU.add, r=['dt_all', 'dtb'], w=['dt_all'])
        TS(K, 'dve', tmpb[:], dt_all[:], -1.0, ALU.mult, r=['dt_all'], w=['tmpb'])
        TT(K, 'dve', tmpa[:], dt_all[:], tmpb[:], ALU.max, r=['dt_all', 'tmpb'], w=['tmpa'])
        ACTV(K, tmpa[:], tmpa[:], AF.Exp, r=['tmpa'], w=['tmpa'], scale=-1.0)
        ACTV(K, tmpa[:], tmpa[:], AF.Ln, r=['tmpa', 'one_t'], w=['tmpa'], bias=one_t[:, 0:1])
        TS(K, 'dve', tmpb[:], dt_all[:], 0.0, ALU.max, r=['dt_all'], w=['tmpb'])
        TT(K, 'dve', dt_all[:], tmpa[:], tmpb[:], ALU.add, r=['tmpa', 'tmpb'], w=['dt_all'])
        TT(K, 'dve', dta[:], dt_all[:], bc32(abc), ALU.mult, r=['dt_all', 'abc'], w=['dta'])
        for c in range(NC_):
            bi = c // 8
            for v in range(4):
                cols = slice((c % 8) * 64 + v * 16, (c % 8) * 64 + v * 16 + 16)
                dsl = slice(0, 16) if v < 2 else slice(16, 32)
                MM(K, bk[bi][:, cols], tri[:, v, :], dta[:, c, dsl], True, True, r=['tri', 'dta'], w=[B(bi)])
        for bi in range(4):
            ACTV(K, E[:, bi * 8:(bi + 1) * 8, :], bk[bi][:].rearrange("p (c k) -> p c k", k=64), AF.Exp, r=[B(bi)], w=['E'])
        for hf in range(2):
            MM(K, bk[4 + hf][:], onesf[:], dta[:, hf * 16:(hf + 1) * 16, :], True, True, r=['onesf', 'dta'], w=[B(4 + hf)])
            ACTV(K, cd[:, hf * 16:(hf + 1) * 16, :], bk[4 + hf][:].rearrange("p (c k) -> p c k", k=32), AF.Exp,
                 r=[B(4 + hf)], w=['cd'])
        for dr in range(2):
            CP(K, 'dve', wx[:, :, dr, :], dt_all[:, :, dr * 16:(dr + 1) * 16], r=['dt_all'], w=['wx'])
            TT(K, 'dve', wx[:, :, 2 + dr, :], dt_all[:, :, dr * 16:(dr + 1) * 16], E[:, :, 16 + 32 * dr:32 + 32 * dr], ALU.mult,
               r=['dt_all', 'E'], w=['wx'])
        S.op('dve', lambda e: e.memset(Sf[:], 0.0), w=['Sf'])
        S.op('dve', lambda e: e.memset(Sfb[:], 0.0), w=['Sfb'])
        bc = lambda ap, shape, ax: ap.unsqueeze(ax).to_broadcast(shape)
        nd = 0
        for c in range(NC_):
            T0 = c * 128
            DMA(K, 'sp', xin[:], K.xbcT[:, T0:T0 + 132].rearrange("(ct p) t -> p ct t", p=128), r=['xbcT'], w=['xin'])
            for ct in range(12):
                bi = ct // 4
                for k in range(5):
                    MM(K, bk[bi][:, (ct % 4) * 128:(ct % 4 + 1) * 128], dg[:, ct, k, :], xin[:, ct, k:k + 128], k == 0, k == 4,
                       r=['dg', 'xin'], w=[B(bi)])
            for ct in range(12):
                bi = ct // 4
                ACTV(K, xc[:, ct, :], bk[bi][:, (ct % 4) * 128:(ct % 4 + 1) * 128], AF.Silu, r=[B(bi), 'cb'], w=['xc'],
                     bias=cb[:, ct:ct + 1])
            CP(K, 'pool', CTall[:, c, :, :], xc[:, 10:12, :], r=['xc'], w=['CTall'])
            bv3 = bk[3][:].bitcast(BF16)
            bv4 = bk[4][:].bitcast(BF16)
            for ct in range(10):
                dst = bv3[:, ct * 128:(ct + 1) * 128] if ct < 8 else bv4[:, (ct - 8) * 128:(ct - 7) * 128]
                TR(K, dst, xc[:, ct, :], r=['xc', 'ident'], w=[B(3) if ct < 8 else B(4)])
            CP(K, 'dve', xsB[:, 0:1024], bv3[:, 0:1024], r=[B(3)], w=['xsB'])
            CP(K, 'dve', xsB[:, 1024:1280], bv4[:, 0:256], r=[B(4)], w=['xsB'])
            for g in range(2):
                MM(K, bk[5][:, g * 128:(g + 1) * 128], xc[:, 8 + g, :], xc[:, 10 + g, :], True, True, r=['xc'], w=[B(5)])
            for dr in range(2):
                TT(K, 'dve', cbm[:, dr, :, :], bk[5][:, 0:256].rearrange("p (g l) -> p g l", g=2),
                   bc(tri[:, 0 if dr == 0 else 2, :], [128, 2, 128], 1), ALU.mult, r=[B(5), 'tri'], w=['cbm'])
            for dr in range(2):
                TT(K, 'pool', X[:, dr, :, :], bc(tri[:, 0 if dr == 0 else 2, :], [128, 16, 128], 1),
                   bc(dta[:, c, dr * 16:(dr + 1) * 16], [128, 16, 128], 2), ALU.mult, r=['tri', 'dta'], w=['X'])
            for dr in range(2):
                for q4 in range(4):
                    bi = 6 + (nd % 2)
                    nd += 1
                    MM(K, bk[bi][:], tri[:, 1 if dr == 0 else 3, :], X[:, dr, q4 * 4:(q4 + 1) * 4, :], True, True,
                       r=['tri', 'X'], w=[B(bi)])
                    ACTV(K, seg[:, dr, q4 * 4:(q4 + 1) * 4, :], bk[bi][:].rearrange("p (h l) -> p h l", h=4), AF.Exp,
                         r=[B(bi)], w=['seg'])
                for g in range(2):
                    TT(K, 'dve' if g == 0 else 'pool', MT[:, dr, g * 8:(g + 1) * 8, :], seg[:, dr, g * 8:(g + 1) * 8, :],
                       bc(cbm[:, dr, g, :], [128, 8, 128], 1), ALU.mult, r=['seg', 'cbm'], w=['MT'])
            xs3 = xsB[:, 0:1024].rearrange("p (h e) -> p h e", e=64)
            for dr in range(2):
                TT(K, 'pool', xdt[:, dr, :].rearrange("p (h e) -> p h e", e=64), xs3, bc(wx[:, c, dr, :], [128, 16, 64], 2),
                   ALU.mult, r=['xsB', 'wx'], w=['xdt'])
                TT(K, 'dve', xdd[:, dr, :].rearrange("p (h e) -> p h e", e=64), xs3, bc(wx[:, c, 2 + dr, :], [128, 16, 64], 2),
                   ALU.mult, r=['xsB', 'wx'], w=['xdd'])
            for hh in range(16):
                bi = hh // 8
                cols = slice((hh % 8) * 64, (hh % 8) * 64 + 64)
                MM(K, bk[bi][:, cols], MT[:, 0, hh, :], xdt[:, 0, hh * 64:(hh + 1) * 64], True, False, r=['MT', 'xdt'], w=[B(bi)])
                MM(K, bk[bi][:, cols], MT[:, 1, hh, :], xdt[:, 1, hh * 64:(hh + 1) * 64], False, True, r=['MT', 'xdt'], w=[B(bi)])
            for g in range(2):
                MM(K, bk[2 + g][:], xc[:, 10 + g, :], Sfb[:, g * 512:(g + 1) * 512], True, True, r=['xc', 'Sfb'], w=[B(2 + g)])
            for dr in range(2):
                for g in range(2):
                    MM(K, bk[4 + 2 * dr + g][:], xsB[:, 1024 + g * 128:1024 + (g + 1) * 128], xdd[:, dr, g * 512:(g + 1) * 512],
                       True, True, r=['xsB', 'xdd'], w=[B(4 + 2 * dr + g)])
            for g in range(2):
                cs = slice(g * 512, (g + 1) * 512)
                v3 = lambda ap: ap.rearrange("p (h e) -> p h e", e=64)
                TT(K, 'dve', v3(tA[:, cs]), v3(bk[2 + g][:]), bc(E[:, c, g * 8:(g + 1) * 8], [128, 8, 64], 2), ALU.mult,
                   r=[B(2 + g), 'E'], w=['tA'])
                TT(K, 'dve', tA[:, cs], tA[:, cs], bk[g][:], ALU.add, r=['tA', B(g)], w=['tA'])
            TT(K, 'pool', tB[:].rearrange("p (h e) -> p h e", e=64), xs3, bc(dsk[:, :], [128, 16, 64], 2), ALU.mult,
               r=['xsB', 'dsk'], w=['tB'])
            TT(K, 'pool', tA[:], tA[:], tB[:], ALU.add, r=['tA', 'tB'], w=['tA'])
            DMA(K, 'sp', K.ypart[T0:T0 + 128, :], tA[:], r=['tA'], w=['ypart'])
            for g in range(2):
                CP(K, 'act', stt[:, g * 512:(g + 1) * 512], bk[6 + g][:], r=[B(6 + g)], w=['stt'])
            DMA(K, 'sp', K.stb[c], stt[:], r=['stt'], w=['stb'])
            TT(K, 'dve', Sf[:].rearrange("p (h e) -> p h e", e=64), Sf[:].rearrange("p (h e) -> p h e", e=64),
               bc(cd[:, c, 0:16], [128, 16, 64], 2), ALU.mult, r=['Sf', 'cd'], w=['Sf'])
            for g in range(2):
                cs = slice(g * 512, (g + 1) * 512)
                TT(K, 'dve', Sf[:, cs], Sf[:, cs], bk[4 + g][:], ALU.add, r=['Sf', B(4 + g)], w=['Sf'])
            CP(K, 'act', Sfb[:], Sf[:], r=['Sf'], w=['Sfb'])
        S.op('dve', lambda e: e.memset(Sf[:], 0.0), w=['Sf'])
        S.op('dve', lambda e: e.memset(Sfb[:], 0.0), w=['Sfb'])
        for c in range(NC_ - 1, -1, -1):
            T0 = c * 128
            DMA(K, 'sp', tA[:], K.ypart[T0:T0 + 128, :], r=['ypart'], w=['tA'])
            DMA(K, 'sp', tB[:], K.zz[T0:T0 + 128, :], w=['tB'])
            DMA(K, 'sp', stt[:], K.stb[c], r=['stb'], w=['stt'])
            for g in range(2):
                MM(K, bk[g][:], CTall[:, c, g, :], Sfb[:, g * 512:(g + 1) * 512], True, True, r=['CTall', 'Sfb'], w=[B(g)])
            for g in range(2):
                cs = slice(g * 512, (g + 1) * 512)
                v3 = lambda ap: ap.rearrange("p (h e) -> p h e", e=64)
                xq = X[:, 0, 0:4, :].rearrange("p a b -> p (a b)")
                TT(K, 'dve', v3(xq), v3(bk[g][:]), bc(E[:, c, 32 + g * 8:40 + g * 8], [128, 8, 64], 2), ALU.mult,
                   r=[B(g), 'E'], w=['X'])
                TT(K, 'dve', tA[:, cs], tA[:, cs], xq, ALU.add, r=['tA', 'X'], w=['tA'])
            ACTV(K, tB[:], tB[:], AF.Silu, r=['tB'], w=['tB'])
            TT(K, 'dve', tA[:], tA[:], tB[:], ALU.mult, r=['tA', 'tB'], w=['tA'])
            for g in range(2):
                cs = slice(g * 512, (g + 1) * 512)
                STT(K, tB[:, cs], tA[:, cs], 1.0, tA[:, cs], ALU.mult, ALU.mult, r=['tA'], w=['tB', 'st'], accum_out=st[:, g:g + 1])
            ACTV(K, st[:, 2:4], st[:, 0:2], AF.Sqrt, r=['st'], w=['st'], scale=1.0 / 512, bias=K.eps_t[:, 0:1])
            S.op('dve', lambda e: e.reciprocal(out=st[:, 4:6], in_=st[:, 2:4]), r=['st'], w=['st'])
            for g in range(2):
                cs = slice(g * 512, (g + 1) * 512)
                STT(K, yb[:, cs], tA[:, cs], st[:, 4 + g:5 + g], nw[:, cs], ALU.mult, ALU.mult, r=['tA', 'st', 'nw'], w=['yb'])
            bv3 = bk[3][:].bitcast(BF16)
            for a_ in range(8):
                TR(K, bv3[:, a_ * 128:(a_ + 1) * 128], yb[:, a_ * 128:(a_ + 1) * 128], r=['yb', 'ident'], w=[B(3)])
            CP(K, 'act', yT[:], bv3[:, 0:1024].rearrange("p (a t) -> p a t", a=8), r=[B(3)], w=['yT'])
            DMA(K, 'sp', K.mixT[512:1536, T0:T0 + 128].rearrange("(a p) t -> p a t", p=128), yT[:], r=['yT'])
            TT(K, 'dve', Sf[:].rearrange("p (h e) -> p h e", e=64), Sf[:].rearrange("p (h e) -> p h e", e=64),
               bc(cd[:, c, 16:32], [128, 16, 64], 2), ALU.mult, r=['Sf', 'cd'], w=['Sf'])
            TT(K, 'dve', Sf[:], Sf[:], stt[:], ALU.add, r=['Sf', 'stt'], w=['Sf'])
            CP(K, 'act', Sfb[:], Sf[:], r=['Sf'], w=['Sfb'])
        S.flush()


def phase_B(K, l):
    phase_att(K, l)
    phase_pool(K, l)
    phase_ssd(K, l)


def setup_common(K, es):
    nc = K.nc
    K.ident = es.enter_context(nc.sbuf_tensor("ident", [128, 128], BF16))
    K.eps_t = es.enter_context(nc.sbuf_tensor("eps_t", [128, 1], F32))
    K.S.op('sp', lambda e: e.dma_start(out=K.ident[:], in_=K.c_ident[:, :]), w=['ident'], dma=True)
    K.S.op('dve', lambda e: e.memset(K.eps_t[:], EPS), w=['eps'])
    K.S.flush()


def declare_io(K, nc, dbg):
    dbg = dbg or {}
    di = lambda n, s, d: nc.dram_tensor(n, s, d, kind="ExternalInput").ap()
    K.x = di("x", [S_LEN, D], F32)
    K.norm1_w = di("norm1_w", [DEPTH, D], F32)
    K.w_in = di("w_in", [DEPTH, D, IN_W], F32)
    K.conv_w = di("conv_w", [DEPTH, 5, 1536], F32)
    K.conv_b = di("conv_b", [DEPTH, 1536], F32)
    K.dt_bias = di("dt_bias", [DEPTH, 32], F32)
    K.a_log = di("a_log", [DEPTH, 32], F32)
    K.d_skip = di("d_skip", [DEPTH, 16], F32)
    K.ssd_norm_w = di("ssd_norm_w", [DEPTH, 1024], F32)
    K.pool_w = di("pool_w", [DEPTH, 4, 128, 128], F32)
    K.pool_scale = di("pool_scale", [DEPTH, 512], F32)
    K.w_out = di("w_out", [DEPTH, D, D], F32)
    K.norm2_w = di("norm2_w", [DEPTH, D], F32)
    K.w_gate = di("w_gate", [DEPTH, D, DFF], F32)
    K.w_up = di("w_up", [DEPTH, D, DFF], F32)
    K.w_down = di("w_down", [DEPTH, DFF, D], F32)
    K.final_norm_w = di("final_norm_w", [1, D], F32)
    K.c_ident = di("c_ident", [128, 128], BF16)
    K.c_identf = di("c_identf", [128, 128], F32)
    K.c_tri = di("c_tri", [4, 128, 128], F32)
    K.c_b0 = di("c_b0", [9, 128, 256], BF16)
    K.c_sid = di("c_sid", [24, 128, 128], BF16)
    K.c_band = di("c_band", [4, 5, 128, 128], BF16)
    K.out = nc.dram_tensor("out", [S_LEN, D], F32, kind="ExternalOutput").ap()
    dsc = lambda n, s, d: nc.dram_tensor(n, s, d, kind=dbg.get(n, "Internal")).ap()
    NP = S_LEN + 128
    K.qT = dsc("qT", [1536, NP], BF16)
    K.kT = dsc("kT", [1536, NP], BF16)
    K.vv = dsc("vv", [3, NP, 512], BF16)
    K.zz = dsc("zz", [S_LEN, 1024], F32)
    K.xbcT = dsc("xbcT", [1536, S_LEN + 4], BF16)
    K.dtr = dsc("dtr", [S_LEN, 32], F32)
    K.uu = dsc("uu", [S_LEN + 256, 512], BF16)
    K.mixT = dsc("mixT", [D, S_LEN], BF16)
    K.x1 = dsc("x1", [S_LEN, D], F32)
    K.og = dsc("og", [3, S_LEN, 512], F32)
    K.mlg = dsc("mlg", [3, S_LEN, 16], F32)
    K.ypart = dsc("ypart", [S_LEN, 1024], F32)
    K.stb = dsc("stb", [NQT, 128, 1024], F32)


def build(dbg=None, phases=None):
    nc = bass.Bass("TRN2", target_bir_lowering=False)
    K = Ctx()
    K.nc = nc
    declare_io(K, nc, dbg)
    with ExitStack() as es:
        K.S = Sched(nc, es)
        setup_common(K, es)
        if phases is not None:
            phases(K)
        else:
            for l in range(DEPTH):
                xs = K.x if l == 0 else K.x1
                phase_A(K, l, xs)
                phase_B(K, l)
                if l == DEPTH - 1:
                    phase_CD(K, l, xs, None, final_w=K.final_norm_w[0:1, :], out_dst=K.out)
                else:
                    phase_CD(K, l, xs, K.x1)
    return nc


def host_consts():
    bf = ml_dtypes.bfloat16
    c = {}
    c["c_ident"] = np.eye(128, dtype=np.float32).astype(bf)
    c["c_identf"] = np.eye(128, dtype=np.float32)
    j = np.arange(128)[:, None]
    l_ = np.arange(128)[None, :]
    c["c_tri"] = np.stack([(j <= l_), (j > l_), (j >= l_), (j < l_)]).astype(np.float32)
    i = np.arange(128)[:, None]
    jj = np.arange(256)[None, :]
    rel = jj - 64 - i
    b0 = np.zeros((9, 128, 256), np.float32)
    BIG = 1.0e6
    for g, d in enumerate(DILS):
        for v in range(3):
            ok = np.abs(rel) <= 64
            if v == 1:
                ok = ok & (jj >= 64)
            if v == 2:
                ok = ok & (jj < 192)
            b0[g * 3 + v] = np.where(ok, -np.abs(rel) * float(d), -BIG)
    c["c_b0"] = b0.astype(bf)
    kk = np.arange(1, 25, dtype=np.float32)
    slopes = (2.0 ** (-8.0 * kk / 24.0)).astype(np.float32)
    c["c_sid"] = (np.eye(128, dtype=np.float32)[None] * slopes[:, None, None]).astype(bf)
    band = np.zeros((4, 5, 128, 128), np.float32)
    tp = np.arange(128)[:, None]
    t = np.arange(128)[None, :]
    for g, w in enumerate((2, 4, 8, 16)):
        hw = w // 2
        inwin = (tp >= t - hw) & (tp < t + hw)
        eye = (tp == t).astype(np.float32)
        band[g, 0] = inwin / float(w) - eye
        cnt_first = (t + hw) - np.maximum(t - hw, 0)
        band[g, 1] = inwin / cnt_first.astype(np.float32) - eye
        cnt_last = np.minimum(t + hw, 128) - (t - hw)
        band[g, 2] = inwin / cnt_last.astype(np.float32) - eye
        band[g, 3] = ((tp - 128) >= t - hw) / float(w)
        band[g, 4] = ((tp + 128) < t + hw) / float(w)
    c["c_band"] = band.astype(bf)
    return c


def make_inputs(inputs, b):
    f = lambda a: np.ascontiguousarray(np.asarray(a, dtype=np.float32))
    im = {"x": f(inputs["x"][b])}
    for n in ("norm1_w", "w_in", "conv_w", "conv_b", "d_skip", "ssd_norm_w", "pool_w", "pool_scale", "w_out",
              "norm2_w", "w_gate", "w_up", "w_down"):
        im[n] = f(inputs[n])
    im["dt_bias"] = f(inputs["dt_bias"]).reshape(DEPTH, 32)
    im["a_log"] = f(inputs["a_log"]).reshape(DEPTH, 32)
    im["final_norm_w"] = f(inputs["final_norm_w"]).reshape(1, D)
    im.update(host_consts())
    return im


_NC_CACHE = {}


def kernel(**inputs):
    if "nc" not in _NC_CACHE:
        _NC_CACHE["nc"] = build()
    nc = _NC_CACHE["nc"]
    B = inputs["x"].shape[0]
    in_maps = [make_inputs(inputs, c % B) for c in range(8)]
    res = run_bass_kernel_spmd(nc, in_maps, core_ids=list(range(8)))
    out = np.stack([np.asarray(res.results[b]["out"], dtype=np.float32) for b in range(B)], axis=0)
    return out
```

```python
import numpy as np
from contextlib import ExitStack
import ml_dtypes
import concourse.bass as bass
import concourse.mybir as mybir
from concourse.bass_utils import run_bass_kernel_spmd

F32, BF16 = mybir.dt.float32, mybir.dt.bfloat16
AF = mybir.ActivationFunctionType
ALU = mybir.AluOpType
AX = mybir.AxisListType

D = 2048
S_LEN = 4096
DEPTH = 2
IN_W = 7712
DFF = 5632
EPS = 1e-6
NQT = S_LEN // 128
DILS = (1, 4, 16)


class Sched:
    BLK = {'pe': 'tensor', 'act': 'scalar', 'dve': 'vector', 'pool': 'gpsimd', 'sp': 'sync'}
    NSLOT = {'sp': 28, 'pool': 12, 'act': 6}

    def __init__(self, nc, es):
        self.nc = nc
        self.sem = {e: es.enter_context(nc.semaphore("s_" + e)) for e in ('pe', 'act', 'dve', 'pool')}
        self.dsem = {e: [es.enter_context(nc.semaphore("d_%s%d" % (e, i))) for i in range(n)]
                     for e, n in self.NSLOT.items()}
        self.cnt = {e: 0 for e in self.sem}
        self.slot_uses = {e: [0] * n for e, n in self.NSLOT.items()}
        self.next_slot = {e: 0 for e in self.NSLOT}
        self.waited = {e: {} for e in self.BLK}
        self.reset()

    def reset(self):
        self.ops = []
        self.lw = {}
        self.rd = {}

    def op(self, eng, fn, r=(), w=(), dma=False):
        deps = set()
        for b in r:
            x = self.lw.get(b)
            if x is not None:
                deps.add(x)
        for b in w:
            x = self.lw.get(b)
            if x is not None:
                deps.add(x)
            rb = self.rd.get(b)
            if rb:
                deps.update(rb.values())
        i = len(self.ops)
        self.ops.append([eng, fn, deps, dma, False, 0, 0, 0])
        for b in r:
            self.rd.setdefault(b, {})[('d', i) if dma else eng] = i
        for b in w:
            self.lw[b] = i
            self.rd[b] = {}
        return i

    def flush(self, name=None):
        ops = self.ops
        for o in ops:
            for d in o[2]:
                D_ = ops[d]
                if D_[3]:
                    continue
                if D_[0] == 'pe' and o[0] == 'pe' and not o[3]:
                    continue
                D_[4] = True
        for o in ops:
            e = o[0]
            if o[3]:
                s = self.next_slot[e]
                self.next_slot[e] = (s + 1) % self.NSLOT[e]
                o[7] = 16 * self.slot_uses[e][s]
                self.slot_uses[e][s] += 1
                o[5] = 16 * self.slot_uses[e][s]
                o[6] = s
            elif o[4]:
                self.cnt[e] += 1
                o[5] = self.cnt[e]
        with self.nc.Block() as block:
            for e, bname in self.BLK.items():
                eops = [o for o in ops if o[0] == e]
                if not eops:
                    continue

                def body(eng, eops=eops, e=e):
                    waited = self.waited[e]
                    for o in eops:
                        reqs = {}
                        for d in o[2]:
                            D_ = ops[d]
                            if D_[3]:
                                key = ('d', D_[0], D_[6])
                            else:
                                if D_[0] == 'pe' and e == 'pe' and not o[3]:
                                    continue
                                key = ('e', D_[0])
                            if reqs.get(key, 0) < D_[5]:
                                reqs[key] = D_[5]
                        if o[3] and o[7] > 0:
                            key = ('d', e, o[6])
                            if reqs.get(key, 0) < o[7]:
                                reqs[key] = o[7]
                        for key, val in reqs.items():
                            if waited.get(key, 0) < val:
                                sem = self.sem[key[1]] if key[0] == 'e' else self.dsem[key[1]][key[2]]
                                eng.wait_ge(sem, val)
                                waited[key] = val
                        ins = o[1](eng)
                        if o[3]:
                            ins.then_inc(self.dsem[e][o[6]], 16)
                        elif o[4]:
                            ins.then_inc(self.sem[e], 1)
                    if e in self.NSLOT:
                        for s in range(self.NSLOT[e]):
                            val = 16 * self.slot_uses[e][s]
                            key = ('d', e, s)
                            if waited.get(key, 0) < val:
                                eng.wait_ge(self.dsem[e][s], val)
                                waited[key] = val

                getattr(block, bname)(body)
        self.reset()


class Ctx:
    pass


_UID = [0]


def _uniq(n):
    _UID[0] += 1
    return "%s_%d" % (n, _UID[0])


def MM(K, out, lhsT, rhs, start, stop, r, w):
    K.S.op('pe', lambda e: e.matmul(out, lhsT=lhsT, rhs=rhs, start=start, stop=stop), r=r, w=w)


def TR(K, out, in_, r, w, ident=None):
    idn = K.ident[:] if ident is None else ident
    K.S.op('pe', lambda e: e.transpose(out=out, in_=in_, identity=idn), r=r, w=w)


def DMA(K, q, out, in_, r=(), w=(), slow=False):
    if slow:
        K.S.op(q, lambda e: e.dma_start(out=out, in_=in_, allow_slow_non_contiguous=True), r=r, w=w, dma=True)
    else:
        K.S.op(q, lambda e: e.dma_start(out=out, in_=in_), r=r, w=w, dma=True)


def ACTV(K, out, in_, func, r, w, bias=None, scale=None, accum_out=None):
    kw = {}
    if bias is not None:
        kw['bias'] = bias
    if scale is not None:
        kw['scale'] = scale
    if accum_out is not None:
        kw['accum_out'] = accum_out
    K.S.op('act', lambda e: e.activation(out=out, in_=in_, func=func, **kw), r=r, w=w)


def TT(K, eng, out, in0, in1, op, r, w):
    K.S.op(eng, lambda e: e.tensor_tensor(out=out, in0=in0, in1=in1, op=op), r=r, w=w)


def TS(K, eng, out, in0, s1, op0, r, w, s2=None, op1=None, accum_out=None):
    kw = {}
    if op1 is not None:
        kw['op1'] = op1
    if accum_out is not None:
        kw['accum_out'] = accum_out
    K.S.op(eng, lambda e: e.tensor_scalar(out=out, in0=in0, scalar1=s1, scalar2=s2, op0=op0, **kw), r=r, w=w)


def STT(K, out, in0, scalar, in1, op0, op1, r, w, accum_out=None):
    kw = {}
    if accum_out is not None:
        kw['accum_out'] = accum_out
    K.S.op('dve', lambda e: e.scalar_tensor_tensor(out=out, in0=in0, scalar=scalar, in1=in1, op0=op0, op1=op1, **kw),
           r=r, w=w)


def CP(K, eng, out, in_, r, w):
    if eng == 'act':
        K.S.op('act', lambda e: e.activation(out=out, in_=in_, func=AF.Copy), r=r, w=w)
    else:
        K.S.op(eng, lambda e: e.tensor_copy(out=out, in_=in_), r=r, w=w)


def _evac(K, idx, out, in_, r, w, scale=None):
    if idx % 2 == 0:
        if scale is None:
            K.S.op('act', lambda e: e.activation(out=out, in_=in_, func=AF.Copy), r=r, w=w)
        else:
            K.S.op('act', lambda e: e.activation(out=out, in_=in_, func=AF.Copy, scale=scale), r=r, w=w)
    else:
        if scale is None:
            K.S.op('dve', lambda e: e.tensor_copy(out=out, in_=in_), r=r, w=w)
        else:
            K.S.op('dve', lambda e: e.tensor_scalar(out=out, in0=in_, scalar1=scale, scalar2=None, op0=ALU.mult), r=r, w=w)


def rms_to_hT(K, es_tiles, x_src, tok0, ntile, wbc, hT, hT_key, xt, hb, st, junk, ptr, st_base=0, keep_x=None):
    S = K.S
    for tt in range(ntile):
        i = tt % len(hb)
        T0 = tok0 + tt * 128
        xti = xt[i] if keep_x is None else keep_x[tt]
        xkey = ('xt', i) if keep_x is None else ('xk', tt)
        if keep_x is None:
            S.op('sp', lambda e, xti=xti, T0=T0: e.dma_start(out=xti[:], in_=x_src[T0:T0 + 128, :]), w=[xkey], dma=True)
        c = (st_base + tt) * 4
        sk = ('st', st_base + tt)
        S.op('dve', lambda e, xti=xti, c=c: e.scalar_tensor_tensor(
            out=junk[:], in0=xti[:], scalar=1.0, in1=xti[:], op0=ALU.mult, op1=ALU.mult,
            accum_out=st[:, c:c + 1]), r=[xkey], w=['junk', sk])
        S.op('act', lambda e, c=c: e.activation(out=st[:, c + 1:c + 2], in_=st[:, c:c + 1], func=AF.Sqrt,
                                                scale=1.0 / D, bias=K.eps_t[:, 0:1]), r=[sk], w=[sk])
        S.op('dve', lambda e, c=c: e.reciprocal(out=st[:, c + 2:c + 3], in_=st[:, c + 1:c + 2]), r=[sk], w=[sk])
        S.op('dve', lambda e, xti=xti, c=c, i=i: e.scalar_tensor_tensor(
            out=hb[i][:], in0=xti[:], scalar=st[:, c + 2:c + 3], in1=wbc[:], op0=ALU.mult, op1=ALU.mult),
            r=[xkey, sk, 'wbc'], w=[('hb', i)])
        for hh in range(2):
            for cc in range(8):
                ch = hh * 8 + cc
                S.op('pe', lambda e, hh=hh, cc=cc, ch=ch, i=i: e.transpose(
                    out=ptr[hh][:, cc, :], in_=hb[i][:, ch * 128:(ch + 1) * 128], identity=K.ident[:]),
                    r=[('hb', i), 'ident'], w=[('ptr', hh)])
            _evac(K, hh, hT[:, hh * 8:(hh + 1) * 8, tt * 128:(tt + 1) * 128], ptr[hh][:],
                  r=[('ptr', hh)], w=[(hT_key, tt)])


def phase_A(K, l, x_src):
    nc, S = K.nc, K.S
    with ExitStack() as es:
        sb = lambda n, s, d: es.enter_context(nc.sbuf_tensor(_uniq(n), s, d))
        ps = lambda n, s, d: es.enter_context(nc.psum_tensor(_uniq(n), s, d))
        hT = sb("hT", [128, 16, 2048], BF16)
        wbc = sb("wbc", [128, D], F32)
        xt = [sb("xt%d" % i, [128, D], F32) for i in range(2)]
        junk = sb("junk", [128, D], BF16)
        hb = [sb("hb%d" % i, [128, D], BF16) for i in range(2)]
        st = sb("st", [128, 16 * 4], F32)
        wbuf = [sb("wb%d" % i, [128, 16, 544], BF16) for i in range(2)]
        obh = [sb("obh%d" % i, [128, 512], BF16) for i in range(4)]
        obf = [sb("obf%d" % i, [128, 512], F32) for i in range(4)]
        obig = [sb("obig%d" % i, [128, 2048], BF16) for i in range(3)]
        ptr = [ps("ptr%d" % i, [128, 8, 128], BF16) for i in range(2)]
        pmm = [ps("pmm%d" % i, [128, 512], F32) for i in range(4)]

        S.op('sp', lambda e: e.dma_start(out=wbc[:], in_=K.norm1_w[l:l + 1, :].partition_broadcast(128)),
             w=['wbc'], dma=True)
        blocks = []
        for g in range(3):
            blocks.append(('qk', K.qT, g, g * 512, 512))
        for g in range(3):
            blocks.append(('qk', K.kT, g, 1536 + g * 512, 512))
        for g in range(3):
            blocks.append(('v', None, g, 3072 + g * 512, 512))
        for j in range(2):
            blocks.append(('z', None, j, 4608 + j * 512, 512))
        for j in range(3):
            blocks.append(('xbc', None, j, 5632 + j * 512, 512))
        blocks.append(('dtu', None, 0, 7168, 544))
        hT_all = [('hT', t) for t in range(16)]
        nblk = 0
        kk = 0
        nbig = 0

        def load_w(bi, c0, wd):
            src = K.w_in[l, :, c0:c0 + wd].rearrange("(c p) n -> p c n", p=128)
            S.op('pool', lambda e: e.dma_start(out=wbuf[bi][:, :, 0:wd], in_=src), w=[('wb', bi)], dma=True)

        for half in range(2):
            H0 = half * 2048
            load_w(nblk % 2, blocks[0][3], blocks[0][4])
            rms_to_hT(K, es, x_src, H0, 16, wbc, hT, 'hT', xt, hb, st, junk, ptr)
            for bidx, (mode, dst, g, c0, wd) in enumerate(blocks):
                bi = nblk % 2
                nblk += 1
                if bidx + 1 < len(blocks):
                    load_w(nblk % 2, blocks[bidx + 1][3], blocks[bidx + 1][4])
                wb = wbuf[bi]
                wkey = ('wb', bi)
                rk = [wkey] + hT_all
                if mode == 'qk':
                    d = DILS[g]
                    npr = 2048 // d
                    for ct in range(4):
                        dview = dst[g * 512 + ct * 128:g * 512 + (ct + 1) * 128, 64:64 + S_LEN] \
                            .rearrange("p (r u) -> p r u", r=d)
                        ob_i = nbig % 3
                        nbig += 1
                        og_ = obig[ob_i][:].rearrange("p (r u) -> p r u", r=d)
                        for j in range(4):
                            k = kk % 4
                            kk += 1
                            for ch in range(16):
                                MM(K, pmm[k][:], wb[:, ch, ct * 128:(ct + 1) * 128], hT[:, ch, j * 512:(j + 1) * 512],
                                   ch == 0, ch == 15, r=[wkey] + hT_all[4 * j:4 * j + 4], w=[('pmm', k)])
                            _evac(K, kk, og_[:, :, j * (512 // d):(j + 1) * (512 // d)],
                                  pmm[k][:].rearrange("p (u r) -> p r u", r=d), r=[('pmm', k)], w=[('obig', ob_i)],
                                  scale=(0.125 if dst is K.qT else None))
                        DMA(K, 'sp', dview[:, :, half * npr:(half + 1) * npr], og_, r=[('obig', ob_i)])
                elif mode in ('v', 'z', 'dtu'):
                    d = DILS[g] if mode == 'v' else 1
                    npr = 2048 // d
                    L = S_LEN // d
                    for i in range(16):
                        k = kk % 4
                        kk += 1
                        r_ = (128 * i) // npr
                        u0 = (128 * i) % npr
                        t0 = r_ + d * u0
                        if d > 1:
                            lsel = lambda ch, t0=t0, d=d: hT[:, ch, t0:t0 + 127 * d + 1:d]
                            rki = [wkey] + hT_all
                        else:
                            lsel = lambda ch, i=i: hT[:, ch, i * 128:(i + 1) * 128]
                            rki = [wkey, hT_all[i]]
                        tok = H0 + i * 128
                        if mode == 'dtu':
                            for ch in range(16):
                                MM(K, pmm[k][:], lsel(ch), wb[:, ch, 32:544], ch == 0, ch == 15, r=rki, w=[('pmm', k)])
                            _evac(K, kk, obh[k][:], pmm[k][:], r=[('pmm', k)], w=[('obh', k)])
                            DMA(K, 'sp', K.uu[128 + tok:128 + tok + 128, :], obh[k][:], r=[('obh', k)])
                            k2 = kk % 4
                            kk += 1
                            for ch in range(16):
                                MM(K, pmm[k2][:, 0:32], lsel(ch), wb[:, ch, 0:32], ch == 0, ch == 15, r=rki, w=[('pmm', k2)])
                            _evac(K, kk, obf[k2][:, 0:32], pmm[k2][:, 0:32], r=[('pmm', k2)], w=[('obf', k2)])
                            DMA(K, 'sp', K.dtr[tok:tok + 128, :], obf[k2][:, 0:32], r=[('obf', k2)])
                            continue
                        for ch in range(16):
                            MM(K, pmm[k][:], lsel(ch), wb[:, ch, 0:512], ch == 0, ch == 15, r=rki, w=[('pmm', k)])
                        if mode == 'v':
                            P = 64 + r_ * L + half * npr + u0
                            _evac(K, kk, obh[k][:], pmm[k][:], r=[('pmm', k)], w=[('obh', k)])
                            DMA(K, 'sp', K.vv[g, P:P + 128, :], obh[k][:], r=[('obh', k)])
                        else:
                            _evac(K, kk, obf[k][:], pmm[k][:], r=[('pmm', k)], w=[('obf', k)])
                            DMA(K, 'sp', K.zz[tok:tok + 128, g * 512:(g + 1) * 512], obf[k][:], r=[('obf', k)])
                else:
                    for ct in range(4):
                        for j in range(4):
                            k = kk % 4
                            kk += 1
                            for ch in range(16):
                                MM(K, pmm[k][:], wb[:, ch, ct * 128:(ct + 1) * 128], hT[:, ch, j * 512:(j + 1) * 512],
                                   ch == 0, ch == 15, r=[wkey] + hT_all[4 * j:4 * j + 4], w=[('pmm', k)])
                            _evac(K, kk, obh[k][:], pmm[k][:], r=[('pmm', k)], w=[('obh', k)])
                            row = g * 512 + ct * 128
                            col = 2 + H0 + j * 512
                            DMA(K, 'sp', K.xbcT[row:row + 128, col:col + 512], obh[k][:], r=[('obh', k)])
        S.flush()


def phase_CD(K, l, x_src, x_dst, final_w=None, out_dst=None):
    nc, S = K.nc, K.S
    TB = 512
    NT = TB // 128
    with ExitStack() as es:
        sb = lambda n, s, d: es.enter_context(nc.sbuf_tensor(_uniq(n), s, d))
        ps = lambda n, s, d: es.enter_context(nc.psum_tensor(_uniq(n), s, d))
        h2T = sb("h2T", [128, 16, TB], BF16)
        xm = [sb("xm%d" % i, [128, D], F32) for i in range(NT)]
        wbc = sb("wbc2", [128, D], F32)
        wbf = sb("wbcf", [128, D], F32) if final_w is not None else None
        hb = [sb("hb2_%d" % i, [128, D], BF16) for i in range(2)]
        junk = sb("junk2", [128, D], BF16)
        st = sb("st2", [128, 8 * NT * 4], F32)
        wA = [sb("wA%d" % i, [128, 16, 512], BF16) for i in range(2)]
        wB = [sb("wB%d" % i, [128, 16, 512], BF16) for i in range(2)]
        ffT = [sb("ffT%d" % i, [128, 4, TB], BF16) for i in range(2)]
        wd = [sb("wd%d" % i, [128, 4, D], BF16) for i in range(1)]
        tmp = [sb("tmp%d" % i, [128, 512], F32) for i in range(2)]
        ptr = [ps("ptr2_%d" % i, [128, 8, 128], BF16) for i in range(2)]
        pgu = [ps("pgu%d" % i, [128, 512], F32) for i in range(4)]
        pd = [ps("pd%d" % i, [128, 512], F32) for i in range(2)]
        DMA(K, 'sp', wbc[:], K.norm2_w[l:l + 1, :].partition_broadcast(128), w=['wbc'])
        if final_w is not None:
            DMA(K, 'sp', wbf[:], final_w.partition_broadcast(128), w=['wbf'])
        na = 0
        nb = 0
        nf = 0
        kd = 0
        for blk in range(S_LEN // TB):
            T0 = blk * TB
            mixT = wB[nb % 2]
            mkey = ('wB', nb % 2)
            nb += 1
            DMA(K, 'sp', mixT[:], K.mixT[:, T0:T0 + TB].rearrange("(c p) t -> p c t", p=128), w=[mkey])
            for tt in range(NT):
                DMA(K, 'sp', xm[tt][:], x_src[T0 + tt * 128:T0 + (tt + 1) * 128, :], w=[('xk', tt)])
            for cb in range(4):
                wo = wA[na % 2]
                wkey = ('wA', na % 2)
                na += 1
                DMA(K, 'pool', wo[:], K.w_out[l, :, cb * 512:(cb + 1) * 512].rearrange("(c p) n -> p c n", p=128), w=[wkey])
                for tt in range(NT):
                    k = kd % 2
                    kd += 1
                    for ch in range(16):
                        MM(K, pd[k][:], mixT[:, ch, tt * 128:(tt + 1) * 128], wo[:, ch, :], ch == 0, ch == 15,
                           r=[mkey, wkey], w=[('pd', k)])
                    TT(K, 'dve', xm[tt][:, cb * 512:(cb + 1) * 512], pd[k][:], xm[tt][:, cb * 512:(cb + 1) * 512], ALU.add,
                       r=[('pd', k), ('xk', tt)], w=[('xk', tt)])
            rms_to_hT(K, None, None, 0, NT, wbc, h2T, 'h2T', None, hb, st, junk, ptr, st_base=(blk % 8) * NT, keep_x=xm)
            h_all = [('h2T', t) for t in range(NT)]
            for gi in range(DFF // 512):
                wg = wA[na % 2]
                gkey = ('wA', na % 2)
                na += 1
                wu = wB[nb % 2]
                ukey = ('wB', nb % 2)
                nb += 1
                DMA(K, 'pool', wg[:], K.w_gate[l, :, gi * 512:(gi + 1) * 512].rearrange("(c p) n -> p c n", p=128), w=[gkey])
                DMA(K, 'pool', wu[:], K.w_up[l, :, gi * 512:(gi + 1) * 512].rearrange("(c p) n -> p c n", p=128), w=[ukey])
                ff = ffT[nf % 2]
                fkey = ('ffT', nf % 2)
                nf += 1
                for fb in range(4):
                    pg = pgu[2 * (fb % 2)]
                    pu = pgu[2 * (fb % 2) + 1]
                    kg = ('pgu', 2 * (fb % 2))
                    ku = ('pgu', 2 * (fb % 2) + 1)
                    for ch in range(16):
                        MM(K, pg[:], wg[:, ch, fb * 128:(fb + 1) * 128], h2T[:, ch, :], ch == 0, ch == 15,
                           r=[gkey] + h_all, w=[kg])
                    for ch in range(16):
                        MM(K, pu[:], wu[:, ch, fb * 128:(fb + 1) * 128], h2T[:, ch, :], ch == 0, ch == 15,
                           r=[ukey] + h_all, w=[ku])
                    tm = tmp[fb % 2]
                    ACTV(K, tm[:], pg[:], AF.Silu, r=[kg], w=[('tmp', fb % 2)])
                    TT(K, 'dve', ff[:, fb, :], tm[:], pu[:], ALU.mult, r=[('tmp', fb % 2), ku], w=[fkey])
                wdn = wd[0]
                DMA(K, 'pool', wdn[:], K.w_down[l, gi * 512:(gi + 1) * 512, :].rearrange("(c p) n -> p c n", p=128), w=['wd'])
                for tt in range(NT):
                    for cb in range(4):
                        k = kd % 2
                        kd += 1
                        for c in range(4):
                            MM(K, pd[k][:], ff[:, c, tt * 128:(tt + 1) * 128], wdn[:, c, cb * 512:(cb + 1) * 512], c == 0, c == 3,
                               r=[fkey, 'wd'], w=[('pd', k)])
                        TT(K, 'dve', xm[tt][:, cb * 512:(cb + 1) * 512], pd[k][:], xm[tt][:, cb * 512:(cb + 1) * 512], ALU.add,
                           r=[('pd', k), ('xk', tt)], w=[('xk', tt)])
            for tt in range(NT):
                rows = slice(T0 + tt * 128, T0 + (tt + 1) * 128)
                if final_w is None:
                    DMA(K, 'sp', x_dst[rows, :], xm[tt][:], r=[('xk', tt)])
                else:
                    c = ((blk % 8) * NT + tt) * 4
                    sk = ('stf', tt)
                    STT(K, junk[:], xm[tt][:], 1.0, xm[tt][:], ALU.mult, ALU.mult, r=[('xk', tt)], w=['junk', sk],
                        accum_out=st[:, c + 3:c + 4])
                    ACTV(K, st[:, c + 1:c + 2], st[:, c + 3:c + 4], AF.Sqrt, r=[sk], w=[sk], scale=1.0 / D, bias=K.eps_t[:, 0:1])
                    K.S.op('dve', lambda e, c=c: e.reciprocal(out=st[:, c + 2:c + 3], in_=st[:, c + 1:c + 2]), r=[sk], w=[sk])
                    STT(K, xm[tt][:], xm[tt][:], st[:, c + 2:c + 3], wbf[:], ALU.mult, ALU.mult, r=[('xk', tt), sk, 'wbf'],
                        w=[('xk', tt)])
                    DMA(K, 'sp', out_dst[rows, :], xm[tt][:], r=[('xk', tt)])
        S.flush()


def phase_att(K, l):
    nc, S = K.nc, K.S
    with ExitStack() as es:
        sb = lambda n, s, d: es.enter_context(nc.sbuf_tensor(_uniq(n), s, d))
        ps = lambda n, s, d: es.enter_context(nc.psum_tensor(_uniq(n), s, d))
        b0 = sb("b0", [128, 9, 256], BF16)
        sid = sb("sid", [128, 24, 128], BF16)
        zt = sb("zt", [128, 512], BF16)
        qs = [sb("qs%d" % i, [128, 4, 128], BF16) for i in range(2)]
        ks = [sb("ks%d" % i, [128, 4, 256], BF16) for i in range(2)]
        vs = [sb("vs%d" % i, [128, 2, 512], BF16) for i in range(2)]
        Pm = [sb("Pm%d" % i, [128, 4, 256], BF16) for i in range(2)]
        PT = [sb("PT%d" % i, [128, 8, 128], BF16) for i in range(2)]
        mx = [sb("mx%d" % i, [128, 8], F32) for i in range(2)]
        nmx = [sb("nmx%d" % i, [128, 8], F32) for i in range(2)]
        ogt = [sb("ogt%d" % i, [128, 512], F32) for i in range(2)]
        mlt = [sb("mlt%d" % i, [128, 16], F32) for i in range(2)]
        psc = [ps("psc%d" % i, [128, 4, 256], F32) for i in range(2)]
        pT = [ps("pT%d" % i, [128, 8, 128], BF16) for i in range(2)]
        po = [ps("po%d" % i, [128, 512], F32) for i in range(2)]
        DMA(K, 'sp', b0[:], K.c_b0.rearrange("v p k -> p v k"), w=['b0'])
        DMA(K, 'sp', sid[:], K.c_sid.rearrange("h p k -> p h k"), w=['sid'])
        S.op('dve', lambda e: e.memset(zt[:], 0.0), w=['zt'])
        NP = S_LEN + 128
        for a_ in range(12):
            DMA(K, 'sp', K.kT[a_ * 128:(a_ + 1) * 128, 0:64], zt[:, 0:64], r=['zt'], w=['kT'])
            DMA(K, 'sp', K.kT[a_ * 128:(a_ + 1) * 128, NP - 64:NP], zt[:, 0:64], r=['zt'], w=['kT'])
        for g in range(3):
            DMA(K, 'sp', K.vv[g, 0:64, :], zt[0:64, :], r=['zt'], w=['vv'])
            DMA(K, 'sp', K.vv[g, NP - 64:NP, :], zt[0:64, :], r=['zt'], w=['vv'])
        units = [(g, ti) for g in range(3) for ti in range(NQT)]
        NU = len(units)

        def unit_info(n):
            g, ti = units[n]
            d = DILS[g]
            L = S_LEN // d
            r_ = (ti * 128) // L
            u0 = (ti * 128) % L
            var = 1 if u0 == 0 else (2 if u0 + 128 == L else 0)
            return g, d, L, r_, u0, var

        def stage_Q(k):
            n, hf = divmod(k, 2)
            i = n % 2
            g, d, L, r_, u0, var = unit_info(n)
            if hf == 0:
                P0 = 64 + r_ * L + u0
                rows = slice(g * 512, (g + 1) * 512)
                DMA(K, 'sp', qs[i][:], K.qT[rows, P0:P0 + 128].rearrange("(a p) t -> p a t", p=128), w=[('qs', i)])
                DMA(K, 'sp', ks[i][:], K.kT[rows, P0 - 64:P0 + 192].rearrange("(a p) t -> p a t", p=128), r=['kT'], w=[('ks', i)])
                DMA(K, 'sp', vs[i][:], K.vv[g, P0 - 64:P0 + 192, :].rearrange("(kt p) c -> p kt c", p=128), r=['vv'], w=[('vs', i)])
            for sl in range(4):
                s_ = 4 * hf + sl
                pr = s_ // 2
                prt = slice(64 * (s_ % 2), 64 * (s_ % 2) + 64)
                MM(K, psc[hf][:, sl, :], qs[i][prt, pr, :], ks[i][prt, pr, :], True, False,
                   r=[('qs', i), ('ks', i)], w=[('psc', hf)])
                MM(K, psc[hf][:, sl, :], sid[:, g * 8 + s_, :], b0[:, g * 3 + var, :], False, True,
                   r=['sid', 'b0'], w=[('psc', hf)])
            nms = mlt[i][:, 4 * hf:4 * hf + 4]
            S.op('dve', lambda e, nms=nms, hf=hf: e.tensor_reduce(out=nms, in_=psc[hf][:], axis=AX.X, op=ALU.max, negate=True),
                 r=[('psc', hf)], w=[('mltm', i, hf)])
            for sl in range(4):
                s_ = 4 * hf + sl
                ACTV(K, Pm[hf][:, sl, :], psc[hf][:, sl, :], AF.Exp, r=[('psc', hf), ('mltm', i, hf)],
                     w=[('Pm', hf, sl), ('mltl', i, s_)], bias=mlt[i][:, s_:s_ + 1],
                     accum_out=mlt[i][:, 8 + s_:9 + s_])

        def stage_T(k):
            n, hf = divmod(k, 2)
            for sl in range(4):
                for kt in range(2):
                    TR(K, pT[hf][:, sl * 2 + kt, :], Pm[hf][:, sl, kt * 128:(kt + 1) * 128],
                       r=[('Pm', hf, sl), 'ident'], w=[('pT', hf)])
            CP(K, 'dve' if hf == 0 else 'act', PT[hf][:], pT[hf][:], r=[('pT', hf)], w=[('PT', hf)])

        def stage_V(k):
            n, hf = divmod(k, 2)
            i = n % 2
            g, d, L, r_, u0, var = unit_info(n)
            for sl in range(4):
                s_ = 4 * hf + sl
                for kt in range(2):
                    MM(K, po[i][:, s_ * 64:(s_ + 1) * 64], PT[hf][:, sl * 2 + kt, :], vs[i][:, kt, s_ * 64:(s_ + 1) * 64],
                       kt == 0, kt == 1, r=[('PT', hf), ('vs', i)], w=[('po', i)])
            if hf == 1:
                CP(K, 'act', ogt[i][:], po[i][:], r=[('po', i)], w=[('ogt', i)])
                t0 = r_ + d * u0
                tsl = slice(t0, t0 + 127 * d + 1, d) if d > 1 else slice(t0, t0 + 128)
                DMA(K, 'sp', K.og[g, tsl, :], ogt[i][:], r=[('ogt', i)])
                DMA(K, 'sp', K.mlg[g, tsl, :], mlt[i][:],
                    r=[('mltm', i, 0), ('mltm', i, 1)] + [('mltl', i, q_) for q_ in range(8)])

        NK = 2 * NU
        for k in range(NK + 2):
            if k < NK:
                stage_Q(k)
            if 0 <= k - 1 < NK:
                stage_T(k - 1)
            if 0 <= k - 2 < NK:
                stage_V(k - 2)
        S.flush()
    with ExitStack() as es:
        sb = lambda n, s, d: es.enter_context(nc.sbuf_tensor(_uniq(n), s, d))
        ps = lambda n, s, d: es.enter_context(nc.psum_tensor(_uniq(n), s, d))
        o3 = [sb("o3_%d" % i, [128, 3, 512], F32) for i in range(2)]
        ml3 = [sb("ml3_%d" % i, [128, 3, 16], F32) for i in range(2)]
        sm = [sb("sm%d" % i, [128, 64], F32) for i in range(2)]
        w3 = [sb("w3_%d" % i, [128, 3, 8], F32) for i in range(2)]
        acc = [sb("acc%d" % i, [128, 512], F32) for i in range(2)]
        t2 = [sb("t2_%d" % i, [128, 512], F32) for i in range(2)]
        ab = [sb("ab%d" % i, [128, 512], BF16) for i in range(2)]
        aT = [sb("aT%d" % i, [128, 4, 128], BF16) for i in range(2)]
        pa = [ps("pa%d" % i, [128, 4, 128], BF16) for i in range(2)]
        def merge_tile(ti):
                i = ti % 2
                rows = slice(ti * 128, (ti + 1) * 128)
                DMA(K, 'sp', o3[i][:], K.og[:, rows, :].rearrange("g p c -> p g c"), w=[('o3', i)])
                yield
                DMA(K, 'sp', ml3[i][:], K.mlg[:, rows, :].rearrange("g p c -> p g c"), w=[('ml3', i)])
                yield
                M = sm[i][:, 0:8]
                den = sm[i][:, 8:16]
                rden = sm[i][:, 16:24]
                k3 = [('ml3', i)]
                ks_ = [('sm', i)]
                TT(K, 'dve', M, ml3[i][:, 0, 0:8], ml3[i][:, 1, 0:8], ALU.min, r=k3, w=ks_)
                yield
                TT(K, 'dve', M, M, ml3[i][:, 2, 0:8], ALU.min, r=k3 + ks_, w=ks_)
                yield
                for g in range(3):
                    TT(K, 'dve', w3[i][:, g, :], ml3[i][:, g, 0:8], M, ALU.subtract, r=k3 + ks_, w=[('w3', i)])
                    yield
                ACTV(K, w3[i][:], w3[i][:], AF.Exp, r=[('w3', i)], w=[('w3', i)], scale=-1.0)
                yield
                for g in range(3):
                    TT(K, 'dve', sm[i][:, 24 + 8 * g:32 + 8 * g], w3[i][:, g, :], ml3[i][:, g, 8:16], ALU.mult,
                       r=k3 + [('w3', i)], w=ks_)
                    yield
                TT(K, 'dve', den, sm[i][:, 24:32], sm[i][:, 32:40], ALU.add, r=ks_, w=ks_)
                yield
                TT(K, 'dve', den, den, sm[i][:, 40:48], ALU.add, r=ks_, w=ks_)
                yield
                S.op('dve', lambda e, rden=rden, den=den: e.reciprocal(out=rden, in_=den), r=ks_, w=ks_)
                yield
                for g in range(3):
                    TT(K, 'dve', w3[i][:, g, :], w3[i][:, g, :], rden, ALU.mult, r=ks_ + [('w3', i)], w=[('w3', i)])
                    yield
                for g in range(3):
                    wb_ = w3[i][:, g, :].unsqueeze(2).to_broadcast([128, 8, 64])
                    src = o3[i][:, g, :].rearrange("p (s e) -> p s e", e=64)
                    dst = (acc[i] if g == 0 else t2[i])[:].rearrange("p (s e) -> p s e", e=64)
                    eng = 'dve' if g != 1 else 'pool'
                    TT(K, eng, dst, src, wb_, ALU.mult, r=[('o3', i), ('w3', i)], w=[('acc', i) if g == 0 else ('t2', i)])
                    yield
                    if g > 0:
                        outap = acc[i][:] if g == 1 else ab[i][:]
                        TT(K, 'dve', outap, acc[i][:], t2[i][:], ALU.add, r=[('acc', i), ('t2', i)],
                           w=[('acc', i)] if g == 1 else [('ab', i)])
                        yield
                for a_ in range(4):
                    TR(K, pa[i][:, a_, :], ab[i][:, a_ * 128:(a_ + 1) * 128], r=[('ab', i), 'ident'], w=[('pa', i)])
                    yield
                CP(K, 'act', aT[i][:], pa[i][:], r=[('pa', i)], w=[('aT', i)])
                yield
                DMA(K, 'sp', K.mixT[0:512, rows].rearrange("(a p) t -> p a t", p=128), aT[i][:], r=[('aT', i)])
                yield

        for t0_ in range(0, NQT, 2):
            gens = [merge_tile(t0_), merge_tile(t0_ + 1)]
            alive = [True, True]
            while any(alive):
                for gi_ in range(2):
                    if alive[gi_]:
                        try:
                            next(gens[gi_])
                        except StopIteration:
                            alive[gi_] = False
        S.flush()


def phase_pool(K, l):
    nc, S = K.nc, K.S
    with ExitStack() as es:
        sb = lambda n, s, d: es.enter_context(nc.sbuf_tensor(_uniq(n), s, d))
        ps = lambda n, s, d: es.enter_context(nc.psum_tensor(_uniq(n), s, d))
        band = sb("band", [128, 20, 128], BF16)
        pw = sb("pw", [128, 4, 128], BF16)
        psc_ = sb("pscale", [128, 4], F32)
        ut = [sb("ut%d" % i, [128, 3, 512], BF16) for i in range(2)]
        rt = [sb("rt%d" % i, [128, 4, 128], BF16) for i in range(2)]
        ot = [sb("ot%d" % i, [128, 4, 128], BF16) for i in range(2)]
        pr = [ps("pr%d" % i, [128, 4, 128], F32) for i in range(2)]
        pq = [ps("pq%d" % i, [128, 4, 128], F32) for i in range(2)]
        DMA(K, 'sp', band[:], K.c_band.rearrange("g v p k -> p (g v) k"), w=['band'])
        DMA(K, 'pool', pw[:], K.pool_w[l].rearrange("g p k -> p g k"), w=['pw'])
        DMA(K, 'sp', psc_[:], K.pool_scale[l].rearrange("(g p) -> p g", p=128), w=['pscale'], slow=True)
        for ti in range(NQT):
            i = ti % 2
            lo = 0 if ti > 0 else 1
            hi = 3 if ti < NQT - 1 else 2
            R0 = 128 + (ti - 1) * 128
            DMA(K, 'sp', ut[i][:, lo:hi, :], K.uu[R0 + lo * 128:R0 + hi * 128, :].rearrange("(a p) c -> p a c", p=128),
                w=[('ut', i)])
            for g in range(4):
                own = 1 if ti == 0 else (2 if ti == NQT - 1 else 0)
                terms = [(1, own)]
                if ti > 0:
                    terms.append((0, 3))
                if ti < NQT - 1:
                    terms.append((2, 4))
                for n_, (a_, v) in enumerate(terms):
                    MM(K, pr[i][:, g, :], ut[i][:, a_, g * 128:(g + 1) * 128], band[:, g * 5 + v, :], n_ == 0, n_ == len(terms) - 1,
                       r=[('ut', i), 'band'], w=[('pr', i)])
            CP(K, 'act', rt[i][:], pr[i][:], r=[('pr', i)], w=[('rt', i)])
            for g in range(4):
                MM(K, pq[i][:, g, :], pw[:, g, :], rt[i][:, g, :], True, True, r=['pw', ('rt', i)], w=[('pq', i)])
            for g in range(4):
                TS(K, 'dve', ot[i][:, g, :], pq[i][:, g, :], psc_[:, g:g + 1], ALU.mult, r=[('pq', i), 'pscale'], w=[('ot', i)])
            DMA(K, 'sp', K.mixT[1536:2048, ti * 128:(ti + 1) * 128].rearrange("(a p) t -> p a t", p=128), ot[i][:],
                r=[('ot', i)])
        S.flush()


def phase_ssd(K, l):
    nc, S = K.nc, K.S
    NC_ = NQT
    with ExitStack() as es:
        sb = lambda n, s, d: es.enter_context(nc.sbuf_tensor(_uniq(n), s, d))
        bk = [es.enter_context(nc.psum_tensor(_uniq("bk%d" % i), [128, 512], F32)) for i in range(8)]
        B = lambda i: ('bk', i)
        tri = sb("tri", [128, 4, 128], F32)
        onesf = sb("onesf", [128, 128], F32)
        identf = sb("identf", [128, 128], F32)
        one_t = sb("one_t", [128, 1], F32)
        zt = sb("zt2", [128, 16], BF16)
        dt_all = sb("dt_all", [128, NC_, 32], F32)
        dta = sb("dta", [128, NC_, 32], F32)
        tmpa = sb("tmpa", [128, NC_, 32], F32)
        tmpb = sb("tmpb", [128, NC_, 32], F32)
        dtb = sb("dtb", [128, 32], F32)
        abc = sb("abc", [128, 32], F32)
        E = sb("E", [128, NC_, 64], F32)
        cd = sb("cd", [128, NC_, 32], F32)
        wx = sb("wx", [128, NC_, 4, 16], F32)
        cw = sb("cw", [128, 5, 12], F32)
        cb = sb("cb", [128, 12], F32)
        dg = sb("dg", [128, 12, 5, 128], BF16)
        dsk = sb("dsk", [128, 16], F32)
        nw = sb("nw", [128, 1024], F32)
        CTall = sb("CTall", [128, NC_, 2, 128], BF16)
        xin = sb("xin", [128, 12, 132], BF16)
        xc = sb("xc", [128, 12, 128], BF16)
        xsB = sb("xsB", [128, 1280], BF16)
        cbm = sb("cbm", [128, 2, 2, 128], F32)
        X = sb("X", [128, 2, 16, 128], F32)
        seg = sb("seg", [128, 2, 16, 128], BF16)
        MT = sb("MT", [128, 2, 16, 128], BF16)
        xdt = sb("xdt", [128, 2, 1024], BF16)
        xdd = sb("xdd", [128, 2, 1024], BF16)
        tA = sb("tA", [128, 1024], F32)
        tB = sb("tB", [128, 1024], F32)
        Sf = sb("Sf", [128, 1024], F32)
        Sfb = sb("Sfb", [128, 1024], BF16)
        stt = sb("stt", [128, 1024], F32)
        yb = sb("yb", [128, 1024], BF16)
        yT = sb("yT", [128, 8, 128], BF16)
        st = sb("st3", [128, 8], F32)

        DMA(K, 'sp', tri[:], K.c_tri.rearrange("v p k -> p v k"), w=['tri'])
        DMA(K, 'sp', identf[:], K.c_identf[:, :], w=['identf'])
        S.op('dve', lambda e: e.memset(onesf[:], 1.0), w=['onesf'])
        S.op('dve', lambda e: e.memset(one_t[:], 1.0), w=['one_t'])
        S.op('dve', lambda e: e.memset(zt[:], 0.0), w=['zt'])
        for a_ in range(12):
            DMA(K, 'sp', K.xbcT[a_ * 128:(a_ + 1) * 128, 0:2], zt[:, 0:2], r=['zt'], w=['xbcT'])
            DMA(K, 'sp', K.xbcT[a_ * 128:(a_ + 1) * 128, S_LEN + 2:S_LEN + 4], zt[:, 0:2], r=['zt'], w=['xbcT'])
        DMA(K, 'sp', dt_all[:], K.dtr.rearrange("(c p) k -> p c k", p=128), w=['dt_all'])
        DMA(K, 'sp', dtb[:], K.dt_bias[l:l + 1, :].partition_broadcast(128), w=['dtb'])
        DMA(K, 'sp', abc[:], K.a_log[l:l + 1, :].partition_broadcast(128), w=['abc'])
        DMA(K, 'sp', dsk[:], K.d_skip[l:l + 1, :].partition_broadcast(128), w=['dsk'])
        DMA(K, 'sp', nw[:], K.ssd_norm_w[l:l + 1, :].partition_broadcast(128), w=['nw'])
        for k in range(5):
            DMA(K, 'sp', cw[:, k, :], K.conv_w[l, k].rearrange("(ct p) -> p ct", p=128), w=['cw'], slow=True)
        DMA(K, 'sp', cb[:], K.conv_b[l].rearrange("(ct p) -> p ct", p=128), w=['cb'], slow=True)
        for ct in range(12):
            for k in range(5):
                TS(K, 'pool' if (ct + k) % 2 else 'dve', dg[:, ct, k, :], identf[:], cw[:, k, ct:ct + 1], ALU.mult,
                   r=['identf', 'cw'], w=['dg'])
        ACTV(K, abc[:], abc[:], AF.Exp, r=['abc'], w=['abc'])
        TS(K, 'dve', abc[:], abc[:], -1.0, ALU.mult, r=['abc'], w=['abc'])
        bc32 = lambda t: t[:].unsqueeze(1).to_broadcast([128, NC_, 32])
        TT(K, 'dve', dt_all[:], dt_all[:], bc32(dtb), ALU.add, r=['dt_all', 'dtb'], w=['dt_all'])
        TS(K, 'dve', tmpb[:], dt_all[:], -1.0, ALU.mult, r=['dt_all'], w=['tmpb'])
        TT(K, 'dve', tmpa[:], dt_all[:], tmpb[:], ALU.max, r=['dt_all', 'tmpb'], w=['tmpa'])
        ACTV(K, tmpa[:], tmpa[:], AF.Exp, r=['tmpa'], w=['tmpa'], scale=-1.0)
        ACTV(K, tmpa[:], tmpa[:], AF.Ln, r=['tmpa', 'one_t'], w=['tmpa'], bias=one_t[:, 0:1])
        TS(K, 'dve', tmpb[:], dt_all[:], 0.0, ALU.max, r=['dt_all'], w=['tmpb'])
        TT(K, 'dve', dt_all[:], tmpa[:], tmpb[:], ALU.add, r=['tmpa', 'tmpb'], w=['dt_all'])
        TT(K, 'dve', dta[:], dt_all[:], bc32(abc), ALU.mult, r=['dt_all', 'abc'], w=['dta'])
        for c in range(NC_):
            bi = c // 8
            for v in range(4):
                cols = slice((c % 8) * 64 + v * 16, (c % 8) * 64 + v * 16 + 16)
                dsl = slice(0, 16) if v < 2 else slice(16, 32)
                MM(K, bk[bi][:, cols], tri[:, v, :], dta[:, c, dsl], True, True, r=['tri', 'dta'], w=[B(bi)])
        for bi in range(4):
            ACTV(K, E[:, bi * 8:(bi + 1) * 8, :], bk[bi][:].rearrange("p (c k) -> p c k", k=64), AF.Exp, r=[B(bi)], w=['E'])
        for hf in range(2):
            MM(K, bk[4 + hf][:], onesf[:], dta[:, hf * 16:(hf + 1) * 16, :], True, True, r=['onesf', 'dta'], w=[B(4 + hf)])
            ACTV(K, cd[:, hf * 16:(hf + 1) * 16, :], bk[4 + hf][:].rearrange("p (c k) -> p c k", k=32), AF.Exp,
                 r=[B(4 + hf)], w=['cd'])
        for dr in range(2):
            CP(K, 'dve', wx[:, :, dr, :], dt_all[:, :, dr * 16:(dr + 1) * 16], r=['dt_all'], w=['wx'])
            TT(K, 'dve', wx[:, :, 2 + dr, :], dt_all[:, :, dr * 16:(dr + 1) * 16], E[:, :, 16 + 32 * dr:32 + 32 * dr], ALU.mult,
               r=['dt_all', 'E'], w=['wx'])
        S.op('dve', lambda e: e.memset(Sf[:], 0.0), w=['Sf'])
        S.op('dve', lambda e: e.memset(Sfb[:], 0.0), w=['Sfb'])
        bc = lambda ap, shape, ax: ap.unsqueeze(ax).to_broadcast(shape)
        nd = 0
        for c in range(NC_):
            T0 = c * 128
            DMA(K, 'sp', xin[:], K.xbcT[:, T0:T0 + 132].rearrange("(ct p) t -> p ct t", p=128), r=['xbcT'], w=['xin'])
            for dr in range(2):
                TT(K, 'pool', X[:, dr, :, :], bc(tri[:, 0 if dr == 0 else 2, :], [128, 16, 128], 1),
                   bc(dta[:, c, dr * 16:(dr + 1) * 16], [128, 16, 128], 2), ALU.mult, r=['tri', 'dta'], w=['X'])
            for ct in range(12):
                bi = ct // 4
                for k in range(5):
                    MM(K, bk[bi][:, (ct % 4) * 128:(ct % 4 + 1) * 128], dg[:, ct, k, :], xin[:, ct, k:k + 128], k == 0, k == 4,
                       r=['dg', 'xin'], w=[B(bi)])
            for ct in range(12):
                bi = ct // 4
                ACTV(K, xc[:, ct, :], bk[bi][:, (ct % 4) * 128:(ct % 4 + 1) * 128], AF.Silu, r=[B(bi), 'cb'], w=['xc'],
                     bias=cb[:, ct:ct + 1])
            CP(K, 'pool', CTall[:, c, :, :], xc[:, 10:12, :], r=['xc'], w=['CTall'])
            for dr in range(2):
                for q4 in range(4):
                    bi = 6 + (nd % 2)
                    nd += 1
                    MM(K, bk[bi][:], tri[:, 1 if dr == 0 else 3, :], X[:, dr, q4 * 4:(q4 + 1) * 4, :], True, True,
                       r=['tri', 'X'], w=[B(bi)])
                    ACTV(K, seg[:, dr, q4 * 4:(q4 + 1) * 4, :], bk[bi][:].rearrange("p (h l) -> p h l", h=4), AF.Exp,
                         r=[B(bi)], w=['seg'])
            bv3 = bk[3][:].bitcast(BF16)
            bv4 = bk[4][:].bitcast(BF16)
            for ct in range(10):
                dst = bv3[:, ct * 128:(ct + 1) * 128] if ct < 8 else bv4[:, (ct - 8) * 128:(ct - 7) * 128]
                TR(K, dst, xc[:, ct, :], r=['xc', 'ident'], w=[B(3) if ct < 8 else B(4)])
            CP(K, 'dve', xsB[:, 0:1024], bv3[:, 0:1024], r=[B(3)], w=['xsB'])
            CP(K, 'dve', xsB[:, 1024:1280], bv4[:, 0:256], r=[B(4)], w=['xsB'])
            for g in range(2):
                MM(K, bk[5][:, g * 128:(g + 1) * 128], xc[:, 8 + g, :], xc[:, 10 + g, :], True, True, r=['xc'], w=[B(5)])
            for dr in range(2):
                TT(K, 'dve', cbm[:, dr, :, :], bk[5][:, 0:256].rearrange("p (g l) -> p g l", g=2),
                   bc(tri[:, 0 if dr == 0 else 2, :], [128, 2, 128], 1), ALU.mult, r=[B(5), 'tri'], w=['cbm'])
            for dr in range(2):
                for g in range(2):
                    TT(K, 'dve' if g == 0 else 'pool', MT[:, dr, g * 8:(g + 1) * 8, :], seg[:, dr, g * 8:(g + 1) * 8, :],
                       bc(cbm[:, dr, g, :], [128, 8, 128], 1), ALU.mult, r=['seg', 'cbm'], w=['MT'])
            xs3 = xsB[:, 0:1024].rearrange("p (h e) -> p h e", e=64)
            for dr in range(2):
                TT(K, 'pool', xdt[:, dr, :].rearrange("p (h e) -> p h e", e=64), xs3, bc(wx[:, c, dr, :], [128, 16, 64], 2),
                   ALU.mult, r=['xsB', 'wx'], w=['xdt'])
                TT(K, 'dve', xdd[:, dr, :].rearrange("p (h e) -> p h e", e=64), xs3, bc(wx[:, c, 2 + dr, :], [128, 16, 64], 2),
                   ALU.mult, r=['xsB', 'wx'], w=['xdd'])
            for hh in range(16):
                bi = hh // 8
                cols = slice((hh % 8) * 64, (hh % 8) * 64 + 64)
                MM(K, bk[bi][:, cols], MT[:, 0, hh, :], xdt[:, 0, hh * 64:(hh + 1) * 64], True, False, r=['MT', 'xdt'], w=[B(bi)])
                MM(K, bk[bi][:, cols], MT[:, 1, hh, :], xdt[:, 1, hh * 64:(hh + 1) * 64], False, True, r=['MT', 'xdt'], w=[B(bi)])
            for g in range(2):
                MM(K, bk[2 + g][:], xc[:, 10 + g, :], Sfb[:, g * 512:(g + 1) * 512], True, True, r=['xc', 'Sfb'], w=[B(2 + g)])
            for dr in range(2):
                for g in range(2):
                    MM(K, bk[4 + 2 * dr + g][:], xsB[:, 1024 + g * 128:1024 + (g + 1) * 128], xdd[:, dr, g * 512:(g + 1) * 512],
                       True, True, r=['xsB', 'xdd'], w=[B(4 + 2 * dr + g)])
            for g in range(2):
                cs = slice(g * 512, (g + 1) * 512)
                v3 = lambda ap: ap.rearrange("p (h e) -> p h e", e=64)
                TT(K, 'dve', v3(tA[:, cs]), v3(bk[2 + g][:]), bc(E[:, c, g * 8:(g + 1) * 8], [128, 8, 64], 2), ALU.mult,
                   r=[B(2 + g), 'E'], w=['tA'])
                TT(K, 'dve', tA[:, cs], tA[:, cs], bk[g][:], ALU.add, r=['tA', B(g)], w=['tA'])
            TT(K, 'pool', tB[:].rearrange("p (h e) -> p h e", e=64), xs3, bc(dsk[:, :], [128, 16, 64], 2), ALU.mult,
               r=['xsB', 'dsk'], w=['tB'])
            TT(K, 'pool', tA[:], tA[:], tB[:], ALU.add, r=['tA', 'tB'], w=['tA'])
            DMA(K, 'sp', K.ypart[T0:T0 + 128, :], tA[:], r=['tA'], w=['ypart'])
            for g in range(2):
                CP(K, 'act', stt[:, g * 512:(g + 1) * 512], bk[6 + g][:], r=[B(6 + g)], w=['stt'])
            DMA(K, 'sp', K.stb[c], stt[:], r=['stt'], w=['stb'])
            TT(K, 'dve', Sf[:].rearrange("p (h e) -> p h e", e=64), Sf[:].rearrange("p (h e) -> p h e", e=64),
               bc(cd[:, c, 0:16], [128, 16, 64], 2), ALU.mult, r=['Sf', 'cd'], w=['Sf'])
            for g in range(2):
                cs = slice(g * 512, (g + 1) * 512)
                TT(K, 'dve', Sf[:, cs], Sf[:, cs], bk[4 + g][:], ALU.add, r=['Sf', B(4 + g)], w=['Sf'])
            CP(K, 'act', Sfb[:], Sf[:], r=['Sf'], w=['Sfb'])
        S.op('dve', lambda e: e.memset(Sf[:], 0.0), w=['Sf'])
        S.op('dve', lambda e: e.memset(Sfb[:], 0.0), w=['Sfb'])
        for c in range(NC_ - 1, -1, -1):
            T0 = c * 128
            DMA(K, 'sp', tA[:], K.ypart[T0:T0 + 128, :], r=['ypart'], w=['tA'])
            DMA(K, 'sp', tB[:], K.zz[T0:T0 + 128, :], w=['tB'])
            DMA(K, 'sp', stt[:], K.stb[c], r=['stb'], w=['stt'])
            for g in range(2):
                MM(K, bk[g][:], CTall[:, c, g, :], Sfb[:, g * 512:(g + 1) * 512], True, True, r=['CTall', 'Sfb'], w=[B(g)])
            for g in range(2):
                cs = slice(g * 512, (g + 1) * 512)
                v3 = lambda ap: ap.rearrange("p (h e) -> p h e", e=64)
                xq = X[:, 0, 0:4, :].rearrange("p a b -> p (a b)")
                TT(K, 'dve', v3(xq), v3(bk[g][:]), bc(E[:, c, 32 + g * 8:40 + g * 8], [128, 8, 64], 2), ALU.mult,
                   r=[B(g), 'E'], w=['X'])
                TT(K, 'dve', tA[:, cs], tA[:, cs], xq, ALU.add, r=['tA', 'X'], w=['tA'])
            ACTV(K, tB[:], tB[:], AF.Silu, r=['tB'], w=['tB'])
            TT(K, 'dve', tA[:], tA[:], tB[:], ALU.mult, r=['tA', 'tB'], w=['tA'])
            for g in range(2):
                cs = slice(g * 512, (g + 1) * 512)
                STT(K, tB[:, cs], tA[:, cs], 1.0, tA[:, cs], ALU.mult, ALU.mult, r=['tA'], w=['tB', 'st'], accum_out=st[:, g:g + 1])
            ACTV(K, st[:, 2:4], st[:, 0:2], AF.Sqrt, r=['st'], w=['st'], scale=1.0 / 512, bias=K.eps_t[:, 0:1])
            S.op('dve', lambda e: e.reciprocal(out=st[:, 4:6], in_=st[:, 2:4]), r=['st'], w=['st'])
            for g in range(2):
                cs = slice(g * 512, (g + 1) * 512)
                STT(K, yb[:, cs], tA[:, cs], st[:, 4 + g:5 + g], nw[:, cs], ALU.mult, ALU.mult, r=['tA', 'st', 'nw'], w=['yb'])
            bv3 = bk[3][:].bitcast(BF16)
            for a_ in range(8):
                TR(K, bv3[:, a_ * 128:(a_ + 1) * 128], yb[:, a_ * 128:(a_ + 1) * 128], r=['yb', 'ident'], w=[B(3)])
            CP(K, 'act', yT[:], bv3[:, 0:1024].rearrange("p (a t) -> p a t", a=8), r=[B(3)], w=['yT'])
            DMA(K, 'sp', K.mixT[512:1536, T0:T0 + 128].rearrange("(a p) t -> p a t", p=128), yT[:], r=['yT'])
            TT(K, 'dve', Sf[:].rearrange("p (h e) -> p h e", e=64), Sf[:].rearrange("p (h e) -> p h e", e=64),
               bc(cd[:, c, 16:32], [128, 16, 64], 2), ALU.mult, r=['Sf', 'cd'], w=['Sf'])
            TT(K, 'dve', Sf[:], Sf[:], stt[:], ALU.add, r=['Sf', 'stt'], w=['Sf'])
            CP(K, 'act', Sfb[:], Sf[:], r=['Sf'], w=['Sfb'])
        S.flush()


def phase_B(K, l):
    phase_att(K, l)
    phase_pool(K, l)
    phase_ssd(K, l)


def setup_common(K, es):
    nc = K.nc
    K.ident = es.enter_context(nc.sbuf_tensor("ident", [128, 128], BF16))
    K.eps_t = es.enter_context(nc.sbuf_tensor("eps_t", [128, 1], F32))
    K.S.op('sp', lambda e: e.dma_start(out=K.ident[:], in_=K.c_ident[:, :]), w=['ident'], dma=True)
    K.S.op('dve', lambda e: e.memset(K.eps_t[:], EPS), w=['eps'])
    K.S.flush()


def declare_io(K, nc, dbg):
    dbg = dbg or {}
    di = lambda n, s, d: nc.dram_tensor(n, s, d, kind="ExternalInput").ap()
    K.x = di("x", [S_LEN, D], F32)
    K.norm1_w = di("norm1_w", [DEPTH, D], F32)
    K.w_in = di("w_in", [DEPTH, D, IN_W], F32)
    K.conv_w = di("conv_w", [DEPTH, 5, 1536], F32)
    K.conv_b = di("conv_b", [DEPTH, 1536], F32)
    K.dt_bias = di("dt_bias", [DEPTH, 32], F32)
    K.a_log = di("a_log", [DEPTH, 32], F32)
    K.d_skip = di("d_skip", [DEPTH, 16], F32)
    K.ssd_norm_w = di("ssd_norm_w", [DEPTH, 1024], F32)
    K.pool_w = di("pool_w", [DEPTH, 4, 128, 128], F32)
    K.pool_scale = di("pool_scale", [DEPTH, 512], F32)
    K.w_out = di("w_out", [DEPTH, D, D], F32)
    K.norm2_w = di("norm2_w", [DEPTH, D], F32)
    K.w_gate = di("w_gate", [DEPTH, D, DFF], F32)
    K.w_up = di("w_up", [DEPTH, D, DFF], F32)
    K.w_down = di("w_down", [DEPTH, DFF, D], F32)
    K.final_norm_w = di("final_norm_w", [1, D], F32)
    K.c_ident = di("c_ident", [128, 128], BF16)
    K.c_identf = di("c_identf", [128, 128], F32)
    K.c_tri = di("c_tri", [4, 128, 128], F32)
    K.c_b0 = di("c_b0", [9, 128, 256], BF16)
    K.c_sid = di("c_sid", [24, 128, 128], BF16)
    K.c_band = di("c_band", [4, 5, 128, 128], BF16)
    K.out = nc.dram_tensor("out", [S_LEN, D], F32, kind="ExternalOutput").ap()
    dsc = lambda n, s, d: nc.dram_tensor(n, s, d, kind=dbg.get(n, "Internal")).ap()
    NP = S_LEN + 128
    K.qT = dsc("qT", [1536, NP], BF16)
    K.kT = dsc("kT", [1536, NP], BF16)
    K.vv = dsc("vv", [3, NP, 512], BF16)
    K.zz = dsc("zz", [S_LEN, 1024], F32)
    K.xbcT = dsc("xbcT", [1536, S_LEN + 4], BF16)
    K.dtr = dsc("dtr", [S_LEN, 32], F32)
    K.uu = dsc("uu", [S_LEN + 256, 512], BF16)
    K.mixT = dsc("mixT", [D, S_LEN], BF16)
    K.x1 = dsc("x1", [S_LEN, D], F32)
    K.og = dsc("og", [3, S_LEN, 512], F32)
    K.mlg = dsc("mlg", [3, S_LEN, 16], F32)
    K.ypart = dsc("ypart", [S_LEN, 1024], F32)
    K.stb = dsc("stb", [NQT, 128, 1024], F32)


def build(dbg=None, phases=None):
    nc = bass.Bass("TRN2", target_bir_lowering=False)
    K = Ctx()
    K.nc = nc
    declare_io(K, nc, dbg)
    with ExitStack() as es:
        K.S = Sched(nc, es)
        setup_common(K, es)
        if phases is not None:
            phases(K)
        else:
            for l in range(DEPTH):
                xs = K.x if l == 0 else K.x1
                phase_A(K, l, xs)
                phase_B(K, l)
                if l == DEPTH - 1:
                    phase_CD(K, l, xs, None, final_w=K.final_norm_w[0:1, :], out_dst=K.out)
                else:
                    phase_CD(K, l, xs, K.x1)
    return nc


def host_consts():
    bf = ml_dtypes.bfloat16
    c = {}
    c["c_ident"] = np.eye(128, dtype=np.float32).astype(bf)
    c["c_identf"] = np.eye(128, dtype=np.float32)
    j = np.arange(128)[:, None]
    l_ = np.arange(128)[None, :]
    c["c_tri"] = np.stack([(j <= l_), (j > l_), (j >= l_), (j < l_)]).astype(np.float32)
    i = np.arange(128)[:, None]
    jj = np.arange(256)[None, :]
    rel = jj - 64 - i
    b0 = np.zeros((9, 128, 256), np.float32)
    BIG = 1.0e6
    for g, d in enumerate(DILS):
        for v in range(3):
            ok = np.abs(rel) <= 64
            if v == 1:
                ok = ok & (jj >= 64)
            if v == 2:
                ok = ok & (jj < 192)
            b0[g * 3 + v] = np.where(ok, -np.abs(rel) * float(d), -BIG)
    c["c_b0"] = b0.astype(bf)
    kk = np.arange(1, 25, dtype=np.float32)
    slopes = (2.0 ** (-8.0 * kk / 24.0)).astype(np.float32)
    c["c_sid"] = (np.eye(128, dtype=np.float32)[None] * slopes[:, None, None]).astype(bf)
    band = np.zeros((4, 5, 128, 128), np.float32)
    tp = np.arange(128)[:, None]
    t = np.arange(128)[None, :]
    for g, w in enumerate((2, 4, 8, 16)):
        hw = w // 2
        inwin = (tp >= t - hw) & (tp < t + hw)
        eye = (tp == t).astype(np.float32)
        band[g, 0] = inwin / float(w) - eye
        cnt_first = (t + hw) - np.maximum(t - hw, 0)
        band[g, 1] = inwin / cnt_first.astype(np.float32) - eye
        cnt_last = np.minimum(t + hw, 128) - (t - hw)
        band[g, 2] = inwin / cnt_last.astype(np.float32) - eye
        band[g, 3] = ((tp - 128) >= t - hw) / float(w)
        band[g, 4] = ((tp + 128) < t + hw) / float(w)
    c["c_band"] = band.astype(bf)
    return c


def make_inputs(inputs, b):
    f = lambda a: np.ascontiguousarray(np.asarray(a, dtype=np.float32))
    im = {"x": f(inputs["x"][b])}
    for n in ("norm1_w", "w_in", "conv_w", "conv_b", "d_skip", "ssd_norm_w", "pool_w", "pool_scale", "w_out",
              "norm2_w", "w_gate", "w_up", "w_down"):
        im[n] = f(inputs[n])
    im["dt_bias"] = f(inputs["dt_bias"]).reshape(DEPTH, 32)
    im["a_log"] = f(inputs["a_log"]).reshape(DEPTH, 32)
    im["final_norm_w"] = f(inputs["final_norm_w"]).reshape(1, D)
    im.update(host_consts())
    return im


_NC_CACHE = {}


def kernel(**inputs):
    if "nc" not in _NC_CACHE:
        _NC_CACHE["nc"] = build()
    nc = _NC_CACHE["nc"]
    B = inputs["x"].shape[0]
    in_maps = [make_inputs(inputs, c % B) for c in range(8)]
    res = run_bass_kernel_spmd(nc, in_maps, core_ids=list(range(8)))
    out = np.stack([np.asarray(res.results[b]["out"], dtype=np.float32) for b in range(B)], axis=0)
    return out
```

```python
import numpy as np
from contextlib import ExitStack
import ml_dtypes
import concourse.bass as bass
import concourse.mybir as mybir
from concourse.bass_utils import run_bass_kernel_spmd

F32, BF16 = mybir.dt.float32, mybir.dt.bfloat16
AF = mybir.ActivationFunctionType
ALU = mybir.AluOpType
AX = mybir.AxisListType

D = 2048
S_LEN = 4096
DEPTH = 2
IN_W = 7712
DFF = 5632
EPS = 1e-6
NQT = S_LEN // 128
DILS = (1, 4, 16)


class Sched:
    BLK = {'pe': 'tensor', 'act': 'scalar', 'dve': 'vector', 'pool': 'gpsimd', 'sp': 'sync'}
    NSLOT = {'sp': 28, 'pool': 12, 'act': 6}

    def __init__(self, nc, es):
        self.nc = nc
        self.sem = {e: es.enter_context(nc.semaphore("s_" + e)) for e in ('pe', 'act', 'dve', 'pool')}
        self.dsem = {e: [es.enter_context(nc.semaphore("d_%s%d" % (e, i))) for i in range(n)]
                     for e, n in self.NSLOT.items()}
        self.cnt = {e: 0 for e in self.sem}
        self.slot_uses = {e: [0] * n for e, n in self.NSLOT.items()}
        self.next_slot = {e: 0 for e in self.NSLOT}
        self.waited = {e: {} for e in self.BLK}
        self.reset()

    def reset(self):
        self.ops = []
        self.lw = {}
        self.rd = {}

    def op(self, eng, fn, r=(), w=(), dma=False):
        deps = set()
        for b in r:
            x = self.lw.get(b)
            if x is not None:
                deps.add(x)
        for b in w:
            x = self.lw.get(b)
            if x is not None:
                deps.add(x)
            rb = self.rd.get(b)
            if rb:
                deps.update(rb.values())
        i = len(self.ops)
        self.ops.append([eng, fn, deps, dma, False, 0, 0, 0])
        for b in r:
            self.rd.setdefault(b, {})[('d', i) if dma else eng] = i
        for b in w:
            self.lw[b] = i
            self.rd[b] = {}
        return i

    def flush(self, name=None):
        ops = self.ops
        for o in ops:
            for d in o[2]:
                D_ = ops[d]
                if D_[3]:
                    continue
                if D_[0] == 'pe' and o[0] == 'pe' and not o[3]:
                    continue
                D_[4] = True
        for o in ops:
            e = o[0]
            if o[3]:
                s = self.next_slot[e]
                self.next_slot[e] = (s + 1) % self.NSLOT[e]
                o[7] = 16 * self.slot_uses[e][s]
                self.slot_uses[e][s] += 1
                o[5] = 16 * self.slot_uses[e][s]
                o[6] = s
            elif o[4]:
                self.cnt[e] += 1
                o[5] = self.cnt[e]
        with self.nc.Block() as block:
            for e, bname in self.BLK.items():
                eops = [o for o in ops if o[0] == e]
                if not eops:
                    continue

                def body(eng, eops=eops, e=e):
                    waited = self.waited[e]
                    for o in eops:
                        reqs = {}
                        for d in o[2]:
                            D_ = ops[d]
                            if D_[3]:
                                key = ('d', D_[0], D_[6])
                            else:
                                if D_[0] == 'pe' and e == 'pe' and not o[3]:
                                    continue
                                key = ('e', D_[0])
                            if reqs.get(key, 0) < D_[5]:
                                reqs[key] = D_[5]
                        if o[3] and o[7] > 0:
                            key = ('d', e, o[6])
                            if reqs.get(key, 0) < o[7]:
                                reqs[key] = o[7]
                        for key, val in reqs.items():
                            if waited.get(key, 0) < val:
                                sem = self.sem[key[1]] if key[0] == 'e' else self.dsem[key[1]][key[2]]
                                eng.wait_ge(sem, val)
                                waited[key] = val
                        ins = o[1](eng)
                        if o[3]:
                            ins.then_inc(self.dsem[e][o[6]], 16)
                        elif o[4]:
                            ins.then_inc(self.sem[e], 1)
                    if e in self.NSLOT:
                        for s in range(self.NSLOT[e]):
                            val = 16 * self.slot_uses[e][s]
                            key = ('d', e, s)
                            if waited.get(key, 0) < val:
                                eng.wait_ge(self.dsem[e][s], val)
                                waited[key] = val

                getattr(block, bname)(body)
        self.reset()


class Ctx:
    pass


_UID = [0]


def _uniq(n):
    _UID[0] += 1
    return "%s_%d" % (n, _UID[0])


def MM(K, out, lhsT, rhs, start, stop, r, w):
    K.S.op('pe', lambda e: e.matmul(out, lhsT=lhsT, rhs=rhs, start=start, stop=stop), r=r, w=w)


def TR(K, out, in_, r, w, ident=None):
    idn = K.ident[:] if ident is None else ident
    K.S.op('pe', lambda e: e.transpose(out=out, in_=in_, identity=idn), r=r, w=w)


def DMA(K, q, out, in_, r=(), w=(), slow=False):
    if slow:
        K.S.op(q, lambda e: e.dma_start(out=out, in_=in_, allow_slow_non_contiguous=True), r=r, w=w, dma=True)
    else:
        K.S.op(q, lambda e: e.dma_start(out=out, in_=in_), r=r, w=w, dma=True)


def ACTV(K, out, in_, func, r, w, bias=None, scale=None, accum_out=None):
    kw = {}
    if bias is not None:
        kw['bias'] = bias
    if scale is not None:
        kw['scale'] = scale
    if accum_out is not None:
        kw['accum_out'] = accum_out
    K.S.op('act', lambda e: e.activation(out=out, in_=in_, func=func, **kw), r=r, w=w)


def TT(K, eng, out, in0, in1, op, r, w):
    K.S.op(eng, lambda e: e.tensor_tensor(out=out, in0=in0, in1=in1, op=op), r=r, w=w)


def TS(K, eng, out, in0, s1, op0, r, w, s2=None, op1=None, accum_out=None):
    kw = {}
    if op1 is not None:
        kw['op1'] = op1
    if accum_out is not None:
        kw['accum_out'] = accum_out
    K.S.op(eng, lambda e: e.tensor_scalar(out=out, in0=in0, scalar1=s1, scalar2=s2, op0=op0, **kw), r=r, w=w)


def STT(K, out, in0, scalar, in1, op0, op1, r, w, accum_out=None):
    kw = {}
    if accum_out is not None:
        kw['accum_out'] = accum_out
    K.S.op('dve', lambda e: e.scalar_tensor_tensor(out=out, in0=in0, scalar=scalar, in1=in1, op0=op0, op1=op1, **kw),
           r=r, w=w)


def CP(K, eng, out, in_, r, w):
    if eng == 'act':
        K.S.op('act', lambda e: e.activation(out=out, in_=in_, func=AF.Copy), r=r, w=w)
    else:
        K.S.op(eng, lambda e: e.tensor_copy(out=out, in_=in_), r=r, w=w)


def _evac(K, idx, out, in_, r, w, scale=None):
    if idx % 2 == 0:
        if scale is None:
            K.S.op('act', lambda e: e.activation(out=out, in_=in_, func=AF.Copy), r=r, w=w)
        else:
            K.S.op('act', lambda e: e.activation(out=out, in_=in_, func=AF.Copy, scale=scale), r=r, w=w)
    else:
        if scale is None:
            K.S.op('dve', lambda e: e.tensor_copy(out=out, in_=in_), r=r, w=w)
        else:
            K.S.op('dve', lambda e: e.tensor_scalar(out=out, in0=in_, scalar1=scale, scalar2=None, op0=ALU.mult), r=r, w=w)


def rms_to_hT(K, es_tiles, x_src, tok0, ntile, wbc, hT, hT_key, xt, hb, st, junk, ptr, st_base=0, keep_x=None, ptr_keys=None):
    S = K.S
    pk = ptr_keys if ptr_keys is not None else [('ptr', 0), ('ptr', 1)]
    for tt in range(ntile):
        i = tt % len(hb)
        T0 = tok0 + tt * 128
        xti = xt[i] if keep_x is None else keep_x[tt]
        xkey = ('xt', i) if keep_x is None else ('xk', tt)
        if keep_x is None:
            S.op('sp', lambda e, xti=xti, T0=T0: e.dma_start(out=xti[:], in_=x_src[T0:T0 + 128, :]), w=[xkey], dma=True)
        c = (st_base + tt) * 4
        sk = ('st', st_base + tt)
        S.op('dve', lambda e, xti=xti, c=c: e.scalar_tensor_tensor(
            out=junk[:], in0=xti[:], scalar=1.0, in1=xti[:], op0=ALU.mult, op1=ALU.mult,
            accum_out=st[:, c:c + 1]), r=[xkey], w=['junk', sk])
        S.op('act', lambda e, c=c: e.activation(out=st[:, c + 1:c + 2], in_=st[:, c:c + 1], func=AF.Sqrt,
                                                scale=1.0 / D, bias=K.eps_t[:, 0:1]), r=[sk], w=[sk])
        S.op('dve', lambda e, c=c: e.reciprocal(out=st[:, c + 2:c + 3], in_=st[:, c + 1:c + 2]), r=[sk], w=[sk])
        S.op('dve', lambda e, xti=xti, c=c, i=i: e.scalar_tensor_tensor(
            out=hb[i][:], in0=xti[:], scalar=st[:, c + 2:c + 3], in1=wbc[:], op0=ALU.mult, op1=ALU.mult),
            r=[xkey, sk, 'wbc'], w=[('hb', i)])
        for hh in range(2):
            for cc in range(8):
                ch = hh * 8 + cc
                S.op('pe', lambda e, hh=hh, cc=cc, ch=ch, i=i: e.transpose(
                    out=ptr[hh][:, cc, :], in_=hb[i][:, ch * 128:(ch + 1) * 128], identity=K.ident[:]),
                    r=[('hb', i), 'ident'], w=[pk[hh]])
            _evac(K, hh, hT[:, hh * 8:(hh + 1) * 8, tt * 128:(tt + 1) * 128], ptr[hh][:],
                  r=[pk[hh]], w=[(hT_key, tt)])


def phase_A(K, l, x_src):
    nc, S = K.nc, K.S
    with ExitStack() as es:
        sb = lambda n, s, d: es.enter_context(nc.sbuf_tensor(_uniq(n), s, d))
        ps = lambda n, s, d: es.enter_context(nc.psum_tensor(_uniq(n), s, d))
        hT = sb("hT", [128, 16, 2048], BF16)
        wbc = sb("wbc", [128, D], F32)
        xt = [sb("xt%d" % i, [128, D], F32) for i in range(2)]
        junk = sb("junk", [128, D], BF16)
        hb = [sb("hb%d" % i, [128, D], BF16) for i in range(2)]
        st = sb("st", [128, 16 * 4], F32)
        wbuf = [sb("wb%d" % i, [128, 16, 544], BF16) for i in range(2)]
        obh = [sb("obh%d" % i, [128, 512], BF16) for i in range(4)]
        obf = [sb("obf%d" % i, [128, 512], F32) for i in range(4)]
        obig = [sb("obig%d" % i, [128, 2048], BF16) for i in range(3)]
        ptr = [ps("ptr%d" % i, [128, 8, 128], BF16) for i in range(2)]
        pmm = [ps("pmm%d" % i, [128, 512], F32) for i in range(4)]

        S.op('sp', lambda e: e.dma_start(out=wbc[:], in_=K.norm1_w[l:l + 1, :].partition_broadcast(128)),
             w=['wbc'], dma=True)
        blocks = []
        for g in range(3):
            blocks.append(('qk', K.qT, g, g * 512, 512))
        for g in range(3):
            blocks.append(('qk', K.kT, g, 1536 + g * 512, 512))
        for g in range(3):
            blocks.append(('v', None, g, 3072 + g * 512, 512))
        for j in range(2):
            blocks.append(('z', None, j, 4608 + j * 512, 512))
        for j in range(3):
            blocks.append(('xbc', None, j, 5632 + j * 512, 512))
        blocks.append(('dtu', None, 0, 7168, 544))
        hT_all = [('hT', t) for t in range(16)]
        nblk = 0
        kk = 0
        nbig = 0

        def load_w(bi, c0, wd):
            src = K.w_in[l, :, c0:c0 + wd].rearrange("(c p) n -> p c n", p=128)
            S.op('pool', lambda e: e.dma_start(out=wbuf[bi][:, :, 0:wd], in_=src), w=[('wb', bi)], dma=True)

        for half in range(2):
            H0 = half * 2048
            load_w(nblk % 2, blocks[0][3], blocks[0][4])
            rms_to_hT(K, es, x_src, H0, 16, wbc, hT, 'hT', xt, hb, st, junk, ptr)
            for bidx, (mode, dst, g, c0, wd) in enumerate(blocks):
                bi = nblk % 2
                nblk += 1
                if bidx + 1 < len(blocks):
                    load_w(nblk % 2, blocks[bidx + 1][3], blocks[bidx + 1][4])
                wb = wbuf[bi]
                wkey = ('wb', bi)
                rk = [wkey] + hT_all
                if mode == 'qk':
                    d = DILS[g]
                    npr = 2048 // d
                    for ct in range(4):
                        dview = dst[g * 512 + ct * 128:g * 512 + (ct + 1) * 128, 64:64 + S_LEN] \
                            .rearrange("p (r u) -> p r u", r=d)
                        ob_i = nbig % 3
                        nbig += 1
                        og_ = obig[ob_i][:].rearrange("p (r u) -> p r u", r=d)
                        for j in range(4):
                            k = kk % 4
                            kk += 1
                            for ch in range(16):
                                MM(K, pmm[k][:], wb[:, ch, ct * 128:(ct + 1) * 128], hT[:, ch, j * 512:(j + 1) * 512],
                                   ch == 0, ch == 15, r=[wkey] + hT_all[4 * j:4 * j + 4], w=[('pmm', k)])
                            _evac(K, kk, og_[:, :, j * (512 // d):(j + 1) * (512 // d)],
                                  pmm[k][:].rearrange("p (u r) -> p r u", r=d), r=[('pmm', k)], w=[('obig', ob_i)],
                                  scale=(0.125 if dst is K.qT else None))
                        DMA(K, 'sp', dview[:, :, half * npr:(half + 1) * npr], og_, r=[('obig', ob_i)])
                elif mode in ('v', 'z', 'dtu'):
                    d = DILS[g] if mode == 'v' else 1
                    npr = 2048 // d
                    L = S_LEN // d
                    for i in range(16):
                        k = kk % 4
                        kk += 1
                        r_ = (128 * i) // npr
                        u0 = (128 * i) % npr
                        t0 = r_ + d * u0
                        if d > 1:
                            lsel = lambda ch, t0=t0, d=d: hT[:, ch, t0:t0 + 127 * d + 1:d]
                            rki = [wkey] + hT_all
                        else:
                            lsel = lambda ch, i=i: hT[:, ch, i * 128:(i + 1) * 128]
                            rki = [wkey, hT_all[i]]
                        tok = H0 + i * 128
                        if mode == 'dtu':
                            for ch in range(16):
                                MM(K, pmm[k][:], lsel(ch), wb[:, ch, 32:544], ch == 0, ch == 15, r=rki, w=[('pmm', k)])
                            _evac(K, kk, obh[k][:], pmm[k][:], r=[('pmm', k)], w=[('obh', k)])
                            DMA(K, 'sp', K.uu[128 + tok:128 + tok + 128, :], obh[k][:], r=[('obh', k)])
                            k2 = kk % 4
                            kk += 1
                            for ch in range(16):
                                MM(K, pmm[k2][:, 0:32], lsel(ch), wb[:, ch, 0:32], ch == 0, ch == 15, r=rki, w=[('pmm', k2)])
                            _evac(K, kk, obf[k2][:, 0:32], pmm[k2][:, 0:32], r=[('pmm', k2)], w=[('obf', k2)])
                            DMA(K, 'sp', K.dtr[tok:tok + 128, :], obf[k2][:, 0:32], r=[('obf', k2)])
                            continue
                        for ch in range(16):
                            MM(K, pmm[k][:], lsel(ch), wb[:, ch, 0:512], ch == 0, ch == 15, r=rki, w=[('pmm', k)])
                        if mode == 'v':
                            P = 64 + r_ * L + half * npr + u0
                            _evac(K, kk, obh[k][:], pmm[k][:], r=[('pmm', k)], w=[('obh', k)])
                            DMA(K, 'sp', K.vv[g, P:P + 128, :], obh[k][:], r=[('obh', k)])
                        else:
                            _evac(K, kk, obf[k][:], pmm[k][:], r=[('pmm', k)], w=[('obf', k)])
                            DMA(K, 'sp', K.zz[tok:tok + 128, g * 512:(g + 1) * 512], obf[k][:], r=[('obf', k)])
                else:
                    for ct in range(4):
                        for j in range(4):
                            k = kk % 4
                            kk += 1
                            for ch in range(16):
                                MM(K, pmm[k][:], wb[:, ch, ct * 128:(ct + 1) * 128], hT[:, ch, j * 512:(j + 1) * 512],
                                   ch == 0, ch == 15, r=[wkey] + hT_all[4 * j:4 * j + 4], w=[('pmm', k)])
                            _evac(K, kk, obh[k][:], pmm[k][:], r=[('pmm', k)], w=[('obh', k)])
                            row = g * 512 + ct * 128
                            col = 2 + H0 + j * 512
                            DMA(K, 'sp', K.xbcT[row:row + 128, col:col + 512], obh[k][:], r=[('obh', k)])
        S.flush()


def phase_CD(K, l, x_src, x_dst, final_w=None, out_dst=None):
    nc, S = K.nc, K.S
    TB = 512
    NT = TB // 128
    with ExitStack() as es:
        sb = lambda n, s, d: es.enter_context(nc.sbuf_tensor(_uniq(n), s, d))
        ps = lambda n, s, d: es.enter_context(nc.psum_tensor(_uniq(n), s, d))
        h2T = sb("h2T", [128, 16, TB], BF16)
        xm = [sb("xm%d" % i, [128, D], F32) for i in range(NT)]
        wbc = sb("wbc2", [128, D], F32)
        wbf = sb("wbcf", [128, D], F32) if final_w is not None else None
        hb = [sb("hb2_%d" % i, [128, D], BF16) for i in range(2)]
        junk = sb("junk2", [128, D], BF16)
        st = sb("st2", [128, 8 * NT * 4], F32)
        wA = [sb("wA%d" % i, [128, 16, 512], BF16) for i in range(2)]
        wB = [sb("wB%d" % i, [128, 16, 512], BF16) for i in range(2)]
        ffT = [sb("ffT%d" % i, [128, 4, TB], BF16) for i in range(2)]
        wd = [sb("wd%d" % i, [128, 4, D], BF16) for i in range(1)]
        tmp = [sb("tmp%d" % i, [128, 512], F32) for i in range(2)]
        pgu = [ps("pgu%d" % i, [128, 512], F32) for i in range(4)]
        pd = [ps("pd%d" % i, [128, 512], F32) for i in range(4)]
        ptr = [pd[2 + i][:].bitcast(BF16).rearrange("p (c t) -> p c t", c=8) for i in range(2)]
        DMA(K, 'sp', wbc[:], K.norm2_w[l:l + 1, :].partition_broadcast(128), w=['wbc'])
        if final_w is not None:
            DMA(K, 'sp', wbf[:], final_w.partition_broadcast(128), w=['wbf'])
        na = 0
        nb = 0
        nf = 0
        kd = 0
        for blk in range(S_LEN // TB):
            T0 = blk * TB
            mixT = wB[nb % 2]
            mkey = ('wB', nb % 2)
            nb += 1
            DMA(K, 'sp', mixT[:], K.mixT[:, T0:T0 + TB].rearrange("(c p) t -> p c t", p=128), w=[mkey])
            for tt in range(NT):
                DMA(K, 'sp', xm[tt][:], x_src[T0 + tt * 128:T0 + (tt + 1) * 128, :], w=[('xk', tt)])
            for cb in range(4):
                wo = wA[na % 2]
                wkey = ('wA', na % 2)
                na += 1
                DMA(K, 'pool', wo[:], K.w_out[l, :, cb * 512:(cb + 1) * 512].rearrange("(c p) n -> p c n", p=128), w=[wkey])
                for tt in range(NT):
                    k = kd % 4
                    kd += 1
                    for ch in range(16):
                        MM(K, pd[k][:], mixT[:, ch, tt * 128:(tt + 1) * 128], wo[:, ch, :], ch == 0, ch == 15,
                           r=[mkey, wkey], w=[('pd', k)])
                    TT(K, 'dve', xm[tt][:, cb * 512:(cb + 1) * 512], pd[k][:], xm[tt][:, cb * 512:(cb + 1) * 512], ALU.add,
                       r=[('pd', k), ('xk', tt)], w=[('xk', tt)])
            rms_to_hT(K, None, None, 0, NT, wbc, h2T, 'h2T', None, hb, st, junk, ptr, st_base=(blk % 8) * NT, keep_x=xm,
                      ptr_keys=[('pd', 2), ('pd', 3)])
            h_all = [('h2T', t) for t in range(NT)]
            for gi in range(DFF // 512):
                wg = wA[na % 2]
                gkey = ('wA', na % 2)
                na += 1
                wu = wB[nb % 2]
                ukey = ('wB', nb % 2)
                nb += 1
                DMA(K, 'pool', wg[:], K.w_gate[l, :, gi * 512:(gi + 1) * 512].rearrange("(c p) n -> p c n", p=128), w=[gkey])
                DMA(K, 'pool', wu[:], K.w_up[l, :, gi * 512:(gi + 1) * 512].rearrange("(c p) n -> p c n", p=128), w=[ukey])
                ff = ffT[nf % 2]
                fkey = ('ffT', nf % 2)
                nf += 1
                for fb in range(4):
                    pg = pgu[2 * (fb % 2)]
                    pu = pgu[2 * (fb % 2) + 1]
                    kg = ('pgu', 2 * (fb % 2))
                    ku = ('pgu', 2 * (fb % 2) + 1)
                    for ch in range(16):
                        MM(K, pg[:], wg[:, ch, fb * 128:(fb + 1) * 128], h2T[:, ch, :], ch == 0, ch == 15,
                           r=[gkey] + h_all, w=[kg])
                    for ch in range(16):
                        MM(K, pu[:], wu[:, ch, fb * 128:(fb + 1) * 128], h2T[:, ch, :], ch == 0, ch == 15,
                           r=[ukey] + h_all, w=[ku])
                    tm = tmp[fb % 2]
                    ACTV(K, tm[:], pg[:], AF.Silu, r=[kg], w=[('tmp', fb % 2)])
                    TT(K, 'dve', ff[:, fb, :], tm[:], pu[:], ALU.mult, r=[('tmp', fb % 2), ku], w=[fkey])
                wdn = wd[0]
                DMA(K, 'pool', wdn[:], K.w_down[l, gi * 512:(gi + 1) * 512, :].rearrange("(c p) n -> p c n", p=128), w=['wd'])
                for tt in range(NT):
                    for cb in range(4):
                        k = kd % 4
                        kd += 1
                        for c in range(4):
                            MM(K, pd[k][:], ff[:, c, tt * 128:(tt + 1) * 128], wdn[:, c, cb * 512:(cb + 1) * 512], c == 0, c == 3,
                               r=[fkey, 'wd'], w=[('pd', k)])
                        TT(K, 'dve', xm[tt][:, cb * 512:(cb + 1) * 512], pd[k][:], xm[tt][:, cb * 512:(cb + 1) * 512], ALU.add,
                           r=[('pd', k), ('xk', tt)], w=[('xk', tt)])
            for tt in range(NT):
                rows = slice(T0 + tt * 128, T0 + (tt + 1) * 128)
                if final_w is None:
                    DMA(K, 'sp', x_dst[rows, :], xm[tt][:], r=[('xk', tt)])
                else:
                    c = ((blk % 8) * NT + tt) * 4
                    sk = ('stf', tt)
                    STT(K, junk[:], xm[tt][:], 1.0, xm[tt][:], ALU.mult, ALU.mult, r=[('xk', tt)], w=['junk', sk],
                        accum_out=st[:, c + 3:c + 4])
                    ACTV(K, st[:, c + 1:c + 2], st[:, c + 3:c + 4], AF.Sqrt, r=[sk], w=[sk], scale=1.0 / D, bias=K.eps_t[:, 0:1])
                    K.S.op('dve', lambda e, c=c: e.reciprocal(out=st[:, c + 2:c + 3], in_=st[:, c + 1:c + 2]), r=[sk], w=[sk])
                    STT(K, xm[tt][:], xm[tt][:], st[:, c + 2:c + 3], wbf[:], ALU.mult, ALU.mult, r=[('xk', tt), sk, 'wbf'],
                        w=[('xk', tt)])
                    DMA(K, 'sp', out_dst[rows, :], xm[tt][:], r=[('xk', tt)])
        S.flush()


def phase_att(K, l):
    nc, S = K.nc, K.S
    with ExitStack() as es:
        sb = lambda n, s, d: es.enter_context(nc.sbuf_tensor(_uniq(n), s, d))
        ps = lambda n, s, d: es.enter_context(nc.psum_tensor(_uniq(n), s, d))
        b0 = sb("b0", [128, 9, 256], BF16)
        sid = sb("sid", [128, 24, 128], BF16)
        zt = sb("zt", [128, 512], BF16)
        qs = [sb("qs%d" % i, [128, 4, 128], BF16) for i in range(2)]
        ks = [sb("ks%d" % i, [128, 4, 256], BF16) for i in range(2)]
        vs = [sb("vs%d" % i, [128, 2, 512], BF16) for i in range(2)]
        Pm = [sb("Pm%d" % i, [128, 4, 256], BF16) for i in range(2)]
        PT = [sb("PT%d" % i, [128, 8, 128], BF16) for i in range(2)]
        mx = [sb("mx%d" % i, [128, 8], F32) for i in range(2)]
        nmx = [sb("nmx%d" % i, [128, 8], F32) for i in range(2)]
        ogt = [sb("ogt%d" % i, [128, 512], F32) for i in range(2)]
        mlt = [sb("mlt%d" % i, [128, 16], F32) for i in range(2)]
        psc = [ps("psc%d" % i, [128, 4, 256], F32) for i in range(2)]
        pT = [ps("pT%d" % i, [128, 8, 128], BF16) for i in range(2)]
        po = [ps("po%d" % i, [128, 512], F32) for i in range(2)]
        DMA(K, 'sp', b0[:], K.c_b0.rearrange("v p k -> p v k"), w=['b0'])
        DMA(K, 'sp', sid[:], K.c_sid.rearrange("h p k -> p h k"), w=['sid'])
        S.op('dve', lambda e: e.memset(zt[:], 0.0), w=['zt'])
        NP = S_LEN + 128
        for a_ in range(12):
            DMA(K, 'sp', K.kT[a_ * 128:(a_ + 1) * 128, 0:64], zt[:, 0:64], r=['zt'], w=['kT'])
            DMA(K, 'sp', K.kT[a_ * 128:(a_ + 1) * 128, NP - 64:NP], zt[:, 0:64], r=['zt'], w=['kT'])
        for g in range(3):
            DMA(K, 'sp', K.vv[g, 0:64, :], zt[0:64, :], r=['zt'], w=['vv'])
            DMA(K, 'sp', K.vv[g, NP - 64:NP, :], zt[0:64, :], r=['zt'], w=['vv'])
        units = [(g, ti) for g in range(3) for ti in range(NQT)]
        NU = len(units)

        def unit_info(n):
            g, ti = units[n]
            d = DILS[g]
            L = S_LEN // d
            r_ = (ti * 128) // L
            u0 = (ti * 128) % L
            var = 1 if u0 == 0 else (2 if u0 + 128 == L else 0)
            return g, d, L, r_, u0, var

        def stage_Q(k):
            n, hf = divmod(k, 2)
            i = n % 2
            g, d, L, r_, u0, var = unit_info(n)
            if hf == 0:
                P0 = 64 + r_ * L + u0
                rows = slice(g * 512, (g + 1) * 512)
                DMA(K, 'sp', qs[i][:], K.qT[rows, P0:P0 + 128].rearrange("(a p) t -> p a t", p=128), w=[('qs', i)])
                DMA(K, 'sp', ks[i][:], K.kT[rows, P0 - 64:P0 + 192].rearrange("(a p) t -> p a t", p=128), r=['kT'], w=[('ks', i)])
                DMA(K, 'sp', vs[i][:], K.vv[g, P0 - 64:P0 + 192, :].rearrange("(kt p) c -> p kt c", p=128), r=['vv'], w=[('vs', i)])
            for sl in range(4):
                s_ = 4 * hf + sl
                pr = s_ // 2
                prt = slice(64 * (s_ % 2), 64 * (s_ % 2) + 64)
                MM(K, psc[hf][:, sl, :], qs[i][prt, pr, :], ks[i][prt, pr, :], True, False,
                   r=[('qs', i), ('ks', i)], w=[('psc', hf)])
                MM(K, psc[hf][:, sl, :], sid[:, g * 8 + s_, :], b0[:, g * 3 + var, :], False, True,
                   r=['sid', 'b0'], w=[('psc', hf)])
            nms = mlt[i][:, 4 * hf:4 * hf + 4]
            S.op('dve', lambda e, nms=nms, hf=hf: e.tensor_reduce(out=nms, in_=psc[hf][:], axis=AX.X, op=ALU.max, negate=True),
                 r=[('psc', hf)], w=[('mltm', i, hf)])
            for sl in range(4):
                s_ = 4 * hf + sl
                ACTV(K, Pm[hf][:, sl, :], psc[hf][:, sl, :], AF.Exp, r=[('psc', hf), ('mltm', i, hf)],
                     w=[('Pm', hf, sl), ('mltl', i, s_)], bias=mlt[i][:, s_:s_ + 1],
                     accum_out=mlt[i][:, 8 + s_:9 + s_])

        def stage_T(k):
            n, hf = divmod(k, 2)
            for sl in range(4):
                for kt in range(2):
                    TR(K, pT[hf][:, sl * 2 + kt, :], Pm[hf][:, sl, kt * 128:(kt + 1) * 128],
                       r=[('Pm', hf, sl), 'ident'], w=[('pT', hf)])
            CP(K, 'dve' if hf == 0 else 'act', PT[hf][:], pT[hf][:], r=[('pT', hf)], w=[('PT', hf)])

        def stage_V(k):
            n, hf = divmod(k, 2)
            i = n % 2
            g, d, L, r_, u0, var = unit_info(n)
            for sl in range(4):
                s_ = 4 * hf + sl
                for kt in range(2):
                    MM(K, po[i][:, s_ * 64:(s_ + 1) * 64], PT[hf][:, sl * 2 + kt, :], vs[i][:, kt, s_ * 64:(s_ + 1) * 64],
                       kt == 0, kt == 1, r=[('PT', hf), ('vs', i)], w=[('po', i)])
            if hf == 1:
                CP(K, 'act', ogt[i][:], po[i][:], r=[('po', i)], w=[('ogt', i)])
                t0 = r_ + d * u0
                tsl = slice(t0, t0 + 127 * d + 1, d) if d > 1 else slice(t0, t0 + 128)
                DMA(K, 'sp', K.og[g, tsl, :], ogt[i][:], r=[('ogt', i)])
                DMA(K, 'sp', K.mlg[g, tsl, :], mlt[i][:],
                    r=[('mltm', i, 0), ('mltm', i, 1)] + [('mltl', i, q_) for q_ in range(8)])

        NK = 2 * NU
        for k in range(NK + 2):
            if k < NK:
                stage_Q(k)
            if 0 <= k - 1 < NK:
                stage_T(k - 1)
            if 0 <= k - 2 < NK:
                stage_V(k - 2)
        S.flush()
    with ExitStack() as es:
        sb = lambda n, s, d: es.enter_context(nc.sbuf_tensor(_uniq(n), s, d))
        ps = lambda n, s, d: es.enter_context(nc.psum_tensor(_uniq(n), s, d))
        o3 = [sb("o3_%d" % i, [128, 3, 512], F32) for i in range(2)]
        ml3 = [sb("ml3_%d" % i, [128, 3, 16], F32) for i in range(2)]
        sm = [sb("sm%d" % i, [128, 64], F32) for i in range(2)]
        w3 = [sb("w3_%d" % i, [128, 3, 8], F32) for i in range(2)]
        acc = [sb("acc%d" % i, [128, 512], F32) for i in range(2)]
        t2 = [sb("t2_%d" % i, [128, 512], F32) for i in range(2)]
        ab = [sb("ab%d" % i, [128, 512], BF16) for i in range(2)]
        aT = [sb("aT%d" % i, [128, 4, 128], BF16) for i in range(2)]
        pa = [ps("pa%d" % i, [128, 4, 128], BF16) for i in range(2)]
        def merge_tile(ti):
                i = ti % 2
                rows = slice(ti * 128, (ti + 1) * 128)
                DMA(K, 'sp', o3[i][:], K.og[:, rows, :].rearrange("g p c -> p g c"), w=[('o3', i)])
                yield
                DMA(K, 'sp', ml3[i][:], K.mlg[:, rows, :].rearrange("g p c -> p g c"), w=[('ml3', i)])
                yield
                M = sm[i][:, 0:8]
                den = sm[i][:, 8:16]
                rden = sm[i][:, 16:24]
                k3 = [('ml3', i)]
                ks_ = [('sm', i)]
                TT(K, 'dve', M, ml3[i][:, 0, 0:8], ml3[i][:, 1, 0:8], ALU.min, r=k3, w=ks_)
                yield
                TT(K, 'dve', M, M, ml3[i][:, 2, 0:8], ALU.min, r=k3 + ks_, w=ks_)
                yield
                for g in range(3):
                    TT(K, 'dve', w3[i][:, g, :], ml3[i][:, g, 0:8], M, ALU.subtract, r=k3 + ks_, w=[('w3', i)])
                    yield
                ACTV(K, w3[i][:], w3[i][:], AF.Exp, r=[('w3', i)], w=[('w3', i)], scale=-1.0)
                yield
                for g in range(3):
                    TT(K, 'dve', sm[i][:, 24 + 8 * g:32 + 8 * g], w3[i][:, g, :], ml3[i][:, g, 8:16], ALU.mult,
                       r=k3 + [('w3', i)], w=ks_)
                    yield
                TT(K, 'dve', den, sm[i][:, 24:32], sm[i][:, 32:40], ALU.add, r=ks_, w=ks_)
                yield
                TT(K, 'dve', den, den, sm[i][:, 40:48], ALU.add, r=ks_, w=ks_)
                yield
                S.op('dve', lambda e, rden=rden, den=den: e.reciprocal(out=rden, in_=den), r=ks_, w=ks_)
                yield
                for g in range(3):
                    TT(K, 'dve', w3[i][:, g, :], w3[i][:, g, :], rden, ALU.mult, r=ks_ + [('w3', i)], w=[('w3', i)])
                    yield
                for g in range(3):
                    wb_ = w3[i][:, g, :].unsqueeze(2).to_broadcast([128, 8, 64])
                    src = o3[i][:, g, :].rearrange("p (s e) -> p s e", e=64)
                    dst = (acc[i] if g == 0 else t2[i])[:].rearrange("p (s e) -> p s e", e=64)
                    eng = 'dve' if g != 1 else 'pool'
                    TT(K, eng, dst, src, wb_, ALU.mult, r=[('o3', i), ('w3', i)], w=[('acc', i) if g == 0 else ('t2', i)])
                    yield
                    if g > 0:
                        outap = acc[i][:] if g == 1 else ab[i][:]
                        TT(K, 'dve', outap, acc[i][:], t2[i][:], ALU.add, r=[('acc', i), ('t2', i)],
                           w=[('acc', i)] if g == 1 else [('ab', i)])
                        yield
                for a_ in range(4):
                    TR(K, pa[i][:, a_, :], ab[i][:, a_ * 128:(a_ + 1) * 128], r=[('ab', i), 'ident'], w=[('pa', i)])
                    yield
                CP(K, 'act', aT[i][:], pa[i][:], r=[('pa', i)], w=[('aT', i)])
                yield
                DMA(K, 'sp', K.mixT[0:512, rows].rearrange("(a p) t -> p a t", p=128), aT[i][:], r=[('aT', i)])
                yield

        for t0_ in range(0, NQT, 2):
            gens = [merge_tile(t0_), merge_tile(t0_ + 1)]
            alive = [True, True]
            while any(alive):
                for gi_ in range(2):
                    if alive[gi_]:
                        try:
                            next(gens[gi_])
                        except StopIteration:
                            alive[gi_] = False
        S.flush()


def phase_pool(K, l):
    nc, S = K.nc, K.S
    with ExitStack() as es:
        sb = lambda n, s, d: es.enter_context(nc.sbuf_tensor(_uniq(n), s, d))
        ps = lambda n, s, d: es.enter_context(nc.psum_tensor(_uniq(n), s, d))
        band = sb("band", [128, 20, 128], BF16)
        pw = sb("pw", [128, 4, 128], BF16)
        psc_ = sb("pscale", [128, 4], F32)
        ut = [sb("ut%d" % i, [128, 3, 512], BF16) for i in range(2)]
        rt = [sb("rt%d" % i, [128, 4, 128], BF16) for i in range(2)]
        ot = [sb("ot%d" % i, [128, 4, 128], BF16) for i in range(2)]
        pr = [ps("pr%d" % i, [128, 4, 128], F32) for i in range(2)]
        pq = [ps("pq%d" % i, [128, 4, 128], F32) for i in range(2)]
        DMA(K, 'sp', band[:], K.c_band.rearrange("g v p k -> p (g v) k"), w=['band'])
        DMA(K, 'pool', pw[:], K.pool_w[l].rearrange("g p k -> p g k"), w=['pw'])
        DMA(K, 'sp', psc_[:], K.pool_scale[l].rearrange("(g p) -> p g", p=128), w=['pscale'], slow=True)
        for ti in range(NQT):
            i = ti % 2
            lo = 0 if ti > 0 else 1
            hi = 3 if ti < NQT - 1 else 2
            R0 = 128 + (ti - 1) * 128
            DMA(K, 'sp', ut[i][:, lo:hi, :], K.uu[R0 + lo * 128:R0 + hi * 128, :].rearrange("(a p) c -> p a c", p=128),
                w=[('ut', i)])
            for g in range(4):
                own = 1 if ti == 0 else (2 if ti == NQT - 1 else 0)
                terms = [(1, own)]
                if ti > 0:
                    terms.append((0, 3))
                if ti < NQT - 1:
                    terms.append((2, 4))
                for n_, (a_, v) in enumerate(terms):
                    MM(K, pr[i][:, g, :], ut[i][:, a_, g * 128:(g + 1) * 128], band[:, g * 5 + v, :], n_ == 0, n_ == len(terms) - 1,
                       r=[('ut', i), 'band'], w=[('pr', i)])
            CP(K, 'act', rt[i][:], pr[i][:], r=[('pr', i)], w=[('rt', i)])
            for g in range(4):
                MM(K, pq[i][:, g, :], pw[:, g, :], rt[i][:, g, :], True, True, r=['pw', ('rt', i)], w=[('pq', i)])
            for g in range(4):
                TS(K, 'dve', ot[i][:, g, :], pq[i][:, g, :], psc_[:, g:g + 1], ALU.mult, r=[('pq', i), 'pscale'], w=[('ot', i)])
            DMA(K, 'sp', K.mixT[1536:2048, ti * 128:(ti + 1) * 128].rearrange("(a p) t -> p a t", p=128), ot[i][:],
                r=[('ot', i)])
        S.flush()


def phase_ssd(K, l):
    nc, S = K.nc, K.S
    NC_ = NQT
    with ExitStack() as es:
        sb = lambda n, s, d: es.enter_context(nc.sbuf_tensor(_uniq(n), s, d))
        bk = [es.enter_context(nc.psum_tensor(_uniq("bk%d" % i), [128, 512], F32)) for i in range(8)]
        B = lambda i: ('bk', i)
        tri = sb("tri", [128, 4, 128], F32)
        onesf = sb("onesf", [128, 128], F32)
        identf = sb("identf", [128, 128], F32)
        one_t = sb("one_t", [128, 1], F32)
        zt = sb("zt2", [128, 16], BF16)
        dt_all = sb("dt_all", [128, NC_, 32], F32)
        dta = sb("dta", [128, NC_, 32], F32)
        tmpa = sb("tmpa", [128, NC_, 32], F32)
        tmpb = sb("tmpb", [128, NC_, 32], F32)
        dtb = sb("dtb", [128, 32], F32)
        abc = sb("abc", [128, 32], F32)
        E = sb("E", [128, NC_, 64], F32)
        cd = sb("cd", [128, NC_, 32], F32)
        wx = sb("wx", [128, NC_, 4, 16], F32)
        cw = sb("cw", [128, 5, 12], F32)
        cb = sb("cb", [128, 12], F32)
        dg = sb("dg", [128, 12, 5, 128], BF16)
        dsk = sb("dsk", [128, 16], F32)
        nw = sb("nw", [128, 1024], F32)
        CTall = sb("CTall", [128, NC_, 2, 128], BF16)
        xin = sb("xin", [128, 12, 132], BF16)
        xc = sb("xc", [128, 12, 128], BF16)
        xsB = sb("xsB", [128, 1280], BF16)
        cbm = sb("cbm", [128, 2, 2, 128], F32)
        X = sb("X", [128, 2, 16, 128], F32)
        seg = sb("seg", [128, 2, 16, 128], BF16)
        MT = sb("MT", [128, 2, 16, 128], BF16)
        xdt = sb("xdt", [128, 2, 1024], BF16)
        xdd = sb("xdd", [128, 2, 1024], BF16)
        tA = sb("tA", [128, 1024], F32)
        tB = sb("tB", [128, 1024], F32)
        Sf = sb("Sf", [128, 1024], F32)
        Sfb = sb("Sfb", [128, 1024], BF16)
        stt = sb("stt", [128, 1024], F32)
        yb = sb("yb", [128, 1024], BF16)
        yT = sb("yT", [128, 8, 128], BF16)
        st = sb("st3", [128, 8], F32)

        DMA(K, 'sp', tri[:], K.c_tri.rearrange("v p k -> p v k"), w=['tri'])
        DMA(K, 'sp', identf[:], K.c_identf[:, :], w=['identf'])
        S.op('dve', lambda e: e.memset(onesf[:], 1.0), w=['onesf'])
        S.op('dve', lambda e: e.memset(one_t[:], 1.0), w=['one_t'])
        S.op('dve', lambda e: e.memset(zt[:], 0.0), w=['zt'])
        for a_ in range(12):
            DMA(K, 'sp', K.xbcT[a_ * 128:(a_ + 1) * 128, 0:2], zt[:, 0:2], r=['zt'], w=['xbcT'])
            DMA(K, 'sp', K.xbcT[a_ * 128:(a_ + 1) * 128, S_LEN + 2:S_LEN + 4], zt[:, 0:2], r=['zt'], w=['xbcT'])
        DMA(K, 'sp', dt_all[:], K.dtr.rearrange("(c p) k -> p c k", p=128), w=['dt_all'])
        DMA(K, 'sp', dtb[:], K.dt_bias[l:l + 1, :].partition_broadcast(128), w=['dtb'])
        DMA(K, 'sp', abc[:], K.a_log[l:l + 1, :].partition_broadcast(128), w=['abc'])
        DMA(K, 'sp', dsk[:], K.d_skip[l:l + 1, :].partition_broadcast(128), w=['dsk'])
        DMA(K, 'sp', nw[:], K.ssd_norm_w[l:l + 1, :].partition_broadcast(128), w=['nw'])
        for k in range(5):
            DMA(K, 'sp', cw[:, k, :], K.conv_w[l, k].rearrange("(ct p) -> p ct", p=128), w=['cw'], slow=True)
        DMA(K, 'sp', cb[:], K.conv_b[l].rearrange("(ct p) -> p ct", p=128), w=['cb'], slow=True)
        for ct in range(12):
            for k in range(5):
                TS(K, 'pool' if (ct + k) % 2 else 'dve', dg[:, ct, k, :], identf[:], cw[:, k, ct:ct + 1], ALU.mult,
                   r=['identf', 'cw'], w=['dg'])
        ACTV(K, abc[:], abc[:], AF.Exp, r=['abc'], w=['abc'])
        TS(K, 'dve', abc[:], abc[:], -1.0, ALU.mult, r=['abc'], w=['abc'])
        bc32 = lambda t: t[:].unsqueeze(1).to_broadcast([128, NC_, 32])
        TT(K, 'dve', dt_all[:], dt_all[:], bc32(dtb), ALU.add, r=['dt_all', 'dtb'], w=['dt_all'])
        TS(K, 'dve', tmpb[:], dt_all[:], -1.0, ALU.mult, r=['dt_all'], w=['tmpb'])
        TT(K, 'dve', tmpa[:], dt_all[:], tmpb[:], ALU.max, r=['dt_all', 'tmpb'], w=['tmpa'])
        ACTV(K, tmpa[:], tmpa[:], AF.Exp, r=['tmpa'], w=['tmpa'], scale=-1.0)
        ACTV(K, tmpa[:], tmpa[:], AF.Ln, r=['tmpa', 'one_t'], w=['tmpa'], bias=one_t[:, 0:1])
        TS(K, 'dve', tmpb[:], dt_all[:], 0.0, ALU.max, r=['dt_all'], w=['tmpb'])
        TT(K, 'dve', dt_all[:], tmpa[:], tmpb[:], ALU.add, r=['tmpa', 'tmpb'], w=['dt_all'])
        TT(K, 'dve', dta[:], dt_all[:], bc32(abc), ALU.mult, r=['dt_all', 'abc'], w=['dta'])
        for c in range(NC_):
            bi = c // 8
            for v in range(4):
                cols = slice((c % 8) * 64 + v * 16, (c % 8) * 64 + v * 16 + 16)
                dsl = slice(0, 16) if v < 2 else slice(16, 32)
                MM(K, bk[bi][:, cols], tri[:, v, :], dta[:, c, dsl], True, True, r=['tri', 'dta'], w=[B(bi)])
        for bi in range(4):
            ACTV(K, E[:, bi * 8:(bi + 1) * 8, :], bk[bi][:].rearrange("p (c k) -> p c k", k=64), AF.Exp, r=[B(bi)], w=['E'])
        for hf in range(2):
            MM(K, bk[4 + hf][:], onesf[:], dta[:, hf * 16:(hf + 1) * 16, :], True, True, r=['onesf', 'dta'], w=[B(4 + hf)])
            ACTV(K, cd[:, hf * 16:(hf + 1) * 16, :], bk[4 + hf][:].rearrange("p (c k) -> p c k", k=32), AF.Exp,
                 r=[B(4 + hf)], w=['cd'])
        for dr in range(2):
            CP(K, 'dve', wx[:, :, dr, :], dt_all[:, :, dr * 16:(dr + 1) * 16], r=['dt_all'], w=['wx'])
            TT(K, 'dve', wx[:, :, 2 + dr, :], dt_all[:, :, dr * 16:(dr + 1) * 16], E[:, :, 16 + 32 * dr:32 + 32 * dr], ALU.mult,
               r=['dt_all', 'E'], w=['wx'])
        S.op('dve', lambda e: e.memset(Sf[:], 0.0), w=['Sf'])
        S.op('dve', lambda e: e.memset(Sfb[:], 0.0), w=['Sfb'])
        bc = lambda ap, shape, ax: ap.unsqueeze(ax).to_broadcast(shape)
        nd = 0
        for c in range(NC_):
            T0 = c * 128
            DMA(K, 'sp', xin[:], K.xbcT[:, T0:T0 + 132].rearrange("(ct p) t -> p ct t", p=128), r=['xbcT'], w=['xin'])
            for dr in range(2):
                TT(K, 'pool', X[:, dr, :, :], bc(tri[:, 0 if dr == 0 else 2, :], [128, 16, 128], 1),
                   bc(dta[:, c, dr * 16:(dr + 1) * 16], [128, 16, 128], 2), ALU.mult, r=['tri', 'dta'], w=['X'])
            for ct in range(12):
                bi = ct // 4
                for k in range(5):
                    MM(K, bk[bi][:, (ct % 4) * 128:(ct % 4 + 1) * 128], dg[:, ct, k, :], xin[:, ct, k:k + 128], k == 0, k == 4,
                       r=['dg', 'xin'], w=[B(bi)])
            for ct in range(12):
                bi = ct // 4
                ACTV(K, xc[:, ct, :], bk[bi][:, (ct % 4) * 128:(ct % 4 + 1) * 128], AF.Silu, r=[B(bi), 'cb'], w=['xc'],
                     bias=cb[:, ct:ct + 1])
            CP(K, 'pool', CTall[:, c, :, :], xc[:, 10:12, :], r=['xc'], w=['CTall'])
            for dr in range(2):
                for q4 in range(4):
                    bi = 6 + (nd % 2)
                    nd += 1
                    MM(K, bk[bi][:], tri[:, 1 if dr == 0 else 3, :], X[:, dr, q4 * 4:(q4 + 1) * 4, :], True, True,
                       r=['tri', 'X'], w=[B(bi)])
                    ACTV(K, seg[:, dr, q4 * 4:(q4 + 1) * 4, :], bk[bi][:].rearrange("p (h l) -> p h l", h=4), AF.Exp,
                         r=[B(bi)], w=['seg'])
            bv3 = bk[3][:].bitcast(BF16)
            bv4 = bk[4][:].bitcast(BF16)
            for ct in range(10):
                dst = bv3[:, ct * 128:(ct + 1) * 128] if ct < 8 else bv4[:, (ct - 8) * 128:(ct - 7) * 128]
                TR(K, dst, xc[:, ct, :], r=['xc', 'ident'], w=[B(3) if ct < 8 else B(4)])
            CP(K, 'dve', xsB[:, 0:1024], bv3[:, 0:1024], r=[B(3)], w=['xsB'])
            CP(K, 'dve', xsB[:, 1024:1280], bv4[:, 0:256], r=[B(4)], w=['xsB'])
            for g in range(2):
                MM(K, bk[5][:, g * 128:(g + 1) * 128], xc[:, 8 + g, :], xc[:, 10 + g, :], True, True, r=['xc'], w=[B(5)])
            for dr in range(2):
                TT(K, 'dve', cbm[:, dr, :, :], bk[5][:, 0:256].rearrange("p (g l) -> p g l", g=2),
                   bc(tri[:, 0 if dr == 0 else 2, :], [128, 2, 128], 1), ALU.mult, r=[B(5), 'tri'], w=['cbm'])
            for dr in range(2):
                for g in range(2):
                    TT(K, 'dve' if g == 0 else 'pool', MT[:, dr, g * 8:(g + 1) * 8, :], seg[:, dr, g * 8:(g + 1) * 8, :],
                       bc(cbm[:, dr, g, :], [128, 8, 128], 1), ALU.mult, r=['seg', 'cbm'], w=['MT'])
            xs3 = xsB[:, 0:1024].rearrange("p (h e) -> p h e", e=64)
            for dr in range(2):
                TT(K, 'pool', xdt[:, dr, :].rearrange("p (h e) -> p h e", e=64), xs3, bc(wx[:, c, dr, :], [128, 16, 64], 2),
                   ALU.mult, r=['xsB', 'wx'], w=['xdt'])
                TT(K, 'dve', xdd[:, dr, :].rearrange("p (h e) -> p h e", e=64), xs3, bc(wx[:, c, 2 + dr, :], [128, 16, 64], 2),
                   ALU.mult, r=['xsB', 'wx'], w=['xdd'])
            for hh in range(16):
                bi = hh // 8
                cols = slice((hh % 8) * 64, (hh % 8) * 64 + 64)
                MM(K, bk[bi][:, cols], MT[:, 0, hh, :], xdt[:, 0, hh * 64:(hh + 1) * 64], True, False, r=['MT', 'xdt'], w=[B(bi)])
                MM(K, bk[bi][:, cols], MT[:, 1, hh, :], xdt[:, 1, hh * 64:(hh + 1) * 64], False, True, r=['MT', 'xdt'], w=[B(bi)])
            for g in range(2):
                MM(K, bk[2 + g][:], xc[:, 10 + g, :], Sfb[:, g * 512:(g + 1) * 512], True, True, r=['xc', 'Sfb'], w=[B(2 + g)])
            for dr in range(2):
                for g in range(2):
                    MM(K, bk[4 + 2 * dr + g][:], xsB[:, 1024 + g * 128:1024 + (g + 1) * 128], xdd[:, dr, g * 512:(g + 1) * 512],
                       True, True, r=['xsB', 'xdd'], w=[B(4 + 2 * dr + g)])
            for g in range(2):
                cs = slice(g * 512, (g + 1) * 512)
                v3 = lambda ap: ap.rearrange("p (h e) -> p h e", e=64)
                TT(K, 'dve', v3(tA[:, cs]), v3(bk[2 + g][:]), bc(E[:, c, g * 8:(g + 1) * 8], [128, 8, 64], 2), ALU.mult,
                   r=[B(2 + g), 'E'], w=['tA'])
                TT(K, 'dve', tA[:, cs], tA[:, cs], bk[g][:], ALU.add, r=['tA', B(g)], w=['tA'])
            TT(K, 'pool', tB[:].rearrange("p (h e) -> p h e", e=64), xs3, bc(dsk[:, :], [128, 16, 64], 2), ALU.mult,
               r=['xsB', 'dsk'], w=['tB'])
            TT(K, 'pool', tA[:], tA[:], tB[:], ALU.add, r=['tA', 'tB'], w=['tA'])
            DMA(K, 'sp', K.ypart[T0:T0 + 128, :], tA[:], r=['tA'], w=['ypart'])
            for g in range(2):
                CP(K, 'act', stt[:, g * 512:(g + 1) * 512], bk[6 + g][:], r=[B(6 + g)], w=['stt'])
            DMA(K, 'sp', K.stb[c], stt[:], r=['stt'], w=['stb'])
            TT(K, 'dve', Sf[:].rearrange("p (h e) -> p h e", e=64), Sf[:].rearrange("p (h e) -> p h e", e=64),
               bc(cd[:, c, 0:16], [128, 16, 64], 2), ALU.mult, r=['Sf', 'cd'], w=['Sf'])
            for g in range(2):
                cs = slice(g * 512, (g + 1) * 512)
                TT(K, 'dve', Sf[:, cs], Sf[:, cs], bk[4 + g][:], ALU.add, r=['Sf', B(4 + g)], w=['Sf'])
            CP(K, 'act', Sfb[:], Sf[:], r=['Sf'], w=['Sfb'])
        S.op('dve', lambda e: e.memset(Sf[:], 0.0), w=['Sf'])
        S.op('dve', lambda e: e.memset(Sfb[:], 0.0), w=['Sfb'])
        for c in range(NC_ - 1, -1, -1):
            T0 = c * 128
            DMA(K, 'sp', tA[:], K.ypart[T0:T0 + 128, :], r=['ypart'], w=['tA'])
            DMA(K, 'sp', tB[:], K.zz[T0:T0 + 128, :], w=['tB'])
            DMA(K, 'sp', stt[:], K.stb[c], r=['stb'], w=['stt'])
            for g in range(2):
                MM(K, bk[g][:], CTall[:, c, g, :], Sfb[:, g * 512:(g + 1) * 512], True, True, r=['CTall', 'Sfb'], w=[B(g)])
            for g in range(2):
                cs = slice(g * 512, (g + 1) * 512)
                v3 = lambda ap: ap.rearrange("p (h e) -> p h e", e=64)
                xq = X[:, 0, 0:4, :].rearrange("p a b -> p (a b)")
                TT(K, 'dve', v3(xq), v3(bk[g][:]), bc(E[:, c, 32 + g * 8:40 + g * 8], [128, 8, 64], 2), ALU.mult,
                   r=[B(g), 'E'], w=['X'])
                TT(K, 'dve', tA[:, cs], tA[:, cs], xq, ALU.add, r=['tA', 'X'], w=['tA'])
            ACTV(K, tB[:], tB[:], AF.Silu, r=['tB'], w=['tB'])
            TT(K, 'dve', tA[:], tA[:], tB[:], ALU.mult, r=['tA', 'tB'], w=['tA'])
            for g in range(2):
                cs = slice(g * 512, (g + 1) * 512)
                STT(K, tB[:, cs], tA[:, cs], 1.0, tA[:, cs], ALU.mult, ALU.mult, r=['tA'], w=['tB', 'st'], accum_out=st[:, g:g + 1])
            ACTV(K, st[:, 2:4], st[:, 0:2], AF.Sqrt, r=['st'], w=['st'], scale=1.0 / 512, bias=K.eps_t[:, 0:1])
            S.op('dve', lambda e: e.reciprocal(out=st[:, 4:6], in_=st[:, 2:4]), r=['st'], w=['st'])
            for g in range(2):
                cs = slice(g * 512, (g + 1) * 512)
                STT(K, yb[:, cs], tA[:, cs], st[:, 4 + g:5 + g], nw[:, cs], ALU.mult, ALU.mult, r=['tA', 'st', 'nw'], w=['yb'])
            bv3 = bk[3][:].bitcast(BF16)
            for a_ in range(8):
                TR(K, bv3[:, a_ * 128:(a_ + 1) * 128], yb[:, a_ * 128:(a_ + 1) * 128], r=['yb', 'ident'], w=[B(3)])
            CP(K, 'act', yT[:], bv3[:, 0:1024].rearrange("p (a t) -> p a t", a=8), r=[B(3)], w=['yT'])
            DMA(K, 'sp', K.mixT[512:1536, T0:T0 + 128].rearrange("(a p) t -> p a t", p=128), yT[:], r=['yT'])
            TT(K, 'dve', Sf[:].rearrange("p (h e) -> p h e", e=64), Sf[:].rearrange("p (h e) -> p h e", e=64),
               bc(cd[:, c, 16:32], [128, 16, 64], 2), ALU.mult, r=['Sf', 'cd'], w=['Sf'])
            TT(K, 'dve', Sf[:], Sf[:], stt[:], ALU.add, r=['Sf', 'stt'], w=['Sf'])
            CP(K, 'act', Sfb[:], Sf[:], r=['Sf'], w=['Sfb'])
        S.flush()


def phase_B(K, l):
    phase_att(K, l)
    phase_pool(K, l)
    phase_ssd(K, l)


def setup_common(K, es):
    nc = K.nc
    K.ident = es.enter_context(nc.sbuf_tensor("ident", [128, 128], BF16))
    K.eps_t = es.enter_context(nc.sbuf_tensor("eps_t", [128, 1], F32))
    K.S.op('sp', lambda e: e.dma_start(out=K.ident[:], in_=K.c_ident[:, :]), w=['ident'], dma=True)
    K.S.op('dve', lambda e: e.memset(K.eps_t[:], EPS), w=['eps'])
    K.S.flush()


def declare_io(K, nc, dbg):
    dbg = dbg or {}
    di = lambda n, s, d: nc.dram_tensor(n, s, d, kind="ExternalInput").ap()
    K.x = di("x", [S_LEN, D], F32)
    K.norm1_w = di("norm1_w", [DEPTH, D], F32)
    K.w_in = di("w_in", [DEPTH, D, IN_W], F32)
    K.conv_w = di("conv_w", [DEPTH, 5, 1536], F32)
    K.conv_b = di("conv_b", [DEPTH, 1536], F32)
    K.dt_bias = di("dt_bias", [DEPTH, 32], F32)
    K.a_log = di("a_log", [DEPTH, 32], F32)
    K.d_skip = di("d_skip", [DEPTH, 16], F32)
    K.ssd_norm_w = di("ssd_norm_w", [DEPTH, 1024], F32)
    K.pool_w = di("pool_w", [DEPTH, 4, 128, 128], F32)
    K.pool_scale = di("pool_scale", [DEPTH, 512], F32)
    K.w_out = di("w_out", [DEPTH, D, D], F32)
    K.norm2_w = di("norm2_w", [DEPTH, D], F32)
    K.w_gate = di("w_gate", [DEPTH, D, DFF], F32)
    K.w_up = di("w_up", [DEPTH, D, DFF], F32)
    K.w_down = di("w_down", [DEPTH, DFF, D], F32)
    K.final_norm_w = di("final_norm_w", [1, D], F32)
    K.c_ident = di("c_ident", [128, 128], BF16)
    K.c_identf = di("c_identf", [128, 128], F32)
    K.c_tri = di("c_tri", [4, 128, 128], F32)
    K.c_b0 = di("c_b0", [9, 128, 256], BF16)
    K.c_sid = di("c_sid", [24, 128, 128], BF16)
    K.c_band = di("c_band", [4, 5, 128, 128], BF16)
    K.out = nc.dram_tensor("out", [S_LEN, D], F32, kind="ExternalOutput").ap()
    dsc = lambda n, s, d: nc.dram_tensor(n, s, d, kind=dbg.get(n, "Internal")).ap()
    NP = S_LEN + 128
    K.qT = dsc("qT", [1536, NP], BF16)
    K.kT = dsc("kT", [1536, NP], BF16)
    K.vv = dsc("vv", [3, NP, 512], BF16)
    K.zz = dsc("zz", [S_LEN, 1024], F32)
    K.xbcT = dsc("xbcT", [1536, S_LEN + 4], BF16)
    K.dtr = dsc("dtr", [S_LEN, 32], F32)
    K.uu = dsc("uu", [S_LEN + 256, 512], BF16)
    K.mixT = dsc("mixT", [D, S_LEN], BF16)
    K.x1 = dsc("x1", [S_LEN, D], F32)
    K.og = dsc("og", [3, S_LEN, 512], F32)
    K.mlg = dsc("mlg", [3, S_LEN, 16], F32)
    K.ypart = dsc("ypart", [S_LEN, 1024], F32)
    K.stb = dsc("stb", [NQT, 128, 1024], F32)


def build(dbg=None, phases=None):
    nc = bass.Bass("TRN2", target_bir_lowering=False)
    K = Ctx()
    K.nc = nc
    declare_io(K, nc, dbg)
    with ExitStack() as es:
        K.S = Sched(nc, es)
        setup_common(K, es)
        if phases is not None:
            phases(K)
        else:
            for l in range(DEPTH):
                xs = K.x if l == 0 else K.x1
                phase_A(K, l, xs)
                phase_B(K, l)
                if l == DEPTH - 1:
                    phase_CD(K, l, xs, None, final_w=K.final_norm_w[0:1, :], out_dst=K.out)
                else:
                    phase_CD(K, l, xs, K.x1)
    return nc


def host_consts():
    bf = ml_dtypes.bfloat16
    c = {}
    c["c_ident"] = np.eye(128, dtype=np.float32).astype(bf)
    c["c_identf"] = np.eye(128, dtype=np.float32)
    j = np.arange(128)[:, None]
    l_ = np.arange(128)[None, :]
    c["c_tri"] = np.stack([(j <= l_), (j > l_), (j >= l_), (j < l_)]).astype(np.float32)
    i = np.arange(128)[:, None]
    jj = np.arange(256)[None, :]
    rel = jj - 64 - i
    b0 = np.zeros((9, 128, 256), np.float32)
    BIG = 1.0e6
    for g, d in enumerate(DILS):
        for v in range(3):
            ok = np.abs(rel) <= 64
            if v == 1:
                ok = ok & (jj >= 64)
            if v == 2:
                ok = ok & (jj < 192)
            b0[g * 3 + v] = np.where(ok, -np.abs(rel) * float(d), -BIG)
    c["c_b0"] = b0.astype(bf)
    kk = np.arange(1, 25, dtype=np.float32)
    slopes = (2.0 ** (-8.0 * kk / 24.0)).astype(np.float32)
    c["c_sid"] = (np.eye(128, dtype=np.float32)[None] * slopes[:, None, None]).astype(bf)
    band = np.zeros((4, 5, 128, 128), np.float32)
    tp = np.arange(128)[:, None]
    t = np.arange(128)[None, :]
    for g, w in enumerate((2, 4, 8, 16)):
        hw = w // 2
        inwin = (tp >= t - hw) & (tp < t + hw)
        eye = (tp == t).astype(np.float32)
        band[g, 0] = inwin / float(w) - eye
        cnt_first = (t + hw) - np.maximum(t - hw, 0)
        band[g, 1] = inwin / cnt_first.astype(np.float32) - eye
        cnt_last = np.minimum(t + hw, 128) - (t - hw)
        band[g, 2] = inwin / cnt_last.astype(np.float32) - eye
        band[g, 3] = ((tp - 128) >= t - hw) / float(w)
        band[g, 4] = ((tp + 128) < t + hw) / float(w)
    c["c_band"] = band.astype(bf)
    return c


def make_inputs(inputs, b):
    f = lambda a: np.ascontiguousarray(np.asarray(a, dtype=np.float32))
    im = {"x": f(inputs["x"][b])}
    for n in ("norm1_w", "w_in", "conv_w", "conv_b", "d_skip", "ssd_norm_w", "pool_w", "pool_scale", "w_out",
              "norm2_w", "w_gate", "w_up", "w_down"):
        im[n] = f(inputs[n])
    im["dt_bias"] = f(inputs["dt_bias"]).reshape(DEPTH, 32)
    im["a_log"] = f(inputs["a_log"]).reshape(DEPTH, 32)
    im["final_norm_w"] = f(inputs["final_norm_w"]).reshape(1, D)
    im.update(host_consts())
    return im


_NC_CACHE = {}


def kernel(**inputs):
    if "nc" not in _NC_CACHE:
        _NC_CACHE["nc"] = build()
    nc = _NC_CACHE["nc"]
    B = inputs["x"].shape[0]
    in_maps = [make_inputs(inputs, c % B) for c in range(8)]
    res = run_bass_kernel_spmd(nc, in_maps, core_ids=list(range(8)))
    out = np.stack([np.asarray(res.results[b]["out"], dtype=np.float32) for b in range(B)], axis=0)
    return out
```

```python
import numpy as np
from contextlib import ExitStack
import ml_dtypes
import concourse.bass as bass
import concourse.mybir as mybir
from concourse.bass_utils import run_bass_kernel_spmd

F32, BF16 = mybir.dt.float32, mybir.dt.bfloat16
AF = mybir.ActivationFunctionType
ALU = mybir.AluOpType
AX = mybir.AxisListType

D = 2048
S_LEN = 4096
DEPTH = 2
IN_W = 7712
DFF = 5632
EPS = 1e-6
NQT = S_LEN // 128
DILS = (1, 4, 16)


class Sched:
    BLK = {'pe': 'tensor', 'act': 'scalar', 'dve': 'vector', 'pool': 'gpsimd', 'sp': 'sync'}
    NSLOT = {'sp': 28, 'pool': 12, 'act': 6}

    def __init__(self, nc, es):
        self.nc = nc
        self.sem = {e: es.enter_context(nc.semaphore("s_" + e)) for e in ('pe', 'act', 'dve', 'pool')}
        self.dsem = {e: [es.enter_context(nc.semaphore("d_%s%d" % (e, i))) for i in range(n)]
                     for e, n in self.NSLOT.items()}
        self.cnt = {e: 0 for e in self.sem}
        self.slot_uses = {e: [0] * n for e, n in self.NSLOT.items()}
        self.next_slot = {e: 0 for e in self.NSLOT}
        self.waited = {e: {} for e in self.BLK}
        self.reset()

    def reset(self):
        self.ops = []
        self.lw = {}
        self.rd = {}

    def op(self, eng, fn, r=(), w=(), dma=False):
        deps = set()
        for b in r:
            x = self.lw.get(b)
            if x is not None:
                deps.add(x)
        for b in w:
            x = self.lw.get(b)
            if x is not None:
                deps.add(x)
            rb = self.rd.get(b)
            if rb:
                deps.update(rb.values())
        i = len(self.ops)
        self.ops.append([eng, fn, deps, dma, False, 0, 0, 0])
        for b in r:
            self.rd.setdefault(b, {})[('d', i) if dma else eng] = i
        for b in w:
            self.lw[b] = i
            self.rd[b] = {}
        return i

    def flush(self, name=None):
        ops = self.ops
        for o in ops:
            for d in o[2]:
                D_ = ops[d]
                if D_[3]:
                    continue
                if D_[0] == 'pe' and o[0] == 'pe' and not o[3]:
                    continue
                D_[4] = True
        for o in ops:
            e = o[0]
            if o[3]:
                s = self.next_slot[e]
                self.next_slot[e] = (s + 1) % self.NSLOT[e]
                o[7] = 16 * self.slot_uses[e][s]
                self.slot_uses[e][s] += 1
                o[5] = 16 * self.slot_uses[e][s]
                o[6] = s
            elif o[4]:
                self.cnt[e] += 1
                o[5] = self.cnt[e]
        with self.nc.Block() as block:
            for e, bname in self.BLK.items():
                eops = [o for o in ops if o[0] == e]
                if not eops:
                    continue

                def body(eng, eops=eops, e=e):
                    waited = self.waited[e]
                    for o in eops:
                        reqs = {}
                        for d in o[2]:
                            D_ = ops[d]
                            if D_[3]:
                                key = ('d', D_[0], D_[6])
                            else:
                                if D_[0] == 'pe' and e == 'pe' and not o[3]:
                                    continue
                                key = ('e', D_[0])
                            if reqs.get(key, 0) < D_[5]:
                                reqs[key] = D_[5]
                        if o[3] and o[7] > 0:
                            key = ('d', e, o[6])
                            if reqs.get(key, 0) < o[7]:
                                reqs[key] = o[7]
                        for key, val in reqs.items():
                            if waited.get(key, 0) < val:
                                sem = self.sem[key[1]] if key[0] == 'e' else self.dsem[key[1]][key[2]]
                                eng.wait_ge(sem, val)
                                waited[key] = val
                        ins = o[1](eng)
                        if o[3]:
                            ins.then_inc(self.dsem[e][o[6]], 16)
                        elif o[4]:
                            ins.then_inc(self.sem[e], 1)
                    if e in self.NSLOT:
                        for s in range(self.NSLOT[e]):
                            val = 16 * self.slot_uses[e][s]
                            key = ('d', e, s)
                            if waited.get(key, 0) < val:
                                eng.wait_ge(self.dsem[e][s], val)
                                waited[key] = val

                getattr(block, bname)(body)
        self.reset()


class Ctx:
    pass


_UID = [0]


def _uniq(n):
    _UID[0] += 1
    return "%s_%d" % (n, _UID[0])


def MM(K, out, lhsT, rhs, start, stop, r, w):
    K.S.op('pe', lambda e: e.matmul(out, lhsT=lhsT, rhs=rhs, start=start, stop=stop), r=r, w=w)


def TR(K, out, in_, r, w, ident=None):
    idn = K.ident[:] if ident is None else ident
    K.S.op('pe', lambda e: e.transpose(out=out, in_=in_, identity=idn), r=r, w=w)


def DMA(K, q, out, in_, r=(), w=(), slow=False):
    if slow:
        K.S.op(q, lambda e: e.dma_start(out=out, in_=in_, allow_slow_non_contiguous=True), r=r, w=w, dma=True)
    else:
        K.S.op(q, lambda e: e.dma_start(out=out, in_=in_), r=r, w=w, dma=True)


def ACTV(K, out, in_, func, r, w, bias=None, scale=None, accum_out=None):
    kw = {}
    if bias is not None:
        kw['bias'] = bias
    if scale is not None:
        kw['scale'] = scale
    if accum_out is not None:
        kw['accum_out'] = accum_out
    K.S.op('act', lambda e: e.activation(out=out, in_=in_, func=func, **kw), r=r, w=w)


def TT(K, eng, out, in0, in1, op, r, w):
    K.S.op(eng, lambda e: e.tensor_tensor(out=out, in0=in0, in1=in1, op=op), r=r, w=w)


def TS(K, eng, out, in0, s1, op0, r, w, s2=None, op1=None, accum_out=None):
    kw = {}
    if op1 is not None:
        kw['op1'] = op1
    if accum_out is not None:
        kw['accum_out'] = accum_out
    K.S.op(eng, lambda e: e.tensor_scalar(out=out, in0=in0, scalar1=s1, scalar2=s2, op0=op0, **kw), r=r, w=w)


def STT(K, out, in0, scalar, in1, op0, op1, r, w, accum_out=None):
    kw = {}
    if accum_out is not None:
        kw['accum_out'] = accum_out
    K.S.op('dve', lambda e: e.scalar_tensor_tensor(out=out, in0=in0, scalar=scalar, in1=in1, op0=op0, op1=op1, **kw),
           r=r, w=w)


def CP(K, eng, out, in_, r, w):
    if eng == 'act':
        K.S.op('act', lambda e: e.activation(out=out, in_=in_, func=AF.Copy), r=r, w=w)
    else:
        K.S.op(eng, lambda e: e.tensor_copy(out=out, in_=in_), r=r, w=w)


def _evac(K, idx, out, in_, r, w, scale=None):
    if idx % 2 == 0:
        if scale is None:
            K.S.op('act', lambda e: e.activation(out=out, in_=in_, func=AF.Copy), r=r, w=w)
        else:
            K.S.op('act', lambda e: e.activation(out=out, in_=in_, func=AF.Copy, scale=scale), r=r, w=w)
    else:
        if scale is None:
            K.S.op('dve', lambda e: e.tensor_copy(out=out, in_=in_), r=r, w=w)
        else:
            K.S.op('dve', lambda e: e.tensor_scalar(out=out, in0=in_, scalar1=scale, scalar2=None, op0=ALU.mult), r=r, w=w)


def rms_to_hT(K, es_tiles, x_src, tok0, ntile, wbc, hT, hT_key, xt, hb, st, junk, ptr, st_base=0, keep_x=None, ptr_keys=None):
    S = K.S
    pk = ptr_keys if ptr_keys is not None else [('ptr', 0), ('ptr', 1)]
    for tt in range(ntile):
        i = tt % len(hb)
        T0 = tok0 + tt * 128
        xti = xt[i] if keep_x is None else keep_x[tt]
        xkey = ('xt', i) if keep_x is None else ('xk', tt)
        if keep_x is None:
            S.op('sp', lambda e, xti=xti, T0=T0: e.dma_start(out=xti[:], in_=x_src[T0:T0 + 128, :]), w=[xkey], dma=True)
        c = (st_base + tt) * 4
        sk = ('st', st_base + tt)
        S.op('dve', lambda e, xti=xti, c=c: e.scalar_tensor_tensor(
            out=junk[:], in0=xti[:], scalar=1.0, in1=xti[:], op0=ALU.mult, op1=ALU.mult,
            accum_out=st[:, c:c + 1]), r=[xkey], w=['junk', sk])
        S.op('act', lambda e, c=c: e.activation(out=st[:, c + 1:c + 2], in_=st[:, c:c + 1], func=AF.Sqrt,
                                                scale=1.0 / D, bias=K.eps_t[:, 0:1]), r=[sk], w=[sk])
        S.op('dve', lambda e, c=c: e.reciprocal(out=st[:, c + 2:c + 3], in_=st[:, c + 1:c + 2]), r=[sk], w=[sk])
        S.op('dve', lambda e, xti=xti, c=c, i=i: e.scalar_tensor_tensor(
            out=hb[i][:], in0=xti[:], scalar=st[:, c + 2:c + 3], in1=wbc[:], op0=ALU.mult, op1=ALU.mult),
            r=[xkey, sk, 'wbc'], w=[('hb', i)])
        for hh in range(2):
            for cc in range(8):
                ch = hh * 8 + cc
                S.op('pe', lambda e, hh=hh, cc=cc, ch=ch, i=i: e.transpose(
                    out=ptr[hh][:, cc, :], in_=hb[i][:, ch * 128:(ch + 1) * 128], identity=K.ident[:]),
                    r=[('hb', i), 'ident'], w=[pk[hh]])
            _evac(K, hh, hT[:, hh * 8:(hh + 1) * 8, tt * 128:(tt + 1) * 128], ptr[hh][:],
                  r=[pk[hh]], w=[(hT_key, tt)])


def phase_A(K, l, x_src):
    nc, S = K.nc, K.S
    with ExitStack() as es:
        sb = lambda n, s, d: es.enter_context(nc.sbuf_tensor(_uniq(n), s, d))
        ps = lambda n, s, d: es.enter_context(nc.psum_tensor(_uniq(n), s, d))
        hT = sb("hT", [128, 16, 2048], BF16)
        wbc = sb("wbc", [128, D], F32)
        xt = [sb("xt%d" % i, [128, D], F32) for i in range(2)]
        junk = sb("junk", [128, D], BF16)
        hb = [sb("hb%d" % i, [128, D], BF16) for i in range(2)]
        st = sb("st", [128, 16 * 4], F32)
        wbuf = [sb("wb%d" % i, [128, 16, 544], BF16) for i in range(2)]
        obh = [sb("obh%d" % i, [128, 512], BF16) for i in range(4)]
        obf = [sb("obf%d" % i, [128, 512], F32) for i in range(4)]
        obig = [sb("obig%d" % i, [128, 2048], BF16) for i in range(3)]
        ptr = [ps("ptr%d" % i, [128, 8, 128], BF16) for i in range(2)]
        pmm = [ps("pmm%d" % i, [128, 512], F32) for i in range(4)]

        S.op('sp', lambda e: e.dma_start(out=wbc[:], in_=K.norm1_w[l:l + 1, :].partition_broadcast(128)),
             w=['wbc'], dma=True)
        blocks = []
        for g in range(3):
            blocks.append(('qk', K.qT, g, g * 512, 512))
        for g in range(3):
            blocks.append(('qk', K.kT, g, 1536 + g * 512, 512))
        for g in range(3):
            blocks.append(('v', None, g, 3072 + g * 512, 512))
        for j in range(2):
            blocks.append(('z', None, j, 4608 + j * 512, 512))
        for j in range(3):
            blocks.append(('xbc', None, j, 5632 + j * 512, 512))
        blocks.append(('dtu', None, 0, 7168, 544))
        hT_all = [('hT', t) for t in range(16)]
        nblk = 0
        kk = 0
        nbig = 0

        def load_w(bi, c0, wd):
            src = K.w_in[l, :, c0:c0 + wd].rearrange("(c p) n -> p c n", p=128)
            S.op('pool', lambda e: e.dma_start(out=wbuf[bi][:, :, 0:wd], in_=src), w=[('wb', bi)], dma=True)

        for half in range(2):
            H0 = half * 2048
            load_w(nblk % 2, blocks[0][3], blocks[0][4])
            rms_to_hT(K, es, x_src, H0, 16, wbc, hT, 'hT', xt, hb, st, junk, ptr)
            for bidx, (mode, dst, g, c0, wd) in enumerate(blocks):
                bi = nblk % 2
                nblk += 1
                if bidx + 1 < len(blocks):
                    load_w(nblk % 2, blocks[bidx + 1][3], blocks[bidx + 1][4])
                wb = wbuf[bi]
                wkey = ('wb', bi)
                rk = [wkey] + hT_all
                if mode == 'qk':
                    d = DILS[g]
                    npr = 2048 // d
                    for ct in range(4):
                        dview = dst[g * 512 + ct * 128:g * 512 + (ct + 1) * 128, 64:64 + S_LEN] \
                            .rearrange("p (r u) -> p r u", r=d)
                        ob_i = nbig % 3
                        nbig += 1
                        og_ = obig[ob_i][:].rearrange("p (r u) -> p r u", r=d)
                        for j in range(4):
                            k = kk % 4
                            kk += 1
                            for ch in range(16):
                                MM(K, pmm[k][:], wb[:, ch, ct * 128:(ct + 1) * 128], hT[:, ch, j * 512:(j + 1) * 512],
                                   ch == 0, ch == 15, r=[wkey] + hT_all[4 * j:4 * j + 4], w=[('pmm', k)])
                            _evac(K, kk, og_[:, :, j * (512 // d):(j + 1) * (512 // d)],
                                  pmm[k][:].rearrange("p (u r) -> p r u", r=d), r=[('pmm', k)], w=[('obig', ob_i)],
                                  scale=(0.125 if dst is K.qT else None))
                        DMA(K, 'sp', dview[:, :, half * npr:(half + 1) * npr], og_, r=[('obig', ob_i)])
                elif mode in ('v', 'z', 'dtu'):
                    d = DILS[g] if mode == 'v' else 1
                    npr = 2048 // d
                    L = S_LEN // d
                    for i in range(16):
                        k = kk % 4
                        kk += 1
                        r_ = (128 * i) // npr
                        u0 = (128 * i) % npr
                        t0 = r_ + d * u0
                        if d > 1:
                            lsel = lambda ch, t0=t0, d=d: hT[:, ch, t0:t0 + 127 * d + 1:d]
                            rki = [wkey] + hT_all
                        else:
                            lsel = lambda ch, i=i: hT[:, ch, i * 128:(i + 1) * 128]
                            rki = [wkey, hT_all[i]]
                        tok = H0 + i * 128
                        if mode == 'dtu':
                            for ch in range(16):
                                MM(K, pmm[k][:], lsel(ch), wb[:, ch, 32:544], ch == 0, ch == 15, r=rki, w=[('pmm', k)])
                            _evac(K, kk, obh[k][:], pmm[k][:], r=[('pmm', k)], w=[('obh', k)])
                            DMA(K, 'sp', K.uu[128 + tok:128 + tok + 128, :], obh[k][:], r=[('obh', k)])
                            k2 = kk % 4
                            kk += 1
                            for ch in range(16):
                                MM(K, pmm[k2][:, 0:32], lsel(ch), wb[:, ch, 0:32], ch == 0, ch == 15, r=rki, w=[('pmm', k2)])
                            _evac(K, kk, obf[k2][:, 0:32], pmm[k2][:, 0:32], r=[('pmm', k2)], w=[('obf', k2)])
                            DMA(K, 'sp', K.dtr[tok:tok + 128, :], obf[k2][:, 0:32], r=[('obf', k2)])
                            continue
                        for ch in range(16):
                            MM(K, pmm[k][:], lsel(ch), wb[:, ch, 0:512], ch == 0, ch == 15, r=rki, w=[('pmm', k)])
                        if mode == 'v':
                            P = 64 + r_ * L + half * npr + u0
                            _evac(K, kk, obh[k][:], pmm[k][:], r=[('pmm', k)], w=[('obh', k)])
                            DMA(K, 'sp', K.vv[g, P:P + 128, :], obh[k][:], r=[('obh', k)])
                        else:
                            _evac(K, kk, obf[k][:], pmm[k][:], r=[('pmm', k)], w=[('obf', k)])
                            DMA(K, 'sp', K.zz[tok:tok + 128, g * 512:(g + 1) * 512], obf[k][:], r=[('obf', k)])
                else:
                    for ct in range(4):
                        for j in range(4):
                            k = kk % 4
                            kk += 1
                            for ch in range(16):
                                MM(K, pmm[k][:], wb[:, ch, ct * 128:(ct + 1) * 128], hT[:, ch, j * 512:(j + 1) * 512],
                                   ch == 0, ch == 15, r=[wkey] + hT_all[4 * j:4 * j + 4], w=[('pmm', k)])
                            _evac(K, kk, obh[k][:], pmm[k][:], r=[('pmm', k)], w=[('obh', k)])
                            row = g * 512 + ct * 128
                            col = 2 + H0 + j * 512
                            DMA(K, 'sp', K.xbcT[row:row + 128, col:col + 512], obh[k][:], r=[('obh', k)])
        S.flush()


def phase_CD(K, l, x_src, x_dst, final_w=None, out_dst=None):
    nc, S = K.nc, K.S
    TB = 512
    NT = TB // 128
    with ExitStack() as es:
        sb = lambda n, s, d: es.enter_context(nc.sbuf_tensor(_uniq(n), s, d))
        ps = lambda n, s, d: es.enter_context(nc.psum_tensor(_uniq(n), s, d))
        h2T = sb("h2T", [128, 16, TB], BF16)
        xm = [sb("xm%d" % i, [128, D], F32) for i in range(NT)]
        wbc = sb("wbc2", [128, D], F32)
        wbf = sb("wbcf", [128, D], F32) if final_w is not None else None
        hb = [sb("hb2_%d" % i, [128, D], BF16) for i in range(2)]
        junk = sb("junk2", [128, D], BF16)
        st = sb("st2", [128, 8 * NT * 4], F32)
        wA = [sb("wA%d" % i, [128, 16, 512], BF16) for i in range(2)]
        wB = [sb("wB%d" % i, [128, 16, 512], BF16) for i in range(2)]
        ffT = [sb("ffT%d" % i, [128, 4, TB], BF16) for i in range(2)]
        wd = [sb("wd%d" % i, [128, 4, D], BF16) for i in range(1)]
        tmp = [sb("tmp%d" % i, [128, 512], F32) for i in range(2)]
        pgu = [ps("pgu%d" % i, [128, 512], F32) for i in range(4)]
        pd = [ps("pd%d" % i, [128, 512], F32) for i in range(4)]
        ptr = [pd[2 + i][:].bitcast(BF16).rearrange("p (c t) -> p c t", c=8) for i in range(2)]
        DMA(K, 'sp', wbc[:], K.norm2_w[l:l + 1, :].partition_broadcast(128), w=['wbc'])
        if final_w is not None:
            DMA(K, 'sp', wbf[:], final_w.partition_broadcast(128), w=['wbf'])
        na = 0
        nb = 0
        nf = 0
        kd = 0
        for blk in range(S_LEN // TB):
            T0 = blk * TB
            mixT = wB[nb % 2]
            mkey = ('wB', nb % 2)
            nb += 1
            DMA(K, 'sp', mixT[:], K.mixT[:, T0:T0 + TB].rearrange("(c p) t -> p c t", p=128), w=[mkey])
            for tt in range(NT):
                DMA(K, 'sp', xm[tt][:], x_src[T0 + tt * 128:T0 + (tt + 1) * 128, :], w=[('xk', tt)])
            for cb in range(4):
                wo = wA[na % 2]
                wkey = ('wA', na % 2)
                na += 1
                DMA(K, 'pool', wo[:], K.w_out[l, :, cb * 512:(cb + 1) * 512].rearrange("(c p) n -> p c n", p=128), w=[wkey])
                for tt in range(NT):
                    k = kd % 4
                    kd += 1
                    for ch in range(16):
                        MM(K, pd[k][:], mixT[:, ch, tt * 128:(tt + 1) * 128], wo[:, ch, :], ch == 0, ch == 15,
                           r=[mkey, wkey], w=[('pd', k)])
                    TT(K, 'dve', xm[tt][:, cb * 512:(cb + 1) * 512], pd[k][:], xm[tt][:, cb * 512:(cb + 1) * 512], ALU.add,
                       r=[('pd', k), ('xk', tt)], w=[('xk', tt)])
            rms_to_hT(K, None, None, 0, NT, wbc, h2T, 'h2T', None, hb, st, junk, ptr, st_base=(blk % 8) * NT, keep_x=xm,
                      ptr_keys=[('pd', 2), ('pd', 3)])
            h_all = [('h2T', t) for t in range(NT)]
            for gi in range(DFF // 512):
                wg = wA[na % 2]
                gkey = ('wA', na % 2)
                na += 1
                wu = wB[nb % 2]
                ukey = ('wB', nb % 2)
                nb += 1
                DMA(K, 'pool', wg[:], K.w_gate[l, :, gi * 512:(gi + 1) * 512].rearrange("(c p) n -> p c n", p=128), w=[gkey])
                DMA(K, 'pool', wu[:], K.w_up[l, :, gi * 512:(gi + 1) * 512].rearrange("(c p) n -> p c n", p=128), w=[ukey])
                ff = ffT[nf % 2]
                fkey = ('ffT', nf % 2)
                nf += 1
                for fb in range(4):
                    pg = pgu[2 * (fb % 2)]
                    pu = pgu[2 * (fb % 2) + 1]
                    kg = ('pgu', 2 * (fb % 2))
                    ku = ('pgu', 2 * (fb % 2) + 1)
                    for ch in range(16):
                        MM(K, pg[:], wg[:, ch, fb * 128:(fb + 1) * 128], h2T[:, ch, :], ch == 0, ch == 15,
                           r=[gkey] + h_all, w=[kg])
                    for ch in range(16):
                        MM(K, pu[:], wu[:, ch, fb * 128:(fb + 1) * 128], h2T[:, ch, :], ch == 0, ch == 15,
                           r=[ukey] + h_all, w=[ku])
                    tm = tmp[fb % 2]
                    ACTV(K, tm[:], pg[:], AF.Silu, r=[kg], w=[('tmp', fb % 2)])
                    TT(K, 'dve', ff[:, fb, :], tm[:], pu[:], ALU.mult, r=[('tmp', fb % 2), ku], w=[fkey])
                wdn = wd[0]
                DMA(K, 'pool', wdn[:], K.w_down[l, gi * 512:(gi + 1) * 512, :].rearrange("(c p) n -> p c n", p=128), w=['wd'])
                for tt in range(NT):
                    for cb in range(4):
                        k = kd % 4
                        kd += 1
                        for c in range(4):
                            MM(K, pd[k][:], ff[:, c, tt * 128:(tt + 1) * 128], wdn[:, c, cb * 512:(cb + 1) * 512], c == 0, c == 3,
                               r=[fkey, 'wd'], w=[('pd', k)])
                        TT(K, 'dve', xm[tt][:, cb * 512:(cb + 1) * 512], pd[k][:], xm[tt][:, cb * 512:(cb + 1) * 512], ALU.add,
                           r=[('pd', k), ('xk', tt)], w=[('xk', tt)])
            for tt in range(NT):
                rows = slice(T0 + tt * 128, T0 + (tt + 1) * 128)
                if final_w is None:
                    DMA(K, 'sp', x_dst[rows, :], xm[tt][:], r=[('xk', tt)])
                else:
                    c = ((blk % 8) * NT + tt) * 4
                    sk = ('stf', tt)
                    STT(K, junk[:], xm[tt][:], 1.0, xm[tt][:], ALU.mult, ALU.mult, r=[('xk', tt)], w=['junk', sk],
                        accum_out=st[:, c + 3:c + 4])
                    ACTV(K, st[:, c + 1:c + 2], st[:, c + 3:c + 4], AF.Sqrt, r=[sk], w=[sk], scale=1.0 / D, bias=K.eps_t[:, 0:1])
                    K.S.op('dve', lambda e, c=c: e.reciprocal(out=st[:, c + 2:c + 3], in_=st[:, c + 1:c + 2]), r=[sk], w=[sk])
                    STT(K, xm[tt][:], xm[tt][:], st[:, c + 2:c + 3], wbf[:], ALU.mult, ALU.mult, r=[('xk', tt), sk, 'wbf'],
                        w=[('xk', tt)])
                    DMA(K, 'sp', out_dst[rows, :], xm[tt][:], r=[('xk', tt)])
        S.flush()


def phase_att(K, l):
    nc, S = K.nc, K.S
    with ExitStack() as es:
        sb = lambda n, s, d: es.enter_context(nc.sbuf_tensor(_uniq(n), s, d))
        ps = lambda n, s, d: es.enter_context(nc.psum_tensor(_uniq(n), s, d))
        b0 = sb("b0", [128, 9, 256], BF16)
        sid = sb("sid", [128, 24, 128], BF16)
        zt = sb("zt", [128, 512], BF16)
        qs = [sb("qs%d" % i, [128, 4, 128], BF16) for i in range(2)]
        ks = [sb("ks%d" % i, [128, 4, 256], BF16) for i in range(2)]
        vs = [sb("vs%d" % i, [128, 2, 512], BF16) for i in range(2)]
        Pm = [sb("Pm%d" % i, [128, 4, 256], BF16) for i in range(2)]
        PT = [sb("PT%d" % i, [128, 8, 128], BF16) for i in range(2)]
        mx = [sb("mx%d" % i, [128, 8], F32) for i in range(2)]
        nmx = [sb("nmx%d" % i, [128, 8], F32) for i in range(2)]
        ogt = [sb("ogt%d" % i, [128, 512], F32) for i in range(2)]
        mlt = [sb("mlt%d" % i, [128, 16], F32) for i in range(2)]
        psc = [ps("psc%d" % i, [128, 4, 256], F32) for i in range(2)]
        pT = [ps("pT%d" % i, [128, 8, 128], BF16) for i in range(2)]
        po = [ps("po%d" % i, [128, 512], F32) for i in range(2)]
        DMA(K, 'sp', b0[:], K.c_b0.rearrange("v p k -> p v k"), w=['b0'])
        DMA(K, 'sp', sid[:], K.c_sid.rearrange("h p k -> p h k"), w=['sid'])
        S.op('dve', lambda e: e.memset(zt[:], 0.0), w=['zt'])
        NP = S_LEN + 128
        for a_ in range(12):
            DMA(K, 'sp', K.kT[a_ * 128:(a_ + 1) * 128, 0:64], zt[:, 0:64], r=['zt'], w=['kT'])
            DMA(K, 'sp', K.kT[a_ * 128:(a_ + 1) * 128, NP - 64:NP], zt[:, 0:64], r=['zt'], w=['kT'])
        for g in range(3):
            DMA(K, 'sp', K.vv[g, 0:64, :], zt[0:64, :], r=['zt'], w=['vv'])
            DMA(K, 'sp', K.vv[g, NP - 64:NP, :], zt[0:64, :], r=['zt'], w=['vv'])
        units = [(g, ti) for g in range(3) for ti in range(NQT)]
        NU = len(units)

        def unit_info(n):
            g, ti = units[n]
            d = DILS[g]
            L = S_LEN // d
            r_ = (ti * 128) // L
            u0 = (ti * 128) % L
            var = 1 if u0 == 0 else (2 if u0 + 128 == L else 0)
            return g, d, L, r_, u0, var

        def stage_Q(k):
            n, hf = divmod(k, 2)
            i = n % 2
            g, d, L, r_, u0, var = unit_info(n)
            if hf == 0:
                P0 = 64 + r_ * L + u0
                rows = slice(g * 512, (g + 1) * 512)
                DMA(K, 'sp', qs[i][:], K.qT[rows, P0:P0 + 128].rearrange("(a p) t -> p a t", p=128), w=[('qs', i)])
                DMA(K, 'sp', ks[i][:], K.kT[rows, P0 - 64:P0 + 192].rearrange("(a p) t -> p a t", p=128), r=['kT'], w=[('ks', i)])
                DMA(K, 'sp', vs[i][:], K.vv[g, P0 - 64:P0 + 192, :].rearrange("(kt p) c -> p kt c", p=128), r=['vv'], w=[('vs', i)])
            for sl in range(4):
                s_ = 4 * hf + sl
                pr = s_ // 2
                prt = slice(64 * (s_ % 2), 64 * (s_ % 2) + 64)
                MM(K, psc[hf][:, sl, :], qs[i][prt, pr, :], ks[i][prt, pr, :], True, False,
                   r=[('qs', i), ('ks', i)], w=[('psc', hf)])
                MM(K, psc[hf][:, sl, :], sid[:, g * 8 + s_, :], b0[:, g * 3 + var, :], False, True,
                   r=['sid', 'b0'], w=[('psc', hf)])
            nms = mlt[i][:, 4 * hf:4 * hf + 4]
            S.op('dve', lambda e, nms=nms, hf=hf: e.tensor_reduce(out=nms, in_=psc[hf][:], axis=AX.X, op=ALU.max, negate=True),
                 r=[('psc', hf)], w=[('mltm', i, hf)])
            for sl in range(4):
                s_ = 4 * hf + sl
                ACTV(K, Pm[hf][:, sl, :], psc[hf][:, sl, :], AF.Exp, r=[('psc', hf), ('mltm', i, hf)],
                     w=[('Pm', hf, sl), ('mltl', i, s_)], bias=mlt[i][:, s_:s_ + 1],
                     accum_out=mlt[i][:, 8 + s_:9 + s_])

        def stage_T(k):
            n, hf = divmod(k, 2)
            for sl in range(4):
                for kt in range(2):
                    TR(K, pT[hf][:, sl * 2 + kt, :], Pm[hf][:, sl, kt * 128:(kt + 1) * 128],
                       r=[('Pm', hf, sl), 'ident'], w=[('pT', hf)])
            CP(K, 'dve' if hf == 0 else 'act', PT[hf][:], pT[hf][:], r=[('pT', hf)], w=[('PT', hf)])

        def stage_V(k):
            n, hf = divmod(k, 2)
            i = n % 2
            g, d, L, r_, u0, var = unit_info(n)
            for sl in range(4):
                s_ = 4 * hf + sl
                for kt in range(2):
                    MM(K, po[i][:, s_ * 64:(s_ + 1) * 64], PT[hf][:, sl * 2 + kt, :], vs[i][:, kt, s_ * 64:(s_ + 1) * 64],
                       kt == 0, kt == 1, r=[('PT', hf), ('vs', i)], w=[('po', i)])
            if hf == 1:
                CP(K, 'act', ogt[i][:], po[i][:], r=[('po', i)], w=[('ogt', i)])
                t0 = r_ + d * u0
                tsl = slice(t0, t0 + 127 * d + 1, d) if d > 1 else slice(t0, t0 + 128)
                DMA(K, 'sp', K.og[g, tsl, :], ogt[i][:], r=[('ogt', i)])
                DMA(K, 'sp', K.mlg[g, tsl, :], mlt[i][:],
                    r=[('mltm', i, 0), ('mltm', i, 1)] + [('mltl', i, q_) for q_ in range(8)])

        NK = 2 * NU
        for k in range(NK + 2):
            if k < NK:
                stage_Q(k)
            if 0 <= k - 1 < NK:
                stage_T(k - 1)
            if 0 <= k - 2 < NK:
                stage_V(k - 2)
        S.flush()
    with ExitStack() as es:
        sb = lambda n, s, d: es.enter_context(nc.sbuf_tensor(_uniq(n), s, d))
        ps = lambda n, s, d: es.enter_context(nc.psum_tensor(_uniq(n), s, d))
        o3 = [sb("o3_%d" % i, [128, 3, 512], F32) for i in range(2)]
        ml3 = [sb("ml3_%d" % i, [128, 3, 16], F32) for i in range(2)]
        sm = [sb("sm%d" % i, [128, 64], F32) for i in range(2)]
        w3 = [sb("w3_%d" % i, [128, 3, 8], F32) for i in range(2)]
        acc = [sb("acc%d" % i, [128, 512], F32) for i in range(2)]
        t2 = [sb("t2_%d" % i, [128, 512], F32) for i in range(2)]
        ab = [sb("ab%d" % i, [128, 512], BF16) for i in range(2)]
        aT = [sb("aT%d" % i, [128, 4, 128], BF16) for i in range(2)]
        pa = [ps("pa%d" % i, [128, 4, 128], BF16) for i in range(2)]
        def merge_tile(ti):
                i = ti % 2
                rows = slice(ti * 128, (ti + 1) * 128)
                DMA(K, 'sp', o3[i][:], K.og[:, rows, :].rearrange("g p c -> p g c"), w=[('o3', i)])
                yield
                DMA(K, 'sp', ml3[i][:], K.mlg[:, rows, :].rearrange("g p c -> p g c"), w=[('ml3', i)])
                yield
                M = sm[i][:, 0:8]
                den = sm[i][:, 8:16]
                rden = sm[i][:, 16:24]
                k3 = [('ml3', i)]
                ks_ = [('sm', i)]
                m3 = ml3[i][:, :, 0:8]
                l3 = ml3[i][:, :, 8:16]
                S.op('dve', lambda e, M=M, m3=m3: e.tensor_reduce(out=M, in_=m3.rearrange("p g s -> p s g"), axis=AX.X, op=ALU.min),
                     r=k3, w=ks_)
                yield
                TT(K, 'dve', w3[i][:], m3, M.unsqueeze(1).to_broadcast([128, 3, 8]), ALU.subtract, r=k3 + ks_, w=[('w3', i)])
                yield
                ACTV(K, w3[i][:], w3[i][:], AF.Exp, r=[('w3', i)], w=[('w3', i)], scale=-1.0)
                yield
                wl = sm[i][:, 24:48].rearrange("p (g s) -> p g s", g=3)
                TT(K, 'dve', wl, w3[i][:], l3, ALU.mult, r=k3 + [('w3', i)], w=ks_)
                yield
                S.op('dve', lambda e, den=den, wl=wl: e.tensor_reduce(out=den, in_=wl.rearrange("p g s -> p s g"), axis=AX.X, op=ALU.add),
                     r=ks_, w=ks_)
                yield
                S.op('dve', lambda e, rden=rden, den=den: e.reciprocal(out=rden, in_=den), r=ks_, w=ks_)
                yield
                TT(K, 'dve', w3[i][:], w3[i][:], rden.unsqueeze(1).to_broadcast([128, 3, 8]), ALU.mult, r=ks_ + [('w3', i)], w=[('w3', i)])
                yield
                for g in range(3):
                    wb_ = w3[i][:, g, :].unsqueeze(2).to_broadcast([128, 8, 64])
                    src = o3[i][:, g, :].rearrange("p (s e) -> p s e", e=64)
                    dst = (acc[i] if g == 0 else t2[i])[:].rearrange("p (s e) -> p s e", e=64)
                    eng = 'dve' if g != 1 else 'pool'
                    TT(K, eng, dst, src, wb_, ALU.mult, r=[('o3', i), ('w3', i)], w=[('acc', i) if g == 0 else ('t2', i)])
                    yield
                    if g > 0:
                        outap = acc[i][:] if g == 1 else ab[i][:]
                        TT(K, 'dve', outap, acc[i][:], t2[i][:], ALU.add, r=[('acc', i), ('t2', i)],
                           w=[('acc', i)] if g == 1 else [('ab', i)])
                        yield
                for a_ in range(4):
                    TR(K, pa[i][:, a_, :], ab[i][:, a_ * 128:(a_ + 1) * 128], r=[('ab', i), 'ident'], w=[('pa', i)])
                    yield
                CP(K, 'act', aT[i][:], pa[i][:], r=[('pa', i)], w=[('aT', i)])
                yield
                DMA(K, 'sp', K.mixT[0:512, rows].rearrange("(a p) t -> p a t", p=128), aT[i][:], r=[('aT', i)])
                yield

        for t0_ in range(0, NQT, 2):
            gens = [merge_tile(t0_), merge_tile(t0_ + 1)]
            alive = [True, True]
            while any(alive):
                for gi_ in range(2):
                    if alive[gi_]:
                        try:
                            next(gens[gi_])
                        except StopIteration:
                            alive[gi_] = False
        S.flush()


def phase_pool(K, l):
    nc, S = K.nc, K.S
    with ExitStack() as es:
        sb = lambda n, s, d: es.enter_context(nc.sbuf_tensor(_uniq(n), s, d))
        ps = lambda n, s, d: es.enter_context(nc.psum_tensor(_uniq(n), s, d))
        band = sb("band", [128, 20, 128], BF16)
        pw = sb("pw", [128, 4, 128], BF16)
        psc_ = sb("pscale", [128, 4], F32)
        ut = [sb("ut%d" % i, [128, 3, 512], BF16) for i in range(2)]
        rt = [sb("rt%d" % i, [128, 4, 128], BF16) for i in range(2)]
        ot = [sb("ot%d" % i, [128, 4, 128], BF16) for i in range(2)]
        pr = [ps("pr%d" % i, [128, 4, 128], F32) for i in range(2)]
        pq = [ps("pq%d" % i, [128, 4, 128], F32) for i in range(2)]
        DMA(K, 'sp', band[:], K.c_band.rearrange("g v p k -> p (g v) k"), w=['band'])
        DMA(K, 'pool', pw[:], K.pool_w[l].rearrange("g p k -> p g k"), w=['pw'])
        DMA(K, 'sp', psc_[:], K.pool_scale[l].rearrange("(g p) -> p g", p=128), w=['pscale'], slow=True)
        for ti in range(NQT):
            i = ti % 2
            lo = 0 if ti > 0 else 1
            hi = 3 if ti < NQT - 1 else 2
            R0 = 128 + (ti - 1) * 128
            DMA(K, 'sp', ut[i][:, lo:hi, :], K.uu[R0 + lo * 128:R0 + hi * 128, :].rearrange("(a p) c -> p a c", p=128),
                w=[('ut', i)])
            for g in range(4):
                own = 1 if ti == 0 else (2 if ti == NQT - 1 else 0)
                terms = [(1, own)]
                if ti > 0:
                    terms.append((0, 3))
                if ti < NQT - 1:
                    terms.append((2, 4))
                for n_, (a_, v) in enumerate(terms):
                    MM(K, pr[i][:, g, :], ut[i][:, a_, g * 128:(g + 1) * 128], band[:, g * 5 + v, :], n_ == 0, n_ == len(terms) - 1,
                       r=[('ut', i), 'band'], w=[('pr', i)])
            CP(K, 'act', rt[i][:], pr[i][:], r=[('pr', i)], w=[('rt', i)])
            for g in range(4):
                MM(K, pq[i][:, g, :], pw[:, g, :], rt[i][:, g, :], True, True, r=['pw', ('rt', i)], w=[('pq', i)])
            for g in range(4):
                TS(K, 'dve', ot[i][:, g, :], pq[i][:, g, :], psc_[:, g:g + 1], ALU.mult, r=[('pq', i), 'pscale'], w=[('ot', i)])
            DMA(K, 'sp', K.mixT[1536:2048, ti * 128:(ti + 1) * 128].rearrange("(a p) t -> p a t", p=128), ot[i][:],
                r=[('ot', i)])
        S.flush()


def phase_ssd(K, l):
    nc, S = K.nc, K.S
    NC_ = NQT
    with ExitStack() as es:
        sb = lambda n, s, d: es.enter_context(nc.sbuf_tensor(_uniq(n), s, d))
        bk = [es.enter_context(nc.psum_tensor(_uniq("bk%d" % i), [128, 512], F32)) for i in range(8)]
        B = lambda i: ('bk', i)
        tri = sb("tri", [128, 4, 128], F32)
        onesf = sb("onesf", [128, 128], F32)
        identf = sb("identf", [128, 128], F32)
        one_t = sb("one_t", [128, 1], F32)
        zt = sb("zt2", [128, 16], BF16)
        dt_all = sb("dt_all", [128, NC_, 32], F32)
        dta = sb("dta", [128, NC_, 32], F32)
        tmpa = sb("tmpa", [128, NC_, 32], F32)
        tmpb = sb("tmpb", [128, NC_, 32], F32)
        dtb = sb("dtb", [128, 32], F32)
        abc = sb("abc", [128, 32], F32)
        E = sb("E", [128, NC_, 64], F32)
        cd = sb("cd", [128, NC_, 32], F32)
        wx = sb("wx", [128, NC_, 4, 16], F32)
        cw = sb("cw", [128, 5, 12], F32)
        cb = sb("cb", [128, 12], F32)
        dg = sb("dg", [128, 12, 5, 128], BF16)
        dsk = sb("dsk", [128, 16], F32)
        nw = sb("nw", [128, 1024], F32)
        CTall = sb("CTall", [128, NC_, 2, 128], BF16)
        xin = sb("xin", [128, 12, 132], BF16)
        xc = sb("xc", [128, 12, 128], BF16)
        xsB = sb("xsB", [128, 1280], BF16)
        cbm = sb("cbm", [128, 2, 2, 128], F32)
        X = sb("X", [128, 2, 16, 128], F32)
        seg = sb("seg", [128, 2, 16, 128], BF16)
        MT = sb("MT", [128, 2, 16, 128], BF16)
        xdt = sb("xdt", [128, 2, 1024], BF16)
        xdd = sb("xdd", [128, 2, 1024], BF16)
        tA = sb("tA", [128, 1024], F32)
        tB = sb("tB", [128, 1024], F32)
        Sf = sb("Sf", [128, 1024], F32)
        Sfb = sb("Sfb", [128, 1024], BF16)
        stt = sb("stt", [128, 1024], F32)
        yb = sb("yb", [128, 1024], BF16)
        yT = sb("yT", [128, 8, 128], BF16)
        st = sb("st3", [128, 8], F32)

        DMA(K, 'sp', tri[:], K.c_tri.rearrange("v p k -> p v k"), w=['tri'])
        DMA(K, 'sp', identf[:], K.c_identf[:, :], w=['identf'])
        S.op('dve', lambda e: e.memset(onesf[:], 1.0), w=['onesf'])
        S.op('dve', lambda e: e.memset(one_t[:], 1.0), w=['one_t'])
        S.op('dve', lambda e: e.memset(zt[:], 0.0), w=['zt'])
        for a_ in range(12):
            DMA(K, 'sp', K.xbcT[a_ * 128:(a_ + 1) * 128, 0:2], zt[:, 0:2], r=['zt'], w=['xbcT'])
            DMA(K, 'sp', K.xbcT[a_ * 128:(a_ + 1) * 128, S_LEN + 2:S_LEN + 4], zt[:, 0:2], r=['zt'], w=['xbcT'])
        DMA(K, 'sp', dt_all[:], K.dtr.rearrange("(c p) k -> p c k", p=128), w=['dt_all'])
        DMA(K, 'sp', dtb[:], K.dt_bias[l:l + 1, :].partition_broadcast(128), w=['dtb'])
        DMA(K, 'sp', abc[:], K.a_log[l:l + 1, :].partition_broadcast(128), w=['abc'])
        DMA(K, 'sp', dsk[:], K.d_skip[l:l + 1, :].partition_broadcast(128), w=['dsk'])
        DMA(K, 'sp', nw[:], K.ssd_norm_w[l:l + 1, :].partition_broadcast(128), w=['nw'])
        for k in range(5):
            DMA(K, 'sp', cw[:, k, :], K.conv_w[l, k].rearrange("(ct p) -> p ct", p=128), w=['cw'], slow=True)
        DMA(K, 'sp', cb[:], K.conv_b[l].rearrange("(ct p) -> p ct", p=128), w=['cb'], slow=True)
        for ct in range(12):
            for k in range(5):
                TS(K, 'pool' if (ct + k) % 2 else 'dve', dg[:, ct, k, :], identf[:], cw[:, k, ct:ct + 1], ALU.mult,
                   r=['identf', 'cw'], w=['dg'])
        ACTV(K, abc[:], abc[:], AF.Exp, r=['abc'], w=['abc'])
        TS(K, 'dve', abc[:], abc[:], -1.0, ALU.mult, r=['abc'], w=['abc'])
        bc32 = lambda t: t[:].unsqueeze(1).to_broadcast([128, NC_, 32])
        TT(K, 'dve', dt_all[:], dt_all[:], bc32(dtb), ALU.add, r=['dt_all', 'dtb'], w=['dt_all'])
        TS(K, 'dve', tmpb[:], dt_all[:], -1.0, ALU.mult, r=['dt_all'], w=['tmpb'])
        TT(K, 'dve', tmpa[:], dt_all[:], tmpb[:], ALU.max, r=['dt_all', 'tmpb'], w=['tmpa'])
        ACTV(K, tmpa[:], tmpa[:], AF.Exp, r=['tmpa'], w=['tmpa'], scale=-1.0)
        ACTV(K, tmpa[:], tmpa[:], AF.Ln, r=['tmpa', 'one_t'], w=['tmpa'], bias=one_t[:, 0:1])
        TS(K, 'dve', tmpb[:], dt_all[:], 0.0, ALU.max, r=['dt_all'], w=['tmpb'])
        TT(K, 'dve', dt_all[:], tmpa[:], tmpb[:], ALU.add, r=['tmpa', 'tmpb'], w=['dt_all'])
        TT(K, 'dve', dta[:], dt_all[:], bc32(abc), ALU.mult, r=['dt_all', 'abc'], w=['dta'])
        for c in range(NC_):
            bi = c // 8
            for v in range(4):
                cols = slice((c % 8) * 64 + v * 16, (c % 8) * 64 + v * 16 + 16)
                dsl = slice(0, 16) if v < 2 else slice(16, 32)
                MM(K, bk[bi][:, cols], tri[:, v, :], dta[:, c, dsl], True, True, r=['tri', 'dta'], w=[B(bi)])
        for bi in range(4):
            ACTV(K, E[:, bi * 8:(bi + 1) * 8, :], bk[bi][:].rearrange("p (c k) -> p c k", k=64), AF.Exp, r=[B(bi)], w=['E'])
        for hf in range(2):
            MM(K, bk[4 + hf][:], onesf[:], dta[:, hf * 16:(hf + 1) * 16, :], True, True, r=['onesf', 'dta'], w=[B(4 + hf)])
            ACTV(K, cd[:, hf * 16:(hf + 1) * 16, :], bk[4 + hf][:].rearrange("p (c k) -> p c k", k=32), AF.Exp,
                 r=[B(4 + hf)], w=['cd'])
        for dr in range(2):
            CP(K, 'dve', wx[:, :, dr, :], dt_all[:, :, dr * 16:(dr + 1) * 16], r=['dt_all'], w=['wx'])
            TT(K, 'dve', wx[:, :, 2 + dr, :], dt_all[:, :, dr * 16:(dr + 1) * 16], E[:, :, 16 + 32 * dr:32 + 32 * dr], ALU.mult,
               r=['dt_all', 'E'], w=['wx'])
        S.op('dve', lambda e: e.memset(Sf[:], 0.0), w=['Sf'])
        S.op('dve', lambda e: e.memset(Sfb[:], 0.0), w=['Sfb'])
        bc = lambda ap, shape, ax: ap.unsqueeze(ax).to_broadcast(shape)
        nd = 0
        for c in range(NC_):
            T0 = c * 128
            DMA(K, 'sp', xin[:], K.xbcT[:, T0:T0 + 132].rearrange("(ct p) t -> p ct t", p=128), r=['xbcT'], w=['xin'])
            for dr in range(2):
                TT(K, 'pool', X[:, dr, :, :], bc(tri[:, 0 if dr == 0 else 2, :], [128, 16, 128], 1),
                   bc(dta[:, c, dr * 16:(dr + 1) * 16], [128, 16, 128], 2), ALU.mult, r=['tri', 'dta'], w=['X'])
            for ct in range(12):
                bi = ct // 4
                for k in range(5):
                    MM(K, bk[bi][:, (ct % 4) * 128:(ct % 4 + 1) * 128], dg[:, ct, k, :], xin[:, ct, k:k + 128], k == 0, k == 4,
                       r=['dg', 'xin'], w=[B(bi)])
            for ct in range(12):
                bi = ct // 4
                ACTV(K, xc[:, ct, :], bk[bi][:, (ct % 4) * 128:(ct % 4 + 1) * 128], AF.Silu, r=[B(bi), 'cb'], w=['xc'],
                     bias=cb[:, ct:ct + 1])
            CP(K, 'pool', CTall[:, c, :, :], xc[:, 10:12, :], r=['xc'], w=['CTall'])
            for dr in range(2):
                for q4 in range(4):
                    bi = 6 + (nd % 2)
                    nd += 1
                    MM(K, bk[bi][:], tri[:, 1 if dr == 0 else 3, :], X[:, dr, q4 * 4:(q4 + 1) * 4, :], True, True,
                       r=['tri', 'X'], w=[B(bi)])
                    ACTV(K, seg[:, dr, q4 * 4:(q4 + 1) * 4, :], bk[bi][:].rearrange("p (h l) -> p h l", h=4), AF.Exp,
                         r=[B(bi)], w=['seg'])
            bv3 = bk[3][:].bitcast(BF16)
            bv4 = bk[4][:].bitcast(BF16)
            for ct in range(10):
                dst = bv3[:, ct * 128:(ct + 1) * 128] if ct < 8 else bv4[:, (ct - 8) * 128:(ct - 7) * 128]
                TR(K, dst, xc[:, ct, :], r=['xc', 'ident'], w=[B(3) if ct < 8 else B(4)])
            CP(K, 'dve', xsB[:, 0:1024], bv3[:, 0:1024], r=[B(3)], w=['xsB'])
            CP(K, 'dve', xsB[:, 1024:1280], bv4[:, 0:256], r=[B(4)], w=['xsB'])
            for g in range(2):
                MM(K, bk[5][:, g * 128:(g + 1) * 128], xc[:, 8 + g, :], xc[:, 10 + g, :], True, True, r=['xc'], w=[B(5)])
            for dr in range(2):
                TT(K, 'dve', cbm[:, dr, :, :], bk[5][:, 0:256].rearrange("p (g l) -> p g l", g=2),
                   bc(tri[:, 0 if dr == 0 else 2, :], [128, 2, 128], 1), ALU.mult, r=[B(5), 'tri'], w=['cbm'])
            for dr in range(2):
                for g in range(2):
                    TT(K, 'dve' if g == 0 else 'pool', MT[:, dr, g * 8:(g + 1) * 8, :], seg[:, dr, g * 8:(g + 1) * 8, :],
                       bc(cbm[:, dr, g, :], [128, 8, 128], 1), ALU.mult, r=['seg', 'cbm'], w=['MT'])
            xs3 = xsB[:, 0:1024].rearrange("p (h e) -> p h e", e=64)
            for dr in range(2):
                TT(K, 'pool', xdt[:, dr, :].rearrange("p (h e) -> p h e", e=64), xs3, bc(wx[:, c, dr, :], [128, 16, 64], 2),
                   ALU.mult, r=['xsB', 'wx'], w=['xdt'])
                TT(K, 'dve', xdd[:, dr, :].rearrange("p (h e) -> p h e", e=64), xs3, bc(wx[:, c, 2 + dr, :], [128, 16, 64], 2),
                   ALU.mult, r=['xsB', 'wx'], w=['xdd'])
            for hh in range(16):
                bi = hh // 8
                cols = slice((hh % 8) * 64, (hh % 8) * 64 + 64)
                MM(K, bk[bi][:, cols], MT[:, 0, hh, :], xdt[:, 0, hh * 64:(hh + 1) * 64], True, False, r=['MT', 'xdt'], w=[B(bi)])
                MM(K, bk[bi][:, cols], MT[:, 1, hh, :], xdt[:, 1, hh * 64:(hh + 1) * 64], False, True, r=['MT', 'xdt'], w=[B(bi)])
            for g in range(2):
                MM(K, bk[2 + g][:], xc[:, 10 + g, :], Sfb[:, g * 512:(g + 1) * 512], True, True, r=['xc', 'Sfb'], w=[B(2 + g)])
            for dr in range(2):
                for g in range(2):
                    MM(K, bk[4 + 2 * dr + g][:], xsB[:, 1024 + g * 128:1024 + (g + 1) * 128], xdd[:, dr, g * 512:(g + 1) * 512],
                       True, True, r=['xsB', 'xdd'], w=[B(4 + 2 * dr + g)])
            for g in range(2):
                cs = slice(g * 512, (g + 1) * 512)
                v3 = lambda ap: ap.rearrange("p (h e) -> p h e", e=64)
                TT(K, 'dve', v3(tA[:, cs]), v3(bk[2 + g][:]), bc(E[:, c, g * 8:(g + 1) * 8], [128, 8, 64], 2), ALU.mult,
                   r=[B(2 + g), 'E'], w=['tA'])
                TT(K, 'dve', tA[:, cs], tA[:, cs], bk[g][:], ALU.add, r=['tA', B(g)], w=['tA'])
            TT(K, 'pool', tB[:].rearrange("p (h e) -> p h e", e=64), xs3, bc(dsk[:, :], [128, 16, 64], 2), ALU.mult,
               r=['xsB', 'dsk'], w=['tB'])
            TT(K, 'pool', tA[:], tA[:], tB[:], ALU.add, r=['tA', 'tB'], w=['tA'])
            DMA(K, 'sp', K.ypart[T0:T0 + 128, :], tA[:], r=['tA'], w=['ypart'])
            for g in range(2):
                CP(K, 'act', stt[:, g * 512:(g + 1) * 512], bk[6 + g][:], r=[B(6 + g)], w=['stt'])
            DMA(K, 'sp', K.stb[c], stt[:], r=['stt'], w=['stb'])
            TT(K, 'dve', Sf[:].rearrange("p (h e) -> p h e", e=64), Sf[:].rearrange("p (h e) -> p h e", e=64),
               bc(cd[:, c, 0:16], [128, 16, 64], 2), ALU.mult, r=['Sf', 'cd'], w=['Sf'])
            for g in range(2):
                cs = slice(g * 512, (g + 1) * 512)
                TT(K, 'dve', Sf[:, cs], Sf[:, cs], bk[4 + g][:], ALU.add, r=['Sf', B(4 + g)], w=['Sf'])
            CP(K, 'act', Sfb[:], Sf[:], r=['Sf'], w=['Sfb'])
        S.op('dve', lambda e: e.memset(Sf[:], 0.0), w=['Sf'])
        S.op('dve', lambda e: e.memset(Sfb[:], 0.0), w=['Sfb'])
        for c in range(NC_ - 1, -1, -1):
            T0 = c * 128
            DMA(K, 'sp', tA[:], K.ypart[T0:T0 + 128, :], r=['ypart'], w=['tA'])
            DMA(K, 'sp', tB[:], K.zz[T0:T0 + 128, :], w=['tB'])
            DMA(K, 'sp', stt[:], K.stb[c], r=['stb'], w=['stt'])
            for g in range(2):
                MM(K, bk[g][:], CTall[:, c, g, :], Sfb[:, g * 512:(g + 1) * 512], True, True, r=['CTall', 'Sfb'], w=[B(g)])
            for g in range(2):
                cs = slice(g * 512, (g + 1) * 512)
                v3 = lambda ap: ap.rearrange("p (h e) -> p h e", e=64)
                xq = X[:, 0, 0:4, :].rearrange("p a b -> p (a b)")
                TT(K, 'dve', v3(xq), v3(bk[g][:]), bc(E[:, c, 32 + g * 8:40 + g * 8], [128, 8, 64], 2), ALU.mult,
                   r=[B(g), 'E'], w=['X'])
                TT(K, 'dve', tA[:, cs], tA[:, cs], xq, ALU.add, r=['tA', 'X'], w=['tA'])
            ACTV(K, tB[:], tB[:], AF.Silu, r=['tB'], w=['tB'])
            TT(K, 'dve', tA[:], tA[:], tB[:], ALU.mult, r=['tA', 'tB'], w=['tA'])
            for g in range(2):
                cs = slice(g * 512, (g + 1) * 512)
                STT(K, tB[:, cs], tA[:, cs], 1.0, tA[:, cs], ALU.mult, ALU.mult, r=['tA'], w=['tB', 'st'], accum_out=st[:, g:g + 1])
            ACTV(K, st[:, 2:4], st[:, 0:2], AF.Sqrt, r=['st'], w=['st'], scale=1.0 / 512, bias=K.eps_t[:, 0:1])
            S.op('dve', lambda e: e.reciprocal(out=st[:, 4:6], in_=st[:, 2:4]), r=['st'], w=['st'])
            for g in range(2):
                cs = slice(g * 512, (g + 1) * 512)
                STT(K, yb[:, cs], tA[:, cs], st[:, 4 + g:5 + g], nw[:, cs], ALU.mult, ALU.mult, r=['tA', 'st', 'nw'], w=['yb'])
            bv3 = bk[3][:].bitcast(BF16)
            for a_ in range(8):
                TR(K, bv3[:, a_ * 128:(a_ + 1) * 128], yb[:, a_ * 128:(a_ + 1) * 128], r=['yb', 'ident'], w=[B(3)])
            CP(K, 'act', yT[:], bv3[:, 0:1024].rearrange("p (a t) -> p a t", a=8), r=[B(3)], w=['yT'])
            DMA(K, 'sp', K.mixT[512:1536, T0:T0 + 128].rearrange("(a p) t -> p a t", p=128), yT[:], r=['yT'])
            TT(K, 'dve', Sf[:].rearrange("p (h e) -> p h e", e=64), Sf[:].rearrange("p (h e) -> p h e", e=64),
               bc(cd[:, c, 16:32], [128, 16, 64], 2), ALU.mult, r=['Sf', 'cd'], w=['Sf'])
            TT(K, 'dve', Sf[:], Sf[:], stt[:], ALU.add, r=['Sf', 'stt'], w=['Sf'])
            CP(K, 'act', Sfb[:], Sf[:], r=['Sf'], w=['Sfb'])
        S.flush()


def phase_B(K, l):
    phase_att(K, l)
    phase_pool(K, l)
    phase_ssd(K, l)


def setup_common(K, es):
    nc = K.nc
    K.ident = es.enter_context(nc.sbuf_tensor("ident", [128, 128], BF16))
    K.eps_t = es.enter_context(nc.sbuf_tensor("eps_t", [128, 1], F32))
    K.S.op('sp', lambda e: e.dma_start(out=K.ident[:], in_=K.c_ident[:, :]), w=['ident'], dma=True)
    K.S.op('dve', lambda e: e.memset(K.eps_t[:], EPS), w=['eps'])
    K.S.flush()


def declare_io(K, nc, dbg):
    dbg = dbg or {}
    di = lambda n, s, d: nc.dram_tensor(n, s, d, kind="ExternalInput").ap()
    K.x = di("x", [S_LEN, D], F32)
    K.norm1_w = di("norm1_w", [DEPTH, D], F32)
    K.w_in = di("w_in", [DEPTH, D, IN_W], F32)
    K.conv_w = di("conv_w", [DEPTH, 5, 1536], F32)
    K.conv_b = di("conv_b", [DEPTH, 1536], F32)
    K.dt_bias = di("dt_bias", [DEPTH, 32], F32)
    K.a_log = di("a_log", [DEPTH, 32], F32)
    K.d_skip = di("d_skip", [DEPTH, 16], F32)
    K.ssd_norm_w = di("ssd_norm_w", [DEPTH, 1024], F32)
    K.pool_w = di("pool_w", [DEPTH, 4, 128, 128], F32)
    K.pool_scale = di("pool_scale", [DEPTH, 512], F32)
    K.w_out = di("w_out", [DEPTH, D, D], F32)
    K.norm2_w = di("norm2_w", [DEPTH, D], F32)
    K.w_gate = di("w_gate", [DEPTH, D, DFF], F32)
    K.w_up = di("w_up", [DEPTH, D, DFF], F32)
    K.w_down = di("w_down", [DEPTH, DFF, D], F32)
    K.final_norm_w = di("final_norm_w", [1, D], F32)
    K.c_ident = di("c_ident", [128, 128], BF16)
    K.c_identf = di("c_identf", [128, 128], F32)
    K.c_tri = di("c_tri", [4, 128, 128], F32)
    K.c_b0 = di("c_b0", [9, 128, 256], BF16)
    K.c_sid = di("c_sid", [24, 128, 128], BF16)
    K.c_band = di("c_band", [4, 5, 128, 128], BF16)
    K.out = nc.dram_tensor("out", [S_LEN, D], F32, kind="ExternalOutput").ap()
    dsc = lambda n, s, d: nc.dram_tensor(n, s, d, kind=dbg.get(n, "Internal")).ap()
    NP = S_LEN + 128
    K.qT = dsc("qT", [1536, NP], BF16)
    K.kT = dsc("kT", [1536, NP], BF16)
    K.vv = dsc("vv", [3, NP, 512], BF16)
    K.zz = dsc("zz", [S_LEN, 1024], F32)
    K.xbcT = dsc("xbcT", [1536, S_LEN + 4], BF16)
    K.dtr = dsc("dtr", [S_LEN, 32], F32)
    K.uu = dsc("uu", [S_LEN + 256, 512], BF16)
    K.mixT = dsc("mixT", [D, S_LEN], BF16)
    K.x1 = dsc("x1", [S_LEN, D], F32)
    K.og = dsc("og", [3, S_LEN, 512], F32)
    K.mlg = dsc("mlg", [3, S_LEN, 16], F32)
    K.ypart = dsc("ypart", [S_LEN, 1024], F32)
    K.stb = dsc("stb", [NQT, 128, 1024], F32)


def build(dbg=None, phases=None):
    nc = bass.Bass("TRN2", target_bir_lowering=False)
    K = Ctx()
    K.nc = nc
    declare_io(K, nc, dbg)
    with ExitStack() as es:
        K.S = Sched(nc, es)
        setup_common(K, es)
        if phases is not None:
            phases(K)
        else:
            for l in range(DEPTH):
                xs = K.x if l == 0 else K.x1
                phase_A(K, l, xs)
                phase_B(K, l)
                if l == DEPTH - 1:
                    phase_CD(K, l, xs, None, final_w=K.final_norm_w[0:1, :], out_dst=K.out)
                else:
                    phase_CD(K, l, xs, K.x1)
    return nc


def host_consts():
    bf = ml_dtypes.bfloat16
    c = {}
    c["c_ident"] = np.eye(128, dtype=np.float32).astype(bf)
    c["c_identf"] = np.eye(128, dtype=np.float32)
    j = np.arange(128)[:, None]
    l_ = np.arange(128)[None, :]
    c["c_tri"] = np.stack([(j <= l_), (j > l_), (j >= l_), (j < l_)]).astype(np.float32)
    i = np.arange(128)[:, None]
    jj = np.arange(256)[None, :]
    rel = jj - 64 - i
    b0 = np.zeros((9, 128, 256), np.float32)
    BIG = 1.0e6
    for g, d in enumerate(DILS):
        for v in range(3):
            ok = np.abs(rel) <= 64
            if v == 1:
                ok = ok & (jj >= 64)
            if v == 2:
                ok = ok & (jj < 192)
            b0[g * 3 + v] = np.where(ok, -np.abs(rel) * float(d), -BIG)
    c["c_b0"] = b0.astype(bf)
    kk = np.arange(1, 25, dtype=np.float32)
    slopes = (2.0 ** (-8.0 * kk / 24.0)).astype(np.float32)
    c["c_sid"] = (np.eye(128, dtype=np.float32)[None] * slopes[:, None, None]).astype(bf)
    band = np.zeros((4, 5, 128, 128), np.float32)
    tp = np.arange(128)[:, None]
    t = np.arange(128)[None, :]
    for g, w in enumerate((2, 4, 8, 16)):
        hw = w // 2
        inwin = (tp >= t - hw) & (tp < t + hw)
        eye = (tp == t).astype(np.float32)
        band[g, 0] = inwin / float(w) - eye
        cnt_first = (t + hw) - np.maximum(t - hw, 0)
        band[g, 1] = inwin / cnt_first.astype(np.float32) - eye
        cnt_last = np.minimum(t + hw, 128) - (t - hw)
        band[g, 2] = inwin / cnt_last.astype(np.float32) - eye
        band[g, 3] = ((tp - 128) >= t - hw) / float(w)
        band[g, 4] = ((tp + 128) < t + hw) / float(w)
    c["c_band"] = band.astype(bf)
    return c


def make_inputs(inputs, b):
    f = lambda a: np.ascontiguousarray(np.asarray(a, dtype=np.float32))
    im = {"x": f(inputs["x"][b])}
    for n in ("norm1_w", "w_in", "conv_w", "conv_b", "d_skip", "ssd_norm_w", "pool_w", "pool_scale", "w_out",
              "norm2_w", "w_gate", "w_up", "w_down"):
        im[n] = f(inputs[n])
    im["dt_bias"] = f(inputs["dt_bias"]).reshape(DEPTH, 32)
    im["a_log"] = f(inputs["a_log"]).reshape(DEPTH, 32)
    im["final_norm_w"] = f(inputs["final_norm_w"]).reshape(1, D)
    im.update(host_consts())
    return im


_NC_CACHE = {}


def kernel(**inputs):
    if "nc" not in _NC_CACHE:
        _NC_CACHE["nc"] = build()
    nc = _NC_CACHE["nc"]
    B = inputs["x"].shape[0]
    in_maps = [make_inputs(inputs, c % B) for c in range(8)]
    res = run_bass_kernel_spmd(nc, in_maps, core_ids=list(range(8)))
    out = np.stack([np.asarray(res.results[b]["out"], dtype=np.float32) for b in range(B)], axis=0)
    return out
```
